# Optimizing a Trainium2 kernel written in Bass

```python
import jax, jax.numpy as jnp
from jax import lax
import numpy as np

D_MODEL = 2048
BATCH = 8
SEQ = 4096
DEPTH = 1

RET_HEADS = 8
RET_HEAD_DIM = 128
RET_WIDTH = RET_HEADS * RET_HEAD_DIM
RET_CHUNK = 128
MLA_HEADS = 8
MLA_NOPE_DIM = 128
MLA_ROPE_DIM = 64
MLA_V_DIM = 128
MLA_Q_RANK = 512
MLA_KV_RANK = 512
MLA_WIDTH = MLA_HEADS * MLA_V_DIM
MLA_QK_DIM = MLA_NOPE_DIM + MLA_ROPE_DIM
Q_BLOCK = 128
MIX_WIDTH = RET_WIDTH + MLA_WIDTH
IN_COLS = 4 * RET_WIDTH + MLA_Q_RANK + MLA_KV_RANK + MLA_ROPE_DIM
D_FF = -(-(8 * D_MODEL) // (3 * 256)) * 256
N_MOD = 6
ROPE_BASE = 10000.0
EPS = 1e-6

kernel_name = "hybrid_retention_mla_adaln_block"


def rmsnorm(x, w):
    xf = x.astype(jnp.float32)
    y = xf * lax.rsqrt(jnp.mean(xf * xf, axis=-1, keepdims=True) + EPS)
    return (y * w.astype(jnp.float32)).astype(x.dtype)


def rope_tables(positions, dim):
    inv_freq = ROPE_BASE ** (-jnp.arange(0, dim, 2, dtype=jnp.float32) / dim)
    ang = positions.astype(jnp.float32)[..., None] * inv_freq
    return jnp.cos(ang), jnp.sin(ang)


def apply_rope(t, cos, sin):
    tf = t.astype(jnp.float32)
    t1, t2 = jnp.split(tf, 2, axis=-1)
    return jnp.concatenate([t1 * cos - t2 * sin, t2 * cos + t1 * sin], axis=-1).astype(t.dtype)


def retention_one_direction(q, k, v, log_gamma, inclusive):
    C = q.shape[3]
    idx = jnp.arange(C, dtype=jnp.float32)
    diff = idx[:, None] - idx[None, :]
    mask = (diff >= 0) if inclusive else (diff > 0)
    lg = log_gamma[:, None, None]
    intra_decay = jnp.where(mask, jnp.exp(jnp.where(mask, diff, 0.0) * lg), 0.0)
    scores = jnp.einsum('bhncd,bhnsd->bhncs', q, k) * intra_decay[None, :, None]
    intra = jnp.einsum('bhncs,bhnse->bhnce', scores, v)
    zeta = jnp.exp((C - 1 - idx)[None, :] * log_gamma[:, None])
    xi = jnp.exp((idx + 1)[None, :] * log_gamma[:, None])
    chunk_kv = jnp.einsum('bhnsd,bhnse->bhnde', k * zeta[None, :, None, :, None], v)
    chunk_decay = jnp.exp(C * log_gamma)[None, :, None, None]

    def step(state, kv):
        return state * chunk_decay + kv, state

    _, prev = lax.scan(step, jnp.zeros_like(chunk_kv[:, :, 0]), jnp.moveaxis(chunk_kv, 2, 0))
    prev = jnp.moveaxis(prev, 0, 2)
    cross = jnp.einsum('bhncd,bhnde->bhnce', q * xi[None, :, None, :, None], prev)
    return intra + cross


def bidirectional_retention(q, k, v, log_gamma_fb):
    B, S, H, d = q.shape
    nc = S // RET_CHUNK

    def to_chunks(t):
        return t.astype(jnp.float32).reshape(B, nc, RET_CHUNK, H, d).transpose(0, 3, 1, 2, 4)

    def from_chunks(t):
        return t.transpose(0, 2, 3, 1, 4).reshape(B, S, H, d)

    fwd = retention_one_direction(to_chunks(q), to_chunks(k), to_chunks(v), log_gamma_fb[0], True)
    qb, kb, vb = (jnp.flip(t, axis=1) for t in (q, k, v))
    bwd = retention_one_direction(to_chunks(qb), to_chunks(kb), to_chunks(vb), log_gamma_fb[1], False)
    return from_chunks(fwd) + jnp.flip(from_chunks(bwd), axis=1)


def mla_attention(q_nope, q_rope, k_nope, k_rope, v):
    B, S, H, _ = q_nope.shape
    nq = S // Q_BLOCK
    scale = MLA_QK_DIM ** -0.5

    def block(args):
        qn, qr = args
        s = (jnp.einsum('bqhd,bkhd->bhqk', qn, k_nope)
             + jnp.einsum('bqhr,bkr->bhqk', qr, k_rope)).astype(jnp.float32) * scale
        p = jax.nn.softmax(s, axis=-1).astype(v.dtype)
        return jnp.einsum('bhqk,bkhd->bqhd', p, v)

    def to_blocks(t):
        return jnp.moveaxis(t.reshape(B, nq, Q_BLOCK, *t.shape[2:]), 1, 0)

    out = lax.map(block, (to_blocks(q_nope), to_blocks(q_rope)))
    return jnp.moveaxis(out, 0, 1).reshape(B, S, H * MLA_V_DIM)


def setup_inputs(seed: int = 0) -> dict:
    key = jax.random.key(seed)
    ks = jax.random.split(key, 24)
    f32 = jnp.float32

    def w(k, shape, fan_in, gain=1.0):
        return jax.random.normal(k, shape, f32) * (gain * fan_in ** -0.5)

    def gain(k, shape):
        return 1.0 + 0.02 * jax.random.normal(k, shape, f32)

    x = jax.random.normal(ks[0], (BATCH, SEQ, D_MODEL), f32)
    c = jax.random.normal(ks[1], (BATCH, D_MODEL), f32)
    positions = (jnp.arange(SEQ, dtype=jnp.int32)[None, :]
                 + jax.random.randint(ks[2], (BATCH, 1), 0, 1024, dtype=jnp.int32))
    gamma_ms = 1.0 - 2.0 ** (-5.0 - np.arange(RET_HEADS, dtype=np.float32))
    a0 = jnp.asarray(np.log(-np.log(gamma_ms)), f32)
    ret_decay = a0[None, None, :] + 0.05 * jax.random.normal(ks[3], (DEPTH, 2, RET_HEADS), f32)
    return {
        "x": x,
        "c": c,
        "positions": positions,
        "ada_w": w(ks[4], (DEPTH, D_MODEL, N_MOD * D_MODEL), D_MODEL, 0.5),
        "ada_b": 0.01 * jax.random.normal(ks[5], (DEPTH, N_MOD * D_MODEL), f32),
        "norm1_w": gain(ks[6], (DEPTH, D_MODEL)),
        "w_in": w(ks[7], (DEPTH, D_MODEL, IN_COLS), D_MODEL),
        "ret_decay": ret_decay,
        "ret_gn_w": gain(ks[8], (DEPTH, RET_WIDTH)),
        "ret_gn_b": 0.01 * jax.random.normal(ks[9], (DEPTH, RET_WIDTH), f32),
        "mla_q_norm_w": gain(ks[10], (DEPTH, MLA_Q_RANK)),
        "w_uq": w(ks[11], (DEPTH, MLA_Q_RANK, MLA_HEADS * MLA_QK_DIM), MLA_Q_RANK),
        "mla_kv_norm_w": gain(ks[12], (DEPTH, MLA_KV_RANK)),
        "w_ukv": w(ks[13], (DEPTH, MLA_KV_RANK, MLA_HEADS * (MLA_NOPE_DIM + MLA_V_DIM)), MLA_KV_RANK),
        "mla_out_w": gain(ks[14], (DEPTH, MLA_WIDTH)),
        "w_o": w(ks[15], (DEPTH, MIX_WIDTH, D_MODEL), MIX_WIDTH),
        "norm2_w": gain(ks[16], (DEPTH, D_MODEL)),
        "w_gate": w(ks[17], (DEPTH, D_MODEL, D_FF), D_MODEL),
        "w_up": w(ks[18], (DEPTH, D_MODEL, D_FF), D_MODEL),
        "w_down": w(ks[19], (DEPTH, D_FF, D_MODEL), D_FF),
        "final_norm_w": gain(ks[20], (D_MODEL,)),
    }


def reference(x, c, positions, ada_w, ada_b, norm1_w, w_in, ret_decay, ret_gn_w, ret_gn_b,
              mla_q_norm_w, w_uq, mla_kv_norm_w, w_ukv, mla_out_w, w_o, norm2_w,
              w_gate, w_up, w_down, final_norm_w):
    B, S, _ = x.shape
    cos_r, sin_r = rope_tables(positions, RET_HEAD_DIM)
    cos_m, sin_m = rope_tables(positions, MLA_ROPE_DIM)
    c_act = jax.nn.silu(c)
    split_at = list(np.cumsum([RET_WIDTH, RET_WIDTH, RET_WIDTH, RET_WIDTH, MLA_Q_RANK, MLA_KV_RANK]))

    for l in range(DEPTH):
        mod = (c_act @ ada_w[l] + ada_b[l])[:, None, :]
        shift_a, scale_a, gate_a, shift_f, scale_f, gate_f = jnp.split(mod, N_MOD, axis=-1)

        h = rmsnorm(x, norm1_w[l]) * (1.0 + scale_a) + shift_a
        proj = h @ w_in[l]
        q_r, k_r, v_r, g_r, cq, ckv, k_rope = jnp.split(proj, split_at, axis=-1)

        hs = (B, S, RET_HEADS, RET_HEAD_DIM)
        q_r = apply_rope(q_r.reshape(hs), cos_r[:, :, None], sin_r[:, :, None])
        k_r = apply_rope(k_r.reshape(hs), cos_r[:, :, None], sin_r[:, :, None]) * (RET_HEAD_DIM ** -0.5)
        log_gamma = -jnp.exp(ret_decay[l].astype(jnp.float32))
        y_r = bidirectional_retention(q_r, k_r, v_r.reshape(hs), log_gamma)
        mu = jnp.mean(y_r, axis=-1, keepdims=True)
        var = jnp.mean(jnp.square(y_r - mu), axis=-1, keepdims=True)
        y_r = ((y_r - mu) * lax.rsqrt(var + EPS)).reshape(B, S, RET_WIDTH)
        y_r = y_r * ret_gn_w[l].astype(jnp.float32) + ret_gn_b[l].astype(jnp.float32)
        ret_out = (jax.nn.silu(g_r.astype(jnp.float32)) * y_r).astype(x.dtype)

        q_m = (rmsnorm(cq, mla_q_norm_w[l]) @ w_uq[l]).reshape(B, S, MLA_HEADS, MLA_QK_DIM)
        q_nope, q_rope = jnp.split(q_m, [MLA_NOPE_DIM], axis=-1)
        q_rope = apply_rope(q_rope, cos_m[:, :, None], sin_m[:, :, None])
        kv = (rmsnorm(ckv, mla_kv_norm_w[l]) @ w_ukv[l]).reshape(B, S, MLA_HEADS, MLA_NOPE_DIM + MLA_V_DIM)
        k_nope, v_m = jnp.split(kv, [MLA_NOPE_DIM], axis=-1)
        k_rope = apply_rope(k_rope, cos_m, sin_m)
        mla_out = rmsnorm(mla_attention(q_nope, q_rope, k_nope, k_rope, v_m), mla_out_w[l])

        mixed = jnp.concatenate([ret_out, mla_out], axis=-1) @ w_o[l]
        x = x + gate_a * mixed

        h = rmsnorm(x, norm2_w[l]) * (1.0 + scale_f) + shift_f
        ff = (jax.nn.silu(h @ w_gate[l]) * (h @ w_up[l])) @ w_down[l]
        x = x + gate_f * ff

    return rmsnorm(x, final_norm_w)
```

```python
import math
from contextlib import ExitStack

import numpy as np
import ml_dtypes

import concourse.bass as bass
import concourse.mybir as mybir
from concourse.bass_utils import run_bass_kernel_spmd

F32 = mybir.dt.float32
BF16 = mybir.dt.bfloat16
I32 = mybir.dt.int32
ALU = mybir.AluOpType
AF = mybir.ActivationFunctionType

S = 4096
D = 2048
NT = S // 128
NTT = S // 512
KC = D // 128
DFF = 5632
FC = DFF // 128
EPS = 1e-6
ENGS = ["sync", "gpsimd", "scalar", "vector", "tensor"]

TWO_PI = 2.0 * math.pi
CW1 = 6.28125
CW2 = TWO_PI - CW1
PI_LO = 3.14159


class DSem:
    __slots__ = ("h", "n")

    def __init__(self, h):
        self.h = h
        self.n = 0


class Buf:
    __slots__ = ("t", "w", "r", "pr", "ds")

    def __init__(self, t, ds=None):
        self.t = t
        self.w = {}
        self.r = {}
        self.pr = {}
        self.ds = ds


def _merge(dst, tok):
    s, v = tok
    k = id(s)
    if k not in dst or dst[k][1] < v:
        dst[k] = (s, v)


class Prog:
    def __init__(self, nc, stack):
        self.nc = nc
        self.stack = stack
        self.ops = {e: [] for e in ENGS}
        self.cnt = {e: 0 for e in ENGS}
        self.esem = {}
        for e in ["scalar", "vector", "tensor", "gpsimd"]:
            self.esem[e] = stack.enter_context(nc.semaphore("es_" + e))
        self.waited = {}
        self.nsem = 0
        self.nops = 0
        self.dsems = []

    def barrier_tokens(self):
        d = {}
        for e, s in self.esem.items():
            if self.cnt[e] > 0:
                d[id(s)] = (s, self.cnt[e])
        for ds in self.dsems:
            if ds.n > 0:
                d[id(ds.h)] = (ds.h, ds.n)
        return d

    def dsem(self):
        self.nsem += 1
        ds = DSem(self.stack.enter_context(self.nc.semaphore("ds%d" % self.nsem)))
        self.dsems.append(ds)
        return ds

    def op(self, eng, fn, reads=(), writes=(), appends=(), dsem=None):
        deps = {}
        for b in reads:
            for tok in b.w.values():
                _merge(deps, tok)
        for b in writes:
            if b.r:
                for tok in b.r.values():
                    _merge(deps, tok)
            else:
                for tok in b.w.values():
                    _merge(deps, tok)
        for b in appends:
            for tok in b.r.values():
                _merge(deps, tok)
            for tok in b.pr.values():
                _merge(deps, tok)
        waits = []
        for (s, v) in deps.values():
            key = (eng, id(s))
            if self.waited.get(key, 0) < v:
                self.waited[key] = v
                waits.append((s, v))
        if dsem is not None:
            dsem.n += 16
            tok = (dsem.h, dsem.n)
            inc = (dsem.h, 16)
        else:
            s = self.esem[eng]
            self.cnt[eng] += 1
            tok = (s, self.cnt[eng])
            inc = (s, 1)
        self.ops[eng].append((waits, fn, inc))
        self.nops += 1
        for b in reads:
            _merge(b.r, tok)
        for b in writes:
            b.w = {id(tok[0]): tok}
            b.pr = b.r
            b.r = {}
        for b in appends:
            _merge(b.w, tok)
        return tok

    def emit(self, final_bufs):
        nc = self.nc
        fin = {}
        for b in final_bufs:
            for tok in b.w.values():
                _merge(fin, tok)
            for tok in b.r.values():
                _merge(fin, tok)

        with nc.Block() as block:
            def mk(ename):
                def body(e):
                    for (waits, fn, inc) in self.ops[ename]:
                        for (s, v) in waits:
                            e.wait_ge(s, v)
                        ins = fn(e)
                        ins.then_inc(inc[0], inc[1])
                    if ename == "sync":
                        for (s, v) in fin.values():
                            e.wait_ge(s, v)
                return body
            block.sync(mk("sync"))
            block.gpsimd(mk("gpsimd"))
            block.scalar(mk("scalar"))
            block.vector(mk("vector"))
            block.tensor(mk("tensor"))


class Ring:
    def __init__(self, bufs):
        self.bufs = bufs
        self.i = 0

    def next(self):
        b = self.bufs[self.i % len(self.bufs)]
        self.i += 1
        return b


class Builder:
    def __init__(self, debug=False, upto="C"):
        self.debug = debug
        self.upto = upto
        self.nc = bass.Bass("TRN2", target_bir_lowering=False)
        self.inputs = {}
        self.outs = {}

    def din(self, name, shape, dt=F32):
        t = self.nc.dram_tensor(name, list(shape), dt, kind="ExternalInput")
        self.inputs[name] = t
        return t.ap()

    def dscr(self, name, shape, dt):
        kind = "ExternalOutput" if self.debug else "Internal"
        t = self.nc.dram_tensor(name, list(shape), dt, kind=kind)
        if self.debug:
            self.outs[name] = t
        return Buf(t.ap())

    def tile(self, st, name, shape, dt, dma=False):
        t = st.enter_context(self.nc.sbuf_tensor(name, list(shape), dt))
        ds = None
        if dma:
            pool = self.free_ds_sw if dma == "sw" else self.free_ds
            ds = pool.pop() if pool else self.P.dsem()
            st.callback(pool.append, ds)
        b = Buf(t, ds)
        b.r = dict(self.barrier)
        return b

    def ptile(self, st, name, shape, dt):
        t = st.enter_context(self.nc.psum_tensor(name, list(shape), dt))
        b = Buf(t)
        b.r = dict(self.barrier)
        return b

    def fence(self):
        self.barrier = self.P.barrier_tokens()

    def dma(self, eng, out_ap, in_ap, sem_buf, reads=(), writes=(), appends=()):
        return self.P.op(eng, lambda e: e.dma_start(out=out_ap, in_=in_ap), reads=reads, writes=writes,
                         appends=appends, dsem=sem_buf.ds)

    def build(self):
        nc = self.nc
        with ExitStack() as top:
            self.P = Prog(nc, top)
            self.barrier = {}
            self.free_ds = []
            self.free_ds_sw = []
            self.declare()
            self.consts(top)
            self.stage0(top)
            if self.upto >= "A":
                self.stageA(top)
            if self.upto >= "B":
                self.stageB_mla(top)
            if self.upto >= "B2":
                self.stageB_ret(top)
            if self.upto >= "C":
                self.stageC(top)
            fin = [self.out_buf, self.wo_b, self.wg_b, self.wd_b] + ([] if not self.debug else list(self.dbg_bufs))
            self.P.emit(fin)
        return nc

    def declare(self):
        d = self.din
        self.x = d("x", [S, D])
        self.cT = d("cT", [128, KC])
        self.pos = d("pos", [1, S], I32)
        self.ada_w = d("ada_w", [D, 6 * D])
        self.ada_b = d("ada_b", [1, 6 * D])
        self.norm1_w = d("norm1_w", [1, D])
        self.w_in = d("w_in", [D, 5120])
        self.w_kr = d("w_kr", [D, 256])
        self.ret_decay = d("ret_decay", [1, 16])
        self.gnw = d("gnw", [128, 8])
        self.gnb = d("gnb", [128, 8])
        self.qnw = d("qnw", [128, 4])
        self.kvnw = d("kvnw", [128, 4])
        self.w_uqn = d("w_uqn", [512, 1024])
        self.w_uqr = d("w_uqr", [512, 512])
        self.w_uk = d("w_uk", [512, 1024])
        self.w_uv = d("w_uv", [512, 1024])
        self.mow = d("mow", [128, 8])
        self.w_o = d("w_o", [D, D])
        self.norm2_w = d("norm2_w", [1, D])
        self.w_gate = d("w_gate", [D, DFF])
        self.w_up = d("w_up", [D, DFF])
        self.w_down = d("w_down", [DFF, D])
        self.fnw = d("fnw", [1, D])
        self.invf = d("invf", [128, 2])
        out_t = nc_out = self.nc.dram_tensor("out", [S, D], F32, kind="ExternalOutput")
        self.outs["out"] = out_t
        self.out = out_t.ap()
        self.out_buf = Buf(self.out)
        s = self.dscr
        self.tabs = [s("tab%d" % i, [128, S], F32) for i in range(4)]
        self.qr_s = [s("qr_s%d" % h, [128, S], BF16) for h in range(8)]
        self.kr_s = [s("kr_s%d" % h, [128, S], BF16) for h in range(8)]
        self.vr_s = s("vr_s", [8, 128, NT, 128], BF16)
        self.sg_s = [s("sg_s%d" % h, [128, S], BF16) for h in range(8)]
        self.qn_s = [s("qn_s%d" % h, [128, S], BF16) for h in range(8)]
        self.qp_s = [s("qp_s%d" % h, [128, S], BF16) for h in range(4)]
        self.kn_s = [s("kn_s%d" % h, [128, S], BF16) for h in range(8)]
        self.vm_s = s("vm_s", [8, 128, NT, 128], BF16)
        self.kro_s = [s("kro_s%d" % h, [128, S], BF16) for h in range(2)]
        self.mix_s = s("mix_s", [D, S], BF16)
        self.rstdm_s = s("rstdm_s", [128, S], F32)
        self.cb_s = [s("cb_s%d" % i, [128, D], F32) for i in range(7)]
        self.wo_b = Buf(self.nc.dram_tensor("wo_t", [4, 128, KC, 512], BF16, kind="Internal").ap())
        self.wg_b = Buf(self.nc.dram_tensor("wgu_t", [FC // 2, 128, KC, 512], BF16, kind="Internal").ap())
        self.wd_b = Buf(self.nc.dram_tensor("wd_t", [4, 4, 128, 11, 512], BF16, kind="Internal").ap())
        self.dbg_bufs = list(self.tabs) + self.qr_s + self.kr_s + [self.vr_s] + self.sg_s + self.qn_s + \
            self.qp_s + self.kn_s + [self.vm_s] + self.kro_s + [self.mix_s, self.rstdm_s] + self.cb_s

    def consts(self, st):
        P = self.P
        T = lambda name, shape, dt, dma=False: self.tile(st, name, shape, dt, dma)
        self.ident = T("ident", [128, 128], BF16)
        self.ones32 = T("ones32", [128, 128], F32)
        self.onesb = T("onesb", [128, 128], BF16)
        self.epsT = T("epsT", [128, 1], F32)
        self.halfpi = T("halfpi", [128, 1], F32)
        with ExitStack() as tmp:
            ioi = self.tile(tmp, "ioi", [128, 128], I32)
            iof = self.tile(tmp, "iof", [128, 128], F32)
            P.op("gpsimd", lambda e: e.iota(ioi.t[:], pattern=[[1, 128]], base=0, channel_multiplier=-1), writes=[ioi])
            P.op("vector", lambda e: e.tensor_copy(out=iof.t[:], in_=ioi.t[:]), reads=[ioi], writes=[iof])
            P.op("vector", lambda e: e.tensor_single_scalar(out=self.ident.t[:], in_=iof.t[:], scalar=0.0, op=ALU.is_equal),
                 reads=[iof], writes=[self.ident])
            P.op("vector", lambda e: e.memset(self.ones32.t[:], 1.0), writes=[self.ones32])
            P.op("vector", lambda e: e.memset(self.onesb.t[:], 1.0), writes=[self.onesb])
            P.op("vector", lambda e: e.memset(self.epsT.t[:], EPS), writes=[self.epsT])
            P.op("vector", lambda e: e.memset(self.halfpi.t[:], math.pi / 2), writes=[self.halfpi])
        self.alias_guard = [self.ident]

    def stage0(self, top):
        P = self.P
        nc = self.nc
        self.fence()
        with ExitStack() as st:
            T = lambda name, shape, dt, dma=False: self.tile(st, name, shape, dt, dma)
            cT = T("cTs", [128, KC], F32, dma=True)
            cact = T("cact", [128, KC], F32)
            crep = T("crep", [128, KC, 128], BF16)
            adab = T("adab", [128, 6 * D], F32, dma=True)
            n1w = T("n1w", [128, D], F32, dma=True)
            n2w = T("n2w", [128, D], F32, dma=True)
            fnw = T("fnwb", [128, D], F32, dma=True)
            mod = [T("mod%d" % i, [128, D], F32, dma=True) for i in range(6)]
            wr = Ring([T("adw%d" % i, [128, KC, 512], BF16, dma="sw") for i in range(2)])
            pm = Ring([self.ptile(st, "pm%d" % i, [128, 512], F32) for i in range(2)])
            self.dma("sync", cT.t[:], self.cT, cT, writes=[cT])
            self.dma("sync", adab.t[:], self.ada_b[0].partition_broadcast(128), adab, writes=[adab])
            self.dma("sync", n1w.t[:], self.norm1_w[0].partition_broadcast(128), n1w, writes=[n1w])
            self.dma("sync", n2w.t[:], self.norm2_w[0].partition_broadcast(128), n2w, writes=[n2w])
            self.dma("sync", fnw.t[:], self.fnw[0].partition_broadcast(128), fnw, writes=[fnw])
            P.op("scalar", lambda e: e.activation(out=cact.t[:], in_=cT.t[:], func=AF.Silu), reads=[cT], writes=[cact])
            for kc in range(KC):
                P.op("vector", lambda e, kc=kc: e.tensor_copy(out=crep.t[:, kc, :], in_=cact.t[:, kc:kc + 1].to_broadcast([128, 128])),
                     reads=[cact], appends=[crep])
            adw_v = self.ada_w.rearrange("(kc p) n -> p kc n", p=128)
            QW = S // 4
            posi = T("posi", [128, QW], I32, dma=True)
            posf = T("posf", [128, QW], F32)
            invf = T("invf_sb", [128, 2], F32, dma=True)
            ang = T("ang", [128, QW], F32)
            kf = T("kf", [128, QW], F32)
            ki = T("ki", [128, QW], I32)
            r = T("r", [128, QW], F32)
            msk = T("msk", [128, QW], F32)
            res = [T("res%d" % i, [128, QW], F32, dma=True) for i in range(2)]
            self.dma("sync", invf.t[:], self.invf, invf, writes=[invf])
            V = lambda fn, reads, writes: P.op("vector", fn, reads=reads, writes=writes)

            def tab_piece(j, qi):
                cs = slice(qi * QW, (qi + 1) * QW)
                self.dma("sync", posi.t[:], self.pos[0, cs].partition_broadcast(128), posi, writes=[posi])
                V(lambda e: e.tensor_copy(out=posf.t[:], in_=posi.t[:]), [posi], [posf])
                V(lambda e: e.tensor_scalar_mul(out=ang.t[:], in0=posf.t[:], scalar1=invf.t[:, j:j + 1]), [posf, invf], [ang])
                V(lambda e: e.tensor_scalar(out=kf.t[:], in0=ang.t[:], scalar1=1.0 / TWO_PI, scalar2=0.5, op0=ALU.mult, op1=ALU.add), [ang], [kf])
                V(lambda e: e.tensor_copy(out=ki.t[:], in_=kf.t[:]), [kf], [ki])
                V(lambda e: e.tensor_copy(out=kf.t[:], in_=ki.t[:]), [ki], [kf])
                V(lambda e: e.scalar_tensor_tensor(out=r.t[:], in0=kf.t[:], scalar=-CW1, in1=ang.t[:], op0=ALU.mult, op1=ALU.add), [kf, ang], [r])
                V(lambda e: e.scalar_tensor_tensor(out=r.t[:], in0=kf.t[:], scalar=-CW2, in1=r.t[:], op0=ALU.mult, op1=ALU.add), [kf, r], [r])
                V(lambda e: e.tensor_single_scalar(out=msk.t[:], in_=r.t[:], scalar=-math.pi, op=ALU.is_lt), [r], [msk])
                V(lambda e: e.scalar_tensor_tensor(out=r.t[:], in0=msk.t[:], scalar=TWO_PI, in1=r.t[:], op0=ALU.mult, op1=ALU.add), [msk, r], [r])
                V(lambda e: e.tensor_single_scalar(out=msk.t[:], in_=r.t[:], scalar=math.pi, op=ALU.is_gt), [r], [msk])
                V(lambda e: e.scalar_tensor_tensor(out=r.t[:], in0=msk.t[:], scalar=-TWO_PI, in1=r.t[:], op0=ALU.mult, op1=ALU.add), [msk, r], [r])
                V(lambda e: e.tensor_scalar(out=r.t[:], in0=r.t[:], scalar1=-PI_LO, scalar2=PI_LO, op0=ALU.max, op1=ALU.min), [r], [r])
                P.op("scalar", lambda e: e.activation(out=res[1].t[:], in_=r.t[:], func=AF.Sin), reads=[r], writes=[res[1]])
                self.dma("sync", self.tabs[2 * j + 1].t[:, cs], res[1].t[:], res[1], reads=[res[1]], appends=[self.tabs[2 * j + 1]])
                V(lambda e: e.scalar_tensor_tensor(out=msk.t[:], in0=r.t[:], scalar=-1.0, in1=r.t[:], op0=ALU.mult, op1=ALU.max), [r], [msk])
                P.op("scalar", lambda e: e.activation(out=res[0].t[:], in_=msk.t[:], func=AF.Sin, scale=-1.0, bias=self.halfpi.t[:]),
                     reads=[msk, self.halfpi], writes=[res[0]])
                self.dma("sync", self.tabs[2 * j].t[:, cs], res[0].t[:], res[0], reads=[res[0]], appends=[self.tabs[2 * j]])

            tab_jobs = [(j, qi) for j in range(2) for qi in range(4)]
            for g in range(24):
                def ada_g(g):
                    wb = wr.next()
                    self.dma("gpsimd", wb.t[:], adw_v[:, :, g * 512:(g + 1) * 512], wb, writes=[wb])
                    pb = pm.next()

                    def mm(e):
                        ins = None
                        for kc in range(KC):
                            ins = e.matmul(pb.t[:], lhsT=crep.t[:, kc, :], rhs=wb.t[:, kc, :], start=(kc == 0), stop=(kc == KC - 1))
                        return ins
                    P.op("tensor", mm, reads=[crep, wb], writes=[pb])
                    m = mod[g // 4]
                    c0 = (g % 4) * 512
                    P.op("vector", lambda e: e.tensor_tensor(out=m.t[:, c0:c0 + 512], in0=pb.t[:], in1=adab.t[:, g * 512:(g + 1) * 512], op=ALU.add),
                         reads=[pb, adab], appends=[m])
                ada_g(g)
                if g % 3 == 1 and tab_jobs:
                    tab_piece(*tab_jobs.pop(0))
            while tab_jobs:
                tab_piece(*tab_jobs.pop(0))
            shift_a, scale_a, gate_a, shift_f, scale_f, gate_f = mod
            P.op("vector", lambda e: e.scalar_tensor_tensor(out=scale_a.t[:], in0=scale_a.t[:], scalar=1.0, in1=n1w.t[:],
                                                            op0=ALU.add, op1=ALU.mult), reads=[scale_a, n1w], writes=[scale_a])
            P.op("vector", lambda e: e.scalar_tensor_tensor(out=scale_f.t[:], in0=scale_f.t[:], scalar=1.0, in1=n2w.t[:],
                                                            op0=ALU.add, op1=ALU.mult), reads=[scale_f, n2w], writes=[scale_f])
            for i, src in enumerate([gate_a, scale_f, shift_f, gate_f, fnw, scale_a, shift_a]):
                self.dma("sync", self.cb_s[i].t, src.t[:], src, reads=[src], writes=[self.cb_s[i]])

    def rope_fm(self, pb, cos_t, sin_t, scale, tmpA, tmpB, ob):
        P = self.P
        P.op("vector", lambda e: e.scalar_tensor_tensor(out=tmpA.t[:], in0=pb.t[:], scalar=scale, in1=cos_t.t[:], op0=ALU.mult, op1=ALU.mult),
             reads=[pb, cos_t], writes=[tmpA])
        P.op("vector", lambda e: e.scalar_tensor_tensor(out=tmpB.t[0:64, :], in0=pb.t[64:128, :], scalar=scale, in1=sin_t.t[64:128, :],
                                                        op0=ALU.mult, op1=ALU.mult), reads=[pb, sin_t], writes=[tmpB])
        P.op("vector", lambda e: e.scalar_tensor_tensor(out=tmpB.t[64:128, :], in0=pb.t[0:64, :], scalar=scale, in1=sin_t.t[0:64, :],
                                                        op0=ALU.mult, op1=ALU.mult), reads=[pb, sin_t], appends=[tmpB])
        P.op("gpsimd", lambda e: e.tensor_tensor(out=ob.t[0:64, :], in0=tmpA.t[0:64, :], in1=tmpB.t[0:64, :], op=ALU.subtract),
             reads=[tmpA, tmpB], writes=[ob])
        P.op("gpsimd", lambda e: e.tensor_tensor(out=ob.t[64:128, :], in0=tmpA.t[64:128, :], in1=tmpB.t[64:128, :], op=ALU.add),
             reads=[tmpA, tmpB], appends=[ob])

    def stageA(self, top):
        P = self.P
        self.fence()
        with ExitStack() as st:
            T = lambda name, shape, dt, dma=False: self.tile(st, name, shape, dt, dma)
            hT = st.enter_context(self.nc.sbuf_tensor("hT", [128, KC, S], BF16))
            hTb = [Buf(hT) for _ in range(NTT)]
            for b_ in hTb:
                b_.r = dict(self.barrier)
            with ExitStack() as s1:
                T1 = lambda name, shape, dt, dma=False: self.tile(s1, name, shape, dt, dma)
                self.Aabc = T1("Aabc", [128, D], F32, dma=True)
                self.Babc = T1("Babc", [128, D], F32, dma=True)
                self.dma("sync", self.Aabc.t[:], self.cb_s[5].t, self.Aabc, reads=[self.cb_s[5]], writes=[self.Aabc])
                self.dma("sync", self.Babc.t[:], self.cb_s[6].t, self.Babc, reads=[self.cb_s[6]], writes=[self.Babc])
                xr = Ring([T1("xt%d" % i, [128, D], F32, dma=True) for i in range(3)])
                junk = T1("junk", [128, D], BF16)
                msr = Ring([T1("ms%d" % i, [128, 1], F32) for i in range(3)])
                sdr = Ring([T1("sd%d" % i, [128, 1], F32) for i in range(3)])
                rsr = Ring([T1("rs%d" % i, [128, 1], F32) for i in range(3)])
                hbr = Ring([T1("hb%d" % i, [128, D], BF16) for i in range(2)])
                ptr = Ring([self.ptile(s1, "ptA%d" % i, [128, 1024], BF16) for i in range(4)])
                pending = []

                def flush():
                    for (eng, dst, src, pt, tb) in pending:
                        if eng == "scalar":
                            P.op(eng, lambda e, dst=dst, src=src: e.activation(out=dst, in_=src, func=AF.Copy), reads=[pt], appends=[tb])
                        else:
                            P.op(eng, lambda e, dst=dst, src=src: e.tensor_copy(out=dst, in_=src), reads=[pt], appends=[tb])
                    del pending[:]

                for i in range(NT):
                    def a1(i):
                        xt = xr.next(); ms = msr.next(); sd = sdr.next(); rs = rsr.next(); hb = hbr.next()
                        self.dma("sync", xt.t[:], self.x[i * 128:(i + 1) * 128, :], xt, writes=[xt])
                        P.op("scalar", lambda e: e.activation(out=junk.t[:], in_=xt.t[:], func=AF.Square, scale=1.0 / math.sqrt(D),
                                                              accum_out=ms.t[:]), reads=[xt], writes=[ms])
                        P.op("scalar", lambda e: e.activation(out=sd.t[:], in_=ms.t[:], func=AF.Sqrt, bias=self.epsT.t[:], scale=1.0),
                             reads=[ms, self.epsT], writes=[sd])
                        P.op("vector", lambda e: e.reciprocal(out=rs.t[:], in_=sd.t[:]), reads=[sd], writes=[rs])
                        P.op("vector", lambda e: e.scalar_tensor_tensor(out=xt.t[:], in0=xt.t[:], scalar=rs.t[:, 0:1], in1=self.Aabc.t[:],
                                                                        op0=ALU.mult, op1=ALU.mult), reads=[xt, rs, self.Aabc], writes=[xt])
                        H = D // 2
                        P.op("gpsimd", lambda e: e.tensor_tensor(out=hb.t[:, 0:H], in0=xt.t[:, 0:H], in1=self.Babc.t[:, 0:H], op=ALU.add),
                             reads=[xt, self.Babc], writes=[hb])
                        P.op("vector", lambda e: e.tensor_tensor(out=hb.t[:, H:D], in0=xt.t[:, H:D], in1=self.Babc.t[:, H:D], op=ALU.add),
                             reads=[xt, self.Babc], appends=[hb])
                        flush()
                        for half in range(2):
                            pt = ptr.next()

                            def tr(e, pt=pt, half=half):
                                ins = None
                                for j in range(8):
                                    kc = half * 8 + j
                                    ins = e.transpose(out=pt.t[:, j * 128:(j + 1) * 128], in_=hb.t[:, kc * 128:(kc + 1) * 128], identity=self.ident.t[:])
                                return ins
                            P.op("tensor", tr, reads=[hb, self.ident], writes=[pt])
                            eng = "scalar" if half == 0 else "vector"
                            dst = hT[:, half * 8:(half + 1) * 8, i * 128:(i + 1) * 128]
                            src = pt.t[:].rearrange("p (k t) -> p k t", k=8)
                            pending.append((eng, dst, src, pt, hTb[i // 4]))
                    a1(i)
                flush()
            self.fence()
            with ExitStack() as s2:
                T2 = lambda name, shape, dt, dma=False: self.tile(s2, name, shape, dt, dma)
                wr = Ring([T2("wA%d" % i, [128, KC, 256], BF16, dma="sw") for i in range(2)])
                cosr = Ring([T2("cosA%d" % i, [128, 512], F32, dma=True) for i in range(2)])
                sinr = Ring([T2("sinA%d" % i, [128, 512], F32, dma=True) for i in range(2)])
                tmpAr = Ring([T2("tmpA%d" % i, [128, 512], F32) for i in range(2)])
                tmpBr = Ring([T2("tmpB%d" % i, [128, 512], F32) for i in range(2)])
                obr = Ring([T2("obA%d" % i, [128, 512], BF16, dma=True) for i in range(4)])
                sq = T2("sqA", [128, 4, 512], BF16)
                lat32 = T2("lat32", [128, 4, 512], F32)
                sdA = T2("sdA", [128, 512], F32)
                rsA = T2("rsA", [128, 512], F32)
                latR = Ring([T2("lat%d" % i, [128, 4, 512], BF16) for i in range(2)])
                wup = T2("wup", [128, 4, 2048], BF16, dma="sw")
                nwq = T2("nwq", [128, 4], F32, dma=True)
                nwkv = T2("nwkv", [128, 4], F32, dma=True)
                psA = Ring([self.ptile(s2, "psA%d" % i, [128, 512], F32) for i in range(4)])
                pss = self.ptile(s2, "pssA", [128, 512], F32)
                pu = Ring([self.ptile(s2, "puA%d" % i, [128, 512], F32) for i in range(3)])
                self.dma("sync", nwq.t[:], self.qnw, nwq, writes=[nwq])
                self.dma("sync", nwkv.t[:], self.kvnw, nwkv, writes=[nwkv])
                win_v = self.w_in.rearrange("(kc p) n -> p kc n", p=128)
                wkr_v = self.w_kr.rearrange("(kc p) n -> p kc n", p=128)
                groups = []
                for i in range(4):
                    groups.append(("cq" if i < 2 else "ckv", i % 2, win_v[:, :, 4096 + i * 256: 4096 + (i + 1) * 256]))
                groups.append(("kro", 0, wkr_v))
                for i in range(4):
                    groups.append(("q", i, win_v[:, :, i * 256:(i + 1) * 256]))
                for i in range(4):
                    groups.append(("k", i, win_v[:, :, 1024 + i * 256: 1024 + (i + 1) * 256]))
                for i in range(4):
                    groups.append(("v", i, win_v[:, :, 2048 + i * 256: 2048 + (i + 1) * 256]))
                for i in range(4):
                    groups.append(("g", i, win_v[:, :, 3072 + i * 256: 3072 + (i + 1) * 256]))
                wbufs = {}

                def load_w(gi):
                    if gi < len(groups) and gi not in wbufs:
                        wb = wr.next()
                        self.dma("gpsimd", wb.t[:], groups[gi][2], wb, writes=[wb])
                        wbufs[gi] = wb

                def proj_fm(pb, wb, j, tt):
                    def mm(e):
                        ins = None
                        for kc in range(KC):
                            ins = e.matmul(pb.t[:], lhsT=wb.t[:, kc, j * 128:(j + 1) * 128], rhs=hT[:, kc, tt * 512:(tt + 1) * 512],
                                           start=(kc == 0), stop=(kc == KC - 1))
                        return ins
                    P.op("tensor", mm, reads=[wb, hTb[tt]], writes=[pb])

                def store(dst_buf, dst_ap, ob):
                    self.dma("sync", dst_ap, ob.t[:], ob, reads=[ob], appends=[dst_buf])

                def load_tab(ci, tt):
                    ct = cosr.next(); stt = sinr.next()
                    self.dma("sync", ct.t[:], self.tabs[ci].t[:, tt * 512:(tt + 1) * 512], ct, reads=[self.tabs[ci]], writes=[ct])
                    self.dma("sync", stt.t[:], self.tabs[ci + 1].t[:, tt * 512:(tt + 1) * 512], stt, reads=[self.tabs[ci + 1]], writes=[stt])
                    return ct, stt

                gi = 0
                while gi < len(groups):
                    kind, idx, _ = groups[gi]
                    load_w(gi)
                    load_w(gi + 1)
                    if kind in ("cq", "ckv"):
                        wb0, wb1 = wbufs[gi], wbufs[gi + 1]
                        if kind == "cq":
                            self.dma("gpsimd", wup.t[:, :, 0:1024], self.w_uqn.rearrange("(kc p) n -> p kc n", p=128), wup, writes=[wup])
                            self.dma("gpsimd", wup.t[:, :, 1024:1536], self.w_uqr.rearrange("(kc p) n -> p kc n", p=128), wup, appends=[wup])
                            nw = nwq
                        else:
                            self.dma("gpsimd", wup.t[:, :, 0:1024], self.w_uk.rearrange("(kc p) n -> p kc n", p=128), wup, writes=[wup])
                            self.dma("gpsimd", wup.t[:, :, 1024:2048], self.w_uv.rearrange("(kc p) n -> p kc n", p=128), wup, appends=[wup])
                            nw = nwkv
                        def norm_part(tt, kind=kind, nw=nw, wb0=wb0, wb1=wb1):
                            pbs = [psA.next() for _ in range(4)]
                            for fc in range(4):
                                proj_fm(pbs[fc], wb0 if fc < 2 else wb1, fc % 2, tt)
                            for fc in range(4):
                                def ev(fc):
                                    pb = pbs[fc]
                                    P.op("scalar", lambda e: e.activation(out=lat32.t[:, fc, :], in_=pb.t[:], func=AF.Copy),
                                         reads=[pb], writes=[lat32] if fc == 0 else (), appends=[lat32] if fc > 0 else ())
                                    P.op("scalar", lambda e: e.activation(out=sq.t[:, fc, :], in_=pb.t[:], func=AF.Square),
                                         reads=[pb], writes=[sq] if fc == 0 else (), appends=[sq] if fc > 0 else ())
                                ev(fc)
                            return pbs

                        def norm_finish(tt, nw=nw):
                            lt = latR.next()

                            def ones_mm(e):
                                ins = None
                                for fc in range(4):
                                    ins = e.matmul(pss.t[:], lhsT=self.onesb.t[:], rhs=sq.t[:, fc, :], start=(fc == 0), stop=(fc == 3))
                                return ins
                            P.op("tensor", ones_mm, reads=[sq, self.onesb], writes=[pss])
                            P.op("scalar", lambda e: e.activation(out=sdA.t[:], in_=pss.t[:], func=AF.Ln, bias=self.epsT.t[:], scale=1.0 / 512.0),
                                 reads=[pss, self.epsT], writes=[sdA])
                            P.op("scalar", lambda e: e.activation(out=rsA.t[:], in_=sdA.t[:], func=AF.Exp, scale=-0.5), reads=[sdA], writes=[rsA])
                            for fc in range(4):
                                def nm(fc):
                                    P.op("vector", lambda e: e.scalar_tensor_tensor(
                                        out=lt.t[:, fc, :], in0=lat32.t[:, fc, :], scalar=nw.t[:, fc:fc + 1], in1=rsA.t[:], op0=ALU.mult, op1=ALU.mult),
                                        reads=[lat32, nw, rsA], writes=[lt] if fc == 0 else (), appends=[lt] if fc > 0 else ())
                                nm(fc)
                            return lt

                        def up_part(tt, lt, tabs_, kind=kind):
                            def up_fm(pb, c0):
                                def mm(e):
                                    ins = None
                                    for kc in range(4):
                                        ins = e.matmul(pb.t[:], lhsT=wup.t[:, kc, c0:c0 + 128], rhs=lt.t[:, kc, :], start=(kc == 0), stop=(kc == 3))
                                    return ins
                                P.op("tensor", mm, reads=[wup, lt], writes=[pb])

                            def plain(h, dst_list):
                                pb = pu.next(); ob = obr.next()
                                up_fm(pb, h * 128)
                                P.op("scalar", lambda e: e.activation(out=ob.t[:], in_=pb.t[:], func=AF.Copy), reads=[pb], writes=[ob])
                                store(dst_list[h], dst_list[h].t[:, tt * 512:(tt + 1) * 512], ob)
                            if kind == "cq":
                                ct, stt = tabs_
                                for h in range(8):
                                    plain(h, self.qn_s)
                                for pr in range(4):
                                    pb = pu.next(); ob = obr.next()
                                    up_fm(pb, 1024 + pr * 128)
                                    self.rope_fm(pb, ct, stt, 1.0, tmpAr.next(), tmpBr.next(), ob)
                                    store(self.qp_s[pr], self.qp_s[pr].t[:, tt * 512:(tt + 1) * 512], ob)
                            else:
                                for h in range(8):
                                    plain(h, self.kn_s)
                                for sub in range(4):
                                    for half in range(2):
                                        def vpart(sub, half):
                                            pb = pu.next(); ob = obr.next()

                                            def mmv(e):
                                                ins = None
                                                for kc in range(4):
                                                    ins = e.matmul(pb.t[:], lhsT=lt.t[:, kc, sub * 128:(sub + 1) * 128],
                                                                   rhs=wup.t[:, kc, 1024 + half * 512: 1024 + (half + 1) * 512], start=(kc == 0), stop=(kc == 3))
                                                return ins
                                            P.op("tensor", mmv, reads=[wup, lt], writes=[pb])
                                            P.op("vector", lambda e: e.tensor_copy(out=ob.t[:], in_=pb.t[:]), reads=[pb], writes=[ob])
                                            c = tt * 4 + sub
                                            dst = self.vm_s.t[half * 4:(half + 1) * 4, :, c, :].rearrange("h p e -> p h e")
                                            self.dma("sync", dst, ob.t[:].rearrange("p (h e) -> p h e", h=4), ob, reads=[ob], appends=[self.vm_s])
                                        vpart(sub, half)

                        prev = None
                        for tt in range(NTT):
                            tabs_ = load_tab(2, tt) if kind == "cq" else None
                            norm_part(tt)
                            if prev is not None:
                                up_part(*prev)
                            lt = norm_finish(tt)
                            prev = (tt, lt, tabs_)
                        up_part(*prev)
                        gi += 2
                        continue
                    wb = wbufs[gi]
                    if kind in ("q", "k", "kro"):
                        tabi = 0 if kind != "kro" else 2
                        nxt_tab = load_tab(tabi, 0)
                        for tt in range(NTT):
                            ct, stt = nxt_tab
                            if tt + 1 < NTT:
                                nxt_tab = load_tab(tabi, tt + 1)
                            for j in range(2):
                                pb = psA.next(); ob = obr.next()
                                proj_fm(pb, wb, j, tt)
                                self.rope_fm(pb, ct, stt, (128 ** -0.5) if kind == "k" else 1.0, tmpAr.next(), tmpBr.next(), ob)
                                if kind == "kro":
                                    dstb = self.kro_s[j]
                                else:
                                    dstb = (self.qr_s if kind == "q" else self.kr_s)[idx * 2 + j]
                                store(dstb, dstb.t[:, tt * 512:(tt + 1) * 512], ob)
                    elif kind == "g":
                        for tt in range(NTT):
                            for j in range(2):
                                pb = psA.next(); ob = obr.next()
                                proj_fm(pb, wb, j, tt)
                                P.op("scalar", lambda e, pb=pb, ob=ob: e.activation(out=ob.t[:], in_=pb.t[:], func=AF.Silu), reads=[pb], writes=[ob])
                                dstb = self.sg_s[idx * 2 + j]
                                store(dstb, dstb.t[:, tt * 512:(tt + 1) * 512], ob)
                    elif kind == "v":
                        for i in range(NT):
                            pb = psA.next(); ob = obr.next()

                            def mmv(e, pb=pb, i=i, wb=wb):
                                ins = None
                                for kc in range(KC):
                                    ins = e.matmul(pb.t[:, 0:256], lhsT=hT[:, kc, i * 128:(i + 1) * 128], rhs=wb.t[:, kc, :], start=(kc == 0), stop=(kc == KC - 1))
                                return ins
                            P.op("tensor", mmv, reads=[wb, hTb[i // 4]], writes=[pb])
                            eng = "scalar" if i % 2 == 0 else "vector"
                            if eng == "scalar":
                                P.op(eng, lambda e, pb=pb, ob=ob: e.activation(out=ob.t[:, 0:256], in_=pb.t[:, 0:256], func=AF.Copy), reads=[pb], writes=[ob])
                            else:
                                P.op(eng, lambda e, pb=pb, ob=ob: e.tensor_copy(out=ob.t[:, 0:256], in_=pb.t[:, 0:256]), reads=[pb], writes=[ob])
                            dst = self.vr_s.t[idx * 2:(idx + 1) * 2, :, i, :].rearrange("h p e -> p h e")
                            self.dma("sync", dst, ob.t[:, 0:256].rearrange("p (h e) -> p h e", h=2), ob, reads=[ob], appends=[self.vr_s])
                    gi += 1

    def stageB_mla(self, top):
        P = self.P
        SC = 192 ** -0.5
        self.fence()
        with ExitStack() as st:
            T = lambda name, shape, dt, dma=False: self.tile(st, name, shape, dt, dma)
            Qn = Ring([T("Qn%d" % i, [128, S], BF16, dma=True) for i in range(2)])
            Qp = Ring([T("Qp%d" % i, [128, S], BF16, dma=True) for i in range(2)])
            Kn = Ring([T("Kn%d" % i, [128, S], BF16, dma=True) for i in range(2)])
            Vm = Ring([T("Vm%d" % i, [128, NT, 128], BF16, dma=True) for i in range(2)])
            Kr = [T("Kr%d" % i, [128, S], BF16, dma=True) for i in range(2)]
            ssq = T("ssqm", [128, S], F32, dma=True)
            ptr = Ring([T("ptm%d" % i, [128, 512], BF16) for i in range(6)])
            rec = T("recm", [128, 512], F32)
            p2R = Ring([T("p2m%d" % i, [128, 512], BF16) for i in range(4)])
            o32 = Ring([T("o32m%d" % i, [128, 512], F32) for i in range(2)])
            sqm = T("sqm", [128, 512], F32)
            obr = Ring([T("obm%d" % i, [128, 512], BF16, dma=True) for i in range(2)])
            psS = Ring([self.ptile(st, "psS%d" % i, [128, 512], F32) for i in range(4)])
            poR = Ring([self.ptile(st, "poM%d" % i, [128, 512], F32) for i in range(2)])
            pdR = Ring([self.ptile(st, "pdM%d" % i, [128, 512], F32) for i in range(2)])
            for i in range(2):
                self.dma("sync", Kr[i].t[:], self.kro_s[i].t, Kr[i], reads=[self.kro_s[i]], writes=[Kr[i]])
            stg = Ring([T("stgW%d" % i, [128, KC, 512], BF16, dma="sw") for i in range(2)])
            pieces = []
            sv = self.w_o.rearrange("(kc p) n -> p kc n", p=128)
            for cg in range(4):
                pieces.append((sv[:, :, cg * 512:(cg + 1) * 512], None, self.wo_b.t[cg], None, self.wo_b, KC))
            for (src, c_off) in ((self.w_gate, 0), (self.w_up, 256)):
                sv = src.rearrange("(kc p) n -> p kc n", p=128)
                for i in range(FC // 4):
                    dv = self.wg_b.t[2 * i:2 * i + 2, :, :, c_off:c_off + 256].rearrange("g p kc c -> p kc g c")
                    pieces.append((sv[:, :, i * 512:(i + 1) * 512], "split", dv, None, self.wg_b, KC))
            sv = self.w_down.rearrange("(fc p) n -> p fc n", p=128)
            for fq in range(4):
                for cg in range(4):
                    pieces.append((sv[:, fq * 11:(fq + 1) * 11, cg * 512:(cg + 1) * 512], None, self.wd_b.t[cg, fq], None, self.wd_b, 11))
            cast_state = {"i": 0, "pend": None}

            def cast_step():
                if cast_state["pend"] is not None:
                    (dst_ap, src_sb, sb_, dbuf) = cast_state["pend"]
                    self.dma("gpsimd", dst_ap, src_sb, sb_, reads=[sb_], appends=[dbuf])
                    cast_state["pend"] = None
                i = cast_state["i"]
                if i < len(pieces):
                    (sap, mode, dst_ap, _, dbuf, nk) = pieces[i]
                    sb_ = stg.next()
                    self.dma("gpsimd", sb_.t[:, 0:nk, :], sap, sb_, writes=[sb_])
                    src_sb = sb_.t[:, 0:nk, :]
                    if mode == "split":
                        src_sb = sb_.t[:, :, :].rearrange("p kc (g c) -> p kc g c", g=2)
                    cast_state["pend"] = (dst_ap, src_sb, sb_, dbuf)
                    cast_state["i"] = i + 1
            loaded = {}

            def load_head(h):
                if h >= 8 or h in loaded:
                    return
                qn = Qn.next(); qp = Qp.next(); kn = Kn.next(); vm = Vm.next()
                self.dma("sync", qn.t[:], self.qn_s[h].t, qn, reads=[self.qn_s[h]], writes=[qn])
                self.dma("sync", qp.t[:], self.qp_s[h // 2].t, qp, reads=[self.qp_s[h // 2]], writes=[qp])
                self.dma("sync", kn.t[:], self.kn_s[h].t, kn, reads=[self.kn_s[h]], writes=[kn])
                self.dma("sync", vm.t[:], self.vm_s.t[h], vm, reads=[self.vm_s], writes=[vm])
                loaded[h] = (qn, qp, kn, vm)

            tails = []
            load_head(0)
            for h in range(8):
                load_head(h + 1)
                qn, qp, kn, vm = loaded[h]
                kr = Kr[h % 2]
                for qt in range(NTT):
                  def do_qt(h, qt, qn, qp, kn, vm, kr):
                    po = poR.next(); pd = pdR.next()
                    qs = slice(qt * 512, (qt + 1) * 512)

                    def s_op(kt):
                        ps = psS.next()

                        def mm(e, ps=ps, kt=kt):
                            e.matmul(ps.t[:], lhsT=kn.t[:, kt * 128:(kt + 1) * 128], rhs=qn.t[:, qs], start=True, stop=False)
                            return e.matmul(ps.t[:], lhsT=kr.t[:, kt * 128:(kt + 1) * 128], rhs=qp.t[:, qs], start=False, stop=True)
                        P.op("tensor", mm, reads=[kn, qn, kr, qp], writes=[ps])
                        return ps
                    LA = 2
                    psq = [s_op(k_) for k_ in range(LA)]
                    prev_tail = tails.pop(0) if tails else None
                    cast_step()
                    pt_prev = None
                    p2_prev = None
                    for kt in range(NT):
                        ps = psq.pop(0)
                        if kt + LA < NT:
                            psq.append(s_op(kt + LA))
                        if prev_tail is not None and kt in prev_tail:
                            prev_tail[kt]()
                        pt = ptr.next()
                        P.op("scalar", lambda e, ps=ps, pt=pt: e.activation(out=pt.t[:], in_=ps.t[:], func=AF.Exp, scale=SC), reads=[ps], writes=[pt])

                        def pv(e, pt=pt, kt=kt, po=po):
                            return e.matmul(po.t[:], lhsT=vm.t[:, kt, :], rhs=pt.t[:], start=(kt == 0), stop=(kt == NT - 1))
                        if kt == 0:
                            P.op("tensor", pv, reads=[vm, pt], writes=[po])
                        else:
                            P.op("tensor", pv, reads=[vm, pt], appends=[po])
                        if kt % 2 == 0:
                            pt_prev = pt
                        else:
                            p2 = p2R.next()
                            P.op("vector", lambda e, p2=p2, a=pt_prev, b=pt: e.tensor_tensor(out=p2.t[:], in0=a.t[:], in1=b.t[:], op=ALU.add), reads=[pt_prev, pt], writes=[p2])

                            def dn(e, p2=p2, kt=kt, pd=pd):
                                return e.matmul(pd.t[:], lhsT=self.onesb.t[:], rhs=p2.t[:], start=(kt == 1), stop=(kt == NT - 1))
                            if kt == 1:
                                P.op("tensor", dn, reads=[p2, self.onesb], writes=[pd])
                            else:
                                P.op("tensor", dn, reads=[p2, self.onesb], appends=[pd])
                    o3 = o32.next(); ob = obr.next()

                    def t3r(c):
                        def f():
                            cc = slice(c * 128, (c + 1) * 128)
                            P.op("vector", lambda e: e.reciprocal(out=rec.t[:, cc], in_=pd.t[:, cc]), reads=[pd], writes=[rec] if c == 0 else [], appends=[] if c == 0 else [rec])
                        return f

                    def t3():
                        P.op("vector", lambda e: e.tensor_tensor(out=o3.t[:], in0=po.t[:], in1=rec.t[:], op=ALU.mult), reads=[po, rec], writes=[o3])
                        P.op("scalar", lambda e: e.activation(out=sqm.t[:], in_=o3.t[:], func=AF.Square), reads=[o3], writes=[sqm])
                        P.op("vector", lambda e: e.tensor_copy(out=ob.t[:], in_=o3.t[:]), reads=[o3], writes=[ob])
                        r0 = 1024 + h * 128
                        self.dma("sync", self.mix_s.t[r0:r0 + 128, qs], ob.t[:], ob, reads=[ob], appends=[self.mix_s])

                    def t45():
                        P.op("tensor", lambda e: e.matmul(pd.t[:], lhsT=self.ones32.t[:], rhs=sqm.t[:], start=True, stop=True), reads=[sqm, self.ones32], writes=[pd])
                        if h == 0:
                            P.op("vector", lambda e: e.tensor_copy(out=ssq.t[:, qs], in_=pd.t[:]), reads=[pd], appends=[ssq])
                        else:
                            P.op("vector", lambda e: e.tensor_tensor(out=ssq.t[:, qs], in0=ssq.t[:, qs], in1=pd.t[:], op=ALU.add), reads=[pd, ssq], appends=[ssq])
                    tails.append({2: t3r(0), 4: t3r(1), 6: t3r(2), 8: t3r(3), 10: t3, 16: t45})
                  do_qt(h, qt, qn, qp, kn, vm, kr)
            for tl in tails:
                for k_ in sorted(tl):
                    tl[k_]()
            while cast_state["pend"] is not None or cast_state["i"] < len(pieces):
                cast_step()
            P.op("scalar", lambda e: e.activation(out=ssq.t[:], in_=ssq.t[:], func=AF.Sqrt, bias=self.epsT.t[:], scale=1.0 / 1024.0),
                 reads=[ssq, self.epsT], writes=[ssq])
            P.op("vector", lambda e: e.reciprocal(out=ssq.t[:], in_=ssq.t[:]), reads=[ssq], writes=[ssq])
            self.dma("sync", self.rstdm_s.t, ssq.t[:], ssq, reads=[ssq], writes=[self.rstdm_s])

    def stageB_ret(self, top):
        P = self.P
        self.fence()
        with ExitStack() as st:
            T = lambda name, shape, dt, dma=False: self.tile(st, name, shape, dt, dma)
            V = lambda fn, reads, writes=(), appends=(): P.op("vector", fn, reads=reads, writes=writes, appends=appends)
            A = lambda fn, reads, writes=(), appends=(): P.op("scalar", fn, reads=reads, writes=writes, appends=appends)
            G = lambda fn, reads, writes=(), appends=(): P.op("gpsimd", fn, reads=reads, writes=writes, appends=appends)
            abc = T("abc", [128, 16], F32, dma=True)
            lg = T("lg", [128, 16], F32)
            gC = T("gC", [128, 16], F32)
            gnw = T("gnwS", [128, 8], F32, dma=True)
            gnb = T("gnbS", [128, 8], F32, dma=True)
            ii = T("iiR", [128, 128], I32)
            diff = T("diffR", [128, 128], F32)
            posp = T("pospR", [128, 128], F32)
            negp = T("negpR", [128, 128], F32)
            mF = T("mFR", [128, 128], F32)
            mB = T("mBR", [128, 128], F32)
            cp1 = T("cp1R", [128, 128], F32)
            r128 = T("r128R", [128, 128], F32)
            ci = T("ciR", [128, 2], I32)
            cf = T("cfR", [128, 2], F32)
            t1 = T("t1R", [128, 128], F32)
            t2 = T("t2R", [128, 128], F32)
            DT = T("DTR", [128, 8, 128], F32)
            xif = T("xifR", [128, 8, 128], F32)
            xib = T("xibR", [128, 8, 128], F32)
            zf = T("zfR", [128, 8], F32)
            zb = T("zbR", [128, 8], F32)
            self.dma("sync", abc.t[:], self.ret_decay[0].partition_broadcast(128), abc, writes=[abc])
            self.dma("sync", gnw.t[:], self.gnw, gnw, writes=[gnw])
            self.dma("sync", gnb.t[:], self.gnb, gnb, writes=[gnb])
            A(lambda e: e.activation(out=lg.t[:], in_=abc.t[:], func=AF.Exp), [abc], [lg])
            V(lambda e: e.tensor_scalar_mul(out=lg.t[:], in0=lg.t[:], scalar1=-1.0), [lg], [lg])
            A(lambda e: e.activation(out=gC.t[:], in_=lg.t[:], func=AF.Exp, scale=128.0), [lg], [gC])
            G(lambda e: e.iota(ii.t[:], pattern=[[1, 128]], base=0, channel_multiplier=-1), [], [ii])
            V(lambda e: e.tensor_copy(out=diff.t[:], in_=ii.t[:]), [ii], [diff])
            V(lambda e: e.tensor_scalar_max(out=posp.t[:], in0=diff.t[:], scalar1=0.0), [diff], [posp])
            V(lambda e: e.tensor_scalar(out=negp.t[:], in0=diff.t[:], scalar1=-1.0, scalar2=0.0, op0=ALU.mult, op1=ALU.max), [diff], [negp])
            V(lambda e: e.tensor_single_scalar(out=mF.t[:], in_=diff.t[:], scalar=0.0, op=ALU.is_ge), [diff], [mF])
            V(lambda e: e.tensor_single_scalar(out=mB.t[:], in_=diff.t[:], scalar=0.0, op=ALU.is_lt), [diff], [mB])
            G(lambda e: e.iota(ii.t[:], pattern=[[1, 128]], base=1, channel_multiplier=0), [], [ii])
            V(lambda e: e.tensor_copy(out=cp1.t[:], in_=ii.t[:]), [ii], [cp1])
            V(lambda e: e.tensor_scalar(out=r128.t[:], in0=cp1.t[:], scalar1=-1.0, scalar2=129.0, op0=ALU.mult, op1=ALU.add), [cp1], [r128])
            G(lambda e: e.iota(ci.t[:, 0:1], pattern=[[0, 1]], base=127, channel_multiplier=-1), [], [ci])
            G(lambda e: e.iota(ci.t[:, 1:2], pattern=[[0, 1]], base=0, channel_multiplier=1), [], [], [ci])
            V(lambda e: e.tensor_copy(out=cf.t[:], in_=ci.t[:]), [ci], [cf])
            for h in range(8):
                def tabs_h(h):
                    lf = lg.t[:, h:h + 1]
                    lb = lg.t[:, 8 + h:9 + h]
                    A(lambda e: e.activation(out=t1.t[:], in_=posp.t[:], func=AF.Exp, scale=lf), [posp, lg], [t1])
                    A(lambda e: e.activation(out=t2.t[:], in_=negp.t[:], func=AF.Exp, scale=lb), [negp, lg], [t2])
                    V(lambda e: e.tensor_tensor(out=t1.t[:], in0=t1.t[:], in1=mF.t[:], op=ALU.mult), [t1, mF], [t1])
                    V(lambda e: e.tensor_tensor(out=t2.t[:], in0=t2.t[:], in1=mB.t[:], op=ALU.mult), [t2, mB], [t2])
                    V(lambda e: e.tensor_tensor(out=DT.t[:, h, :], in0=t1.t[:], in1=t2.t[:], op=ALU.add), [t1, t2], [], [DT])
                    A(lambda e: e.activation(out=xif.t[:, h, :], in_=cp1.t[:], func=AF.Exp, scale=lf), [cp1, lg], [], [xif])
                    A(lambda e: e.activation(out=xib.t[:, h, :], in_=r128.t[:], func=AF.Exp, scale=lb), [r128, lg], [], [xib])
                    A(lambda e: e.activation(out=zf.t[:, h:h + 1], in_=cf.t[:, 0:1], func=AF.Exp, scale=lf), [cf, lg], [], [zf])
                    A(lambda e: e.activation(out=zb.t[:, h:h + 1], in_=cf.t[:, 1:2], func=AF.Exp, scale=lb), [cf, lg], [], [zb])
                tabs_h(h)
            qR = Ring([T("qR%d" % i, [128, S], BF16, dma=True) for i in range(2)])
            kR = Ring([T("kR%d" % i, [128, S], BF16, dma=True) for i in range(2)])
            sR = Ring([T("sR%d" % i, [128, S], BF16, dma=True) for i in range(2)])
            vR = Ring([T("vR%d" % i, [128, NT, 128], BF16, dma=True) for i in range(2)])
            Kzf = T("Kzf", [128, S], BF16)
            Kzb = T("Kzb", [128, S], BF16)
            Qxf = T("Qxf", [128, NT, 128], BF16)
            Qxb = T("Qxb", [128, NT, 128], BF16)
            SFb = T("SFb", [128, NT, 128], BF16)
            SBb = T("SBb", [128, NT, 128], BF16)
            S32 = [[T("S32_%d_%d" % (d_, i), [128, 128], F32) for i in range(2)] for d_ in range(2)]
            ATr = Ring([T("ATr%d" % i, [128, 4, 128], BF16) for i in range(2)])
            mhalf = T("mhalfR", [128, 512], F32)
            V(lambda e: e.memset(mhalf.t[:], -0.5), [], [mhalf])
            tmpR = Ring([[T("r6_%d_%d" % (k_, i), [128, 512], F32) for k_ in range(8)] for i in range(2)])
            obr = Ring([T("obR%d" % i, [128, 512], BF16, dma=True) for i in range(2)])
            banks = [self.ptile(st, "bkR%d" % i, [128, 512], F32) for i in range(8)]
            ptk = Ring(banks[0:2])
            pkv = Ring(banks[2:4])
            pscR = Ring(banks[4:6])
            pyR = Ring(banks[6:8])
            pstR = Ring([(banks[0], banks[1]), (banks[2], banks[3])])
            loaded = {}

            def load_head(h):
                if h >= 8 or h in loaded:
                    return
                q = qR.next(); k = kR.next(); sg = sR.next(); v = vR.next()
                self.dma("sync", q.t[:], self.qr_s[h].t, q, reads=[self.qr_s[h]], writes=[q])
                self.dma("sync", k.t[:], self.kr_s[h].t, k, reads=[self.kr_s[h]], writes=[k])
                self.dma("sync", sg.t[:], self.sg_s[h].t, sg, reads=[self.sg_s[h]], writes=[sg])
                self.dma("sync", v.t[:], self.vr_s.t[h], v, reads=[self.vr_s], writes=[v])
                loaded[h] = (q, k, sg, v)

            def do_head(h, q, k, sg, v):
                for g8 in range(4):
                    def r1(g8):
                        pt = ptk.next()

                        ptb = pt.t[:].bitcast(BF16)

                        def tr(e):
                            ins = None
                            for j in range(8):
                                n = g8 * 8 + j
                                ins = e.transpose(out=ptb[:, j * 128:(j + 1) * 128], in_=k.t[:, n * 128:(n + 1) * 128], identity=self.ident.t[:])
                            return ins
                        P.op("tensor", tr, reads=[k, self.ident], writes=[pt])
                        cs = slice(g8 * 1024, (g8 + 1) * 1024)
                        V(lambda e: e.tensor_scalar_mul(out=Kzf.t[:, cs], in0=ptb, scalar1=zf.t[:, h:h + 1]), [pt, zf], [Kzf] if g8 == 0 else [], [] if g8 == 0 else [Kzf])
                        A(lambda e: e.activation(out=Kzb.t[:, cs], in_=ptb, func=AF.Identity, scale=zb.t[:, h:h + 1]), [pt, zb, Kzf], [Kzb] if g8 == 0 else [], [] if g8 == 0 else [Kzb])
                    r1(g8)
                q3 = q.t[:].rearrange("p (n c) -> p n c", c=128)
                V(lambda e: e.tensor_tensor(out=Qxf.t[:], in0=q3, in1=xif.t[:, h:h + 1, :].to_broadcast([128, NT, 128]), op=ALU.mult), [q, xif], [Qxf])
                G(lambda e: e.tensor_tensor(out=Qxb.t[:], in0=q3, in1=xib.t[:, h:h + 1, :].to_broadcast([128, NT, 128]), op=ALU.mult), [q, xib], [Qxb])
                V(lambda e: e.memset(SFb.t[:, 0, :], 0.0), [], [SFb])
                V(lambda e: e.memset(SBb.t[:, NT - 1, :], 0.0), [], [SBb])
                V(lambda e: e.memset(S32[0][0].t[:], 0.0), [], [S32[0][0]])
                V(lambda e: e.memset(S32[1][0].t[:], 0.0), [], [S32[1][0]])
                cur = [0, 0]
                for gi in range(8):
                    def scan_g(gi):
                        pks = []
                        for direction in range(2):
                            g4 = gi if direction == 0 else 7 - gi
                            Kz = Kzf if direction == 0 else Kzb
                            pk = pkv.next()

                            def kvmm(e, g4=g4, pk=pk, Kz=Kz):
                                ins = None
                                for j in range(4):
                                    n = g4 * 4 + j
                                    ins = e.matmul(pk.t[:, j * 128:(j + 1) * 128], lhsT=Kz.t[:, n * 128:(n + 1) * 128], rhs=v.t[:, n, :], start=True, stop=True)
                                return ins
                            P.op("tensor", kvmm, reads=[Kz, v], writes=[pk])
                            pks.append((g4, pk))
                        for jj in range(4):
                            for direction in range(2):
                                g4, pk = pks[direction]
                                j = jj if direction == 0 else 3 - jj
                                n = g4 * 4 + j
                                nxt = n + 1 if direction == 0 else n - 1
                                if nxt < 0 or nxt >= NT:
                                    continue

                                def step(direction=direction, j=j, nxt=nxt, pk=pk):
                                    Sb = SFb if direction == 0 else SBb
                                    gcol = gC.t[:, h:h + 1] if direction == 0 else gC.t[:, 8 + h:9 + h]
                                    so = S32[direction][cur[direction]]; sn = S32[direction][1 - cur[direction]]
                                    V(lambda e: e.scalar_tensor_tensor(out=sn.t[:], in0=so.t[:], scalar=gcol, in1=pk.t[:, j * 128:(j + 1) * 128],
                                                                       op0=ALU.mult, op1=ALU.add), [so, gC, pk], [sn])
                                    G(lambda e: e.tensor_copy(out=Sb.t[:, nxt, :], in_=sn.t[:]), [sn], [], [Sb])
                                step()
                                cur[direction] = 1 - cur[direction]
                    scan_g(gi)
                ctx = {}

                def st1(g4):
                    c = {}
                    c["AT"] = ATr.next(); c["psc"] = pscR.next()
                    psc = c["psc"]; AT = c["AT"]

                    def scmm(e):
                        ins = None
                        for j in range(4):
                            n = g4 * 4 + j
                            ins = e.matmul(psc.t[:, j * 128:(j + 1) * 128], lhsT=k.t[:, n * 128:(n + 1) * 128], rhs=q.t[:, n * 128:(n + 1) * 128], start=True, stop=True)
                        return ins
                    P.op("tensor", scmm, reads=[k, q], writes=[psc])
                    V(lambda e: e.tensor_tensor(out=AT.t[:], in0=psc.t[:].rearrange("p (n c) -> p n c", c=128),
                                                in1=DT.t[:, h:h + 1, :].to_broadcast([128, 4, 128]), op=ALU.mult), [psc, DT], [AT])
                    ctx[g4] = c

                def st2(g4):
                    c = ctx[g4]
                    AT = c["AT"]
                    c["py"] = pyR.next(); c["tmp"] = tmpR.next()
                    py = c["py"]
                    y32, ysq = c["tmp"][0], c["tmp"][1]

                    def ymm(e):
                        ins = None
                        for j in range(4):
                            n = g4 * 4 + j
                            o = py.t[:, j * 128:(j + 1) * 128]
                            e.matmul(o, lhsT=v.t[:, n, :], rhs=AT.t[:, j, :], start=True, stop=False)
                            e.matmul(o, lhsT=SFb.t[:, n, :], rhs=Qxf.t[:, n, :], start=False, stop=False)
                            ins = e.matmul(o, lhsT=SBb.t[:, n, :], rhs=Qxb.t[:, n, :], start=False, stop=True)
                        return ins
                    P.op("tensor", ymm, reads=[v, AT, SFb, SBb, Qxf, Qxb], writes=[py])
                    A(lambda e: e.activation(out=y32.t[:], in_=py.t[:], func=AF.Copy), [py], [y32])
                    A(lambda e: e.activation(out=ysq.t[:], in_=py.t[:], func=AF.Square), [py], [ysq])

                def st3(g4):
                    c = ctx.pop(g4)
                    cs = slice(g4 * 512, (g4 + 1) * 512)
                    ob = obr.next(); pst = pstR.next()
                    y32, ysq, mn, msq, var, rsd, yc, yo = c["tmp"]
                    P.op("tensor", lambda e: e.matmul(pst[0].t[:], lhsT=self.ones32.t[:], rhs=y32.t[:], start=True, stop=True), reads=[y32, self.ones32], writes=[pst[0]])
                    P.op("tensor", lambda e: e.matmul(pst[1].t[:], lhsT=self.ones32.t[:], rhs=ysq.t[:], start=True, stop=True), reads=[ysq, self.ones32], writes=[pst[1]])
                    V(lambda e: e.tensor_scalar_mul(out=mn.t[:], in0=pst[0].t[:], scalar1=1.0 / 128.0), [pst[0]], [mn])
                    V(lambda e: e.tensor_tensor(out=msq.t[:], in0=mn.t[:], in1=mn.t[:], op=ALU.mult), [mn], [msq])
                    V(lambda e: e.scalar_tensor_tensor(out=var.t[:], in0=pst[1].t[:], scalar=1.0 / 128.0, in1=msq.t[:], op0=ALU.mult, op1=ALU.subtract), [pst[1], msq], [var])
                    A(lambda e: e.activation(out=var.t[:], in_=var.t[:], func=AF.Ln, bias=self.epsT.t[:], scale=1.0), [var, self.epsT], [var])
                    A(lambda e: e.activation(out=rsd.t[:], in_=var.t[:], func=AF.Exp, scale=-0.5), [var], [rsd])
                    G(lambda e: e.tensor_tensor(out=yc.t[:], in0=y32.t[:], in1=mn.t[:], op=ALU.subtract), [y32, mn], [yc])
                    V(lambda e: e.tensor_tensor(out=yc.t[:], in0=yc.t[:], in1=rsd.t[:], op=ALU.mult), [yc, rsd], [yc])
                    V(lambda e: e.tensor_scalar(out=yo.t[:], in0=yc.t[:], scalar1=gnw.t[:, h:h + 1], scalar2=gnb.t[:, h:h + 1], op0=ALU.mult, op1=ALU.add), [yc, gnw, gnb], [yo])
                    G(lambda e: e.tensor_tensor(out=ob.t[:], in0=yo.t[:], in1=sg.t[:, cs], op=ALU.mult), [yo, sg], [ob])
                    self.dma("sync", self.mix_s.t[h * 128:(h + 1) * 128, cs], ob.t[:], ob, reads=[ob], appends=[self.mix_s])

                for it in range(8 + 2):
                    if it < 8:
                        st1(it)
                    if 0 <= it - 1 < 8:
                        st2(it - 1)
                    if 0 <= it - 2 < 8:
                        st3(it - 2)

            load_head(0)
            for h in range(8):
                load_head(h + 1)
                do_head(h, *loaded[h])

    def stageC(self, top):
        P = self.P
        self.fence()
        with ExitStack() as st:
            T = lambda name, shape, dt, dma=False: self.tile(st, name, shape, dt, dma)
            V = lambda fn, reads, writes=(), appends=(): P.op("vector", fn, reads=reads, writes=writes, appends=appends)
            A = lambda fn, reads, writes=(), appends=(): P.op("scalar", fn, reads=reads, writes=writes, appends=appends)
            G = lambda fn, reads, writes=(), appends=(): P.op("gpsimd", fn, reads=reads, writes=writes, appends=appends)
            xt = T("xtC", [128, 4, D], F32, dma=True)
            xsub = [Buf(xt.t, xt.ds) for _ in range(4)]
            for b_ in xsub:
                b_.r = dict(self.barrier)
            osem = [T("osemC%d" % i, [128, 1], F32, dma=True) for i in range(4)]
            mix = T("mixC", [128, KC, 512], BF16, dma=True)
            rsm = T("rsmC", [128, 512], F32, dma=True)
            mow = T("mowC", [128, 8], F32, dma=True)
            h2T = T("h2T", [128, KC, 512], BF16)
            aT = T("aT", [128, FC, 512], BF16)
            cb = [T("cbC%d" % i, [128, D], F32, dma=True) for i in range(2)]
            gsl = Ring([T("gslC%d" % i, [128, 512], F32, dma=True) for i in range(2)])
            hbr = Ring([T("hbC%d" % i, [128, D], BF16) for i in range(2)])
            msr = Ring([T("msC%d" % i, [128, 1], F32) for i in range(2)])
            sdr = Ring([T("sdC%d" % i, [128, 1], F32) for i in range(2)])
            rsr = Ring([T("rsC%d" % i, [128, 1], F32) for i in range(2)])
            tmr = Ring([T("tmC%d" % i, [128, 512], F32) for i in range(2)])
            sgr = Ring([T("sgC%d" % i, [128, 512], F32) for i in range(2)])
            wr = Ring([T("wC%d" % i, [128, KC, 512], BF16, dma="sw") for i in range(2)])
            wdr = Ring([T("wdC%d" % i, [128, 11, 512], BF16, dma="sw") for i in range(2)])
            pgu = Ring([self.ptile(st, "pgu%d" % i, [128, 512], F32) for i in range(4)])
            pdn = [self.ptile(st, "pdn%d" % i, [128, 512], F32) for i in range(4)]
            self.dma("sync", mow.t[:], self.mow, mow, writes=[mow])
            self.dma("sync", cb[1].t[:], self.cb_s[2].t, cb[1], reads=[self.cb_s[2]], writes=[cb[1]])
            x_v = self.x.rearrange("(n p) d -> p n d", p=128)
            mix_v = self.mix_s.t.rearrange("(kc p) t -> p kc t", p=128)
            t32 = aT.t[:, 0:8, :].rearrange("p a b -> p (a b)").bitcast(F32)
            H = D // 2

            def rms(sub, junk_hb):
                ms = msr.next(); sd = sdr.next(); rs = rsr.next()
                A(lambda e: e.activation(out=junk_hb.t[:], in_=xt.t[:, sub, :], func=AF.Square, scale=1.0 / math.sqrt(D), accum_out=ms.t[:]),
                  [xsub[sub]], [junk_hb, ms])
                A(lambda e: e.activation(out=sd.t[:], in_=ms.t[:], func=AF.Sqrt, bias=self.epsT.t[:], scale=1.0), [ms, self.epsT], [sd])
                V(lambda e: e.reciprocal(out=rs.t[:], in_=sd.t[:]), [sd], [rs])
                return rs

            def load_mix(tt):
                ts = slice(tt * 512, (tt + 1) * 512)
                self.dma("sync", mix.t[:], mix_v[:, :, ts], mix, reads=[self.mix_s], writes=[mix])
                self.dma("sync", rsm.t[:], self.rstdm_s.t[:, ts], rsm, reads=[self.rstdm_s], writes=[rsm])

            def nrm_all():
                for j in range(8):
                    def nrm(j):
                        V(lambda e: e.scalar_tensor_tensor(out=mix.t[:, 8 + j, :], in0=mix.t[:, 8 + j, :], scalar=mow.t[:, j:j + 1], in1=rsm.t[:],
                                                           op0=ALU.mult, op1=ALU.mult), [mix, mow, rsm], [], [mix])
                    nrm(j)

            def do_tile(tt):
                if tt == 0:
                    load_mix(0)
                    nrm_all()
                for sub in range(4):
                    r0 = (tt * 4 + sub) * 128
                    self.dma("sync", xt.t[:, sub, :], self.x[r0:r0 + 128, :], osem[sub], writes=[xsub[sub]])
                self.dma("sync", cb[0].t[:], self.cb_s[1].t, cb[0], reads=[self.cb_s[1]], writes=[cb[0]])
                def load_ga(cg):
                    ga = gsl.next()
                    self.dma("sync", ga.t[:], self.cb_s[0].t[:, cg * 512:(cg + 1) * 512], ga, reads=[self.cb_s[0]], writes=[ga])
                    return ga
                ga_next = [load_ga(0)]
                for cg in range(4):
                    def wo_cg(cg):
                        wb = wr.next()
                        ga = ga_next[0]
                        if cg + 1 < 4:
                            ga_next[0] = load_ga(cg + 1)
                        cs = slice(cg * 512, (cg + 1) * 512)
                        self.dma("gpsimd", wb.t[:], self.wo_b.t[cg], wb, reads=[self.wo_b], writes=[wb])
                        for sub in range(4):
                            def wo_sub(sub):
                                pb = pdn[sub]

                                def mm(e):
                                    ins = None
                                    for kc in range(KC):
                                        ins = e.matmul(pb.t[:], lhsT=mix.t[:, kc, sub * 128:(sub + 1) * 128], rhs=wb.t[:, kc, :], start=(kc == 0), stop=(kc == KC - 1))
                                    return ins
                                P.op("tensor", mm, reads=[mix, wb], writes=[pb])
                                tm = tmr.next()
                                V(lambda e: e.tensor_tensor(out=tm.t[:], in0=pb.t[:], in1=ga.t[:], op=ALU.mult), [pb, ga], [tm])
                                V(lambda e: e.tensor_tensor(out=xt.t[:, sub, cs], in0=tm.t[:], in1=xt.t[:, sub, cs], op=ALU.add), [tm, xsub[sub]], [], [xsub[sub]])
                            wo_sub(sub)
                    wo_cg(cg)
                if tt + 1 < NTT:
                    load_mix(tt + 1)
                pending = []

                def flush():
                    for (eng, dst, src, pt, first) in pending:
                        if eng == "scalar":
                            A(lambda e, dst=dst, src=src: e.activation(out=dst, in_=src, func=AF.Copy), [pt], [h2T] if first else [], [] if first else [h2T])
                        else:
                            V(lambda e, dst=dst, src=src: e.tensor_copy(out=dst, in_=src), [pt], [], [h2T])
                    del pending[:]

                for sub in range(4):
                    def n2(sub):
                        hb = hbr.next()
                        rs = rms(sub, hb)
                        V(lambda e: e.scalar_tensor_tensor(out=t32, in0=xt.t[:, sub, :], scalar=rs.t[:, 0:1], in1=cb[0].t[:], op0=ALU.mult, op1=ALU.mult),
                          [xsub[sub], rs, cb[0]], [aT])
                        V(lambda e: e.tensor_tensor(out=hb.t[:], in0=t32, in1=cb[1].t[:], op=ALU.add), [aT, cb[1]], [hb])
                        flush()
                        for half in range(2):
                            def trh(half):
                                pt = pgu.next()
                                ptb = pt.t[:].bitcast(BF16)

                                def tr(e):
                                    ins = None
                                    for j in range(8):
                                        kc = half * 8 + j
                                        ins = e.transpose(out=ptb[:, j * 128:(j + 1) * 128], in_=hb.t[:, kc * 128:(kc + 1) * 128], identity=self.ident.t[:])
                                    return ins
                                P.op("tensor", tr, reads=[hb, self.ident], writes=[pt])
                                dst = h2T.t[:, half * 8:(half + 1) * 8, sub * 128:(sub + 1) * 128]
                                src = ptb.rearrange("p (k t) -> p k t", k=8)
                                pending.append(("scalar" if half == 0 else "vector", dst, src, pt, (sub == 0 and half == 0)))
                            trh(half)
                    n2(sub)
                flush()
                self.dma("sync", cb[0].t[:], self.cb_s[4].t, cb[0], reads=[self.cb_s[4]], writes=[cb[0]])
                for fg in range(FC // 2):
                    def gu(fg):
                        wb = wr.next()
                        self.dma("gpsimd", wb.t[:], self.wg_b.t[fg], wb, reads=[self.wg_b], writes=[wb])
                        if fg == 6 and tt + 1 < NTT:
                            nrm_all()
                        for j in range(2):
                            def gu_j(j):
                                fc = fg * 2 + j
                                pg = pgu.next(); pu = pgu.next()

                                def mm(e):
                                    ins = None
                                    for kc in range(KC):
                                        ins = e.matmul(pg.t[:], lhsT=wb.t[:, kc, j * 128:(j + 1) * 128], rhs=h2T.t[:, kc, :], start=(kc == 0), stop=(kc == KC - 1))
                                    return ins

                                def mm2(e):
                                    ins = None
                                    for kc in range(KC):
                                        ins = e.matmul(pu.t[:], lhsT=wb.t[:, kc, 256 + j * 128: 256 + (j + 1) * 128], rhs=h2T.t[:, kc, :], start=(kc == 0), stop=(kc == KC - 1))
                                    return ins
                                P.op("tensor", mm, reads=[wb, h2T], writes=[pg])
                                P.op("tensor", mm2, reads=[wb, h2T], writes=[pu])
                                sgt = sgr.next()
                                A(lambda e: e.activation(out=sgt.t[:], in_=pg.t[:], func=AF.Silu), [pg], [sgt])
                                first = (fc == 0)
                                V(lambda e: e.tensor_tensor(out=aT.t[:, fc, :], in0=sgt.t[:], in1=pu.t[:], op=ALU.mult), [sgt, pu], [aT] if first else [], [] if first else [aT])
                            gu_j(j)
                    gu(fg)
                for cg in range(4):
                    def dn_cg(cg):
                        cs = slice(cg * 512, (cg + 1) * 512)
                        gf = gsl.next()
                        self.dma("sync", gf.t[:], self.cb_s[3].t[:, cs], gf, reads=[self.cb_s[3]], writes=[gf])
                        for fq in range(4):
                            def dn_fq(fq):
                                wd = wdr.next()
                                self.dma("gpsimd", wd.t[:], self.wd_b.t[cg, fq], wd, reads=[self.wd_b], writes=[wd])
                                for sub in range(4):
                                    def dn_sub(sub):
                                        pb = pdn[sub]

                                        def mm(e):
                                            ins = None
                                            for j in range(11):
                                                fc = fq * 11 + j
                                                ins = e.matmul(pb.t[:], lhsT=aT.t[:, fc, sub * 128:(sub + 1) * 128], rhs=wd.t[:, j, :],
                                                               start=(fc == 0), stop=(fc == FC - 1))
                                            return ins
                                        if fq == 0:
                                            P.op("tensor", mm, reads=[aT, wd], writes=[pb])
                                        else:
                                            P.op("tensor", mm, reads=[aT, wd], appends=[pb])
                                    dn_sub(sub)
                            dn_fq(fq)
                        for sub in range(4):
                            def dn_ev(sub):
                                pb = pdn[sub]
                                tm = tmr.next()
                                V(lambda e: e.tensor_tensor(out=tm.t[:], in0=pb.t[:], in1=gf.t[:], op=ALU.mult), [pb, gf], [tm])
                                V(lambda e: e.tensor_tensor(out=xt.t[:, sub, cs], in0=tm.t[:], in1=xt.t[:, sub, cs], op=ALU.add), [tm, xsub[sub]], [], [xsub[sub]])
                            dn_ev(sub)
                    dn_cg(cg)
                for sub in range(4):
                    def fin(sub):
                        hb = hbr.next()
                        rs = rms(sub, hb)
                        V(lambda e: e.scalar_tensor_tensor(out=xt.t[:, sub, :], in0=xt.t[:, sub, :], scalar=rs.t[:, 0:1], in1=cb[0].t[:], op0=ALU.mult, op1=ALU.mult),
                          [xsub[sub], rs, cb[0]], [xsub[sub]])
                        r0 = (tt * 4 + sub) * 128
                        self.dma("sync", self.out[r0:r0 + 128, :], xt.t[:, sub, :], osem[sub], reads=[xsub[sub]], appends=[self.out_buf])
                    fin(sub)

            for tt in range(NTT):
                do_tile(tt)


def host_inputs(inp, b):
    f = lambda a: np.ascontiguousarray(a, dtype=np.float32)
    m = {}
    m["x"] = f(inp["x"][b])
    m["cT"] = f(inp["c"][b].reshape(KC, 128).T)
    m["pos"] = np.ascontiguousarray(inp["positions"][b].reshape(1, S).astype(np.int32))
    m["ada_w"] = f(inp["ada_w"][0])
    m["ada_b"] = f(inp["ada_b"][0].reshape(1, -1))
    m["norm1_w"] = f(inp["norm1_w"][0].reshape(1, -1))
    w_in = inp["w_in"][0]
    m["w_in"] = f(w_in[:, :5120])
    kr = w_in[:, 5120:5184]
    z = np.zeros((D, 32), np.float32)
    m["w_kr"] = f(np.concatenate([kr[:, 0:32], z, kr[:, 32:64], z, z, kr[:, 0:32], z, kr[:, 32:64]], axis=1))
    m["ret_decay"] = f(inp["ret_decay"][0].reshape(1, 16))
    m["gnw"] = f(inp["ret_gn_w"][0].reshape(8, 128).T)
    m["gnb"] = f(inp["ret_gn_b"][0].reshape(8, 128).T)
    m["qnw"] = f(inp["mla_q_norm_w"][0].reshape(4, 128).T)
    m["kvnw"] = f(inp["mla_kv_norm_w"][0].reshape(4, 128).T)
    w_uq = inp["w_uq"][0].reshape(512, 8, 192)
    m["w_uqn"] = f(w_uq[:, :, :128].reshape(512, 1024))
    rp = w_uq[:, :, 128:].reshape(512, 4, 2, 2, 32)
    m["w_uqr"] = f(rp.transpose(0, 1, 3, 2, 4).reshape(512, 512))
    w_ukv = inp["w_ukv"][0].reshape(512, 8, 256)
    m["w_uk"] = f(w_ukv[:, :, :128].reshape(512, 1024))
    m["w_uv"] = f(w_ukv[:, :, 128:].reshape(512, 1024))
    m["mow"] = f(inp["mla_out_w"][0].reshape(8, 128).T)
    m["w_o"] = f(inp["w_o"][0])
    m["norm2_w"] = f(inp["norm2_w"][0].reshape(1, -1))
    m["w_gate"] = f(inp["w_gate"][0])
    m["w_up"] = f(inp["w_up"][0])
    m["w_down"] = f(inp["w_down"][0])
    m["fnw"] = f(inp["final_norm_w"].reshape(1, -1))
    fr = (np.float32(10000.0) ** (-np.arange(0, 128, 2, dtype=np.float32) / np.float32(128))).astype(np.float32)
    fm = (np.float32(10000.0) ** (-np.arange(0, 64, 2, dtype=np.float32) / np.float32(64))).astype(np.float32)
    invf = np.zeros((128, 2), np.float32)
    invf[:, 0] = fr[np.arange(128) % 64]
    invf[:, 1] = fm[np.arange(128) % 32]
    m["invf"] = invf
    return m


_CACHE = {}


def kernel(**inputs):
    inp = {k: np.asarray(v) for k, v in inputs.items()}
    if "nc" not in _CACHE:
        _CACHE["nc"] = Builder(debug=False, upto="C").build()
    nc = _CACHE["nc"]
    shared = None
    in_maps = []
    for b in range(8):
        m = host_inputs(inp, b)
        if shared is None:
            shared = m
        else:
            for k in m:
                if k not in ("x", "cT", "pos"):
                    m[k] = shared[k]
        in_maps.append(m)
    res = run_bass_kernel_spmd(nc, in_maps, core_ids=list(range(8)))
    out = np.stack([np.asarray(res.results[b]["out"], dtype=np.float32) for b in range(8)], axis=0)
    return out
```

```python
import math
from contextlib import ExitStack

import numpy as np
import ml_dtypes

import concourse.bass as bass
import concourse.mybir as mybir
from concourse.bass_utils import run_bass_kernel_spmd

F32 = mybir.dt.float32
BF16 = mybir.dt.bfloat16
I32 = mybir.dt.int32
ALU = mybir.AluOpType
AF = mybir.ActivationFunctionType

S = 4096
D = 2048
NT = S // 128
NTT = S // 512
KC = D // 128
DFF = 5632
FC = DFF // 128
EPS = 1e-6
ENGS = ["sync", "gpsimd", "scalar", "vector", "tensor"]

TWO_PI = 2.0 * math.pi
CW1 = 6.28125
CW2 = TWO_PI - CW1
PI_LO = 3.14159


class DSem:
    __slots__ = ("h", "n")

    def __init__(self, h):
        self.h = h
        self.n = 0


class Buf:
    __slots__ = ("t", "w", "r", "pr", "ds")

    def __init__(self, t, ds=None):
        self.t = t
        self.w = {}
        self.r = {}
        self.pr = {}
        self.ds = ds


def _merge(dst, tok):
    s, v = tok
    k = id(s)
    if k not in dst or dst[k][1] < v:
        dst[k] = (s, v)


class Prog:
    def __init__(self, nc, stack):
        self.nc = nc
        self.stack = stack
        self.ops = {e: [] for e in ENGS}
        self.cnt = {e: 0 for e in ENGS}
        self.esem = {}
        for e in ["scalar", "vector", "tensor", "gpsimd"]:
            self.esem[e] = stack.enter_context(nc.semaphore("es_" + e))
        self.waited = {}
        self.nsem = 0
        self.nops = 0
        self.dsems = []

    def barrier_tokens(self):
        d = {}
        for e, s in self.esem.items():
            if self.cnt[e] > 0:
                d[id(s)] = (s, self.cnt[e])
        for ds in self.dsems:
            if ds.n > 0:
                d[id(ds.h)] = (ds.h, ds.n)
        return d

    def dsem(self):
        self.nsem += 1
        ds = DSem(self.stack.enter_context(self.nc.semaphore("ds%d" % self.nsem)))
        self.dsems.append(ds)
        return ds

    def op(self, eng, fn, reads=(), writes=(), appends=(), dsem=None):
        deps = {}
        for b in reads:
            for tok in b.w.values():
                _merge(deps, tok)
        for b in writes:
            if b.r:
                for tok in b.r.values():
                    _merge(deps, tok)
            else:
                for tok in b.w.values():
                    _merge(deps, tok)
        for b in appends:
            for tok in b.r.values():
                _merge(deps, tok)
            for tok in b.pr.values():
                _merge(deps, tok)
        waits = []
        for (s, v) in deps.values():
            key = (eng, id(s))
            if self.waited.get(key, 0) < v:
                self.waited[key] = v
                waits.append((s, v))
        if dsem is not None:
            dsem.n += 16
            tok = (dsem.h, dsem.n)
            inc = (dsem.h, 16)
        else:
            s = self.esem[eng]
            self.cnt[eng] += 1
            tok = (s, self.cnt[eng])
            inc = (s, 1)
        self.ops[eng].append((waits, fn, inc))
        self.nops += 1
        for b in reads:
            _merge(b.r, tok)
        for b in writes:
            b.w = {id(tok[0]): tok}
            b.pr = b.r
            b.r = {}
        for b in appends:
            _merge(b.w, tok)
        return tok

    def emit(self, final_bufs):
        nc = self.nc
        fin = {}
        for b in final_bufs:
            for tok in b.w.values():
                _merge(fin, tok)
            for tok in b.r.values():
                _merge(fin, tok)

        with nc.Block() as block:
            def mk(ename):
                def body(e):
                    for (waits, fn, inc) in self.ops[ename]:
                        for (s, v) in waits:
                            e.wait_ge(s, v)
                        ins = fn(e)
                        ins.then_inc(inc[0], inc[1])
                    if ename == "sync":
                        for (s, v) in fin.values():
                            e.wait_ge(s, v)
                return body
            block.sync(mk("sync"))
            block.gpsimd(mk("gpsimd"))
            block.scalar(mk("scalar"))
            block.vector(mk("vector"))
            block.tensor(mk("tensor"))


class Ring:
    def __init__(self, bufs):
        self.bufs = bufs
        self.i = 0

    def next(self):
        b = self.bufs[self.i % len(self.bufs)]
        self.i += 1
        return b


class Builder:
    def __init__(self, debug=False, upto="C"):
        self.debug = debug
        self.upto = upto
        self.nc = bass.Bass("TRN2", target_bir_lowering=False)
        self.inputs = {}
        self.outs = {}

    def din(self, name, shape, dt=F32):
        t = self.nc.dram_tensor(name, list(shape), dt, kind="ExternalInput")
        self.inputs[name] = t
        return t.ap()

    def dscr(self, name, shape, dt):
        kind = "ExternalOutput" if self.debug else "Internal"
        t = self.nc.dram_tensor(name, list(shape), dt, kind=kind)
        if self.debug:
            self.outs[name] = t
        return Buf(t.ap())

    def tile(self, st, name, shape, dt, dma=False):
        t = st.enter_context(self.nc.sbuf_tensor(name, list(shape), dt))
        ds = None
        if dma:
            pool = self.free_ds_sw if dma == "sw" else self.free_ds
            ds = pool.pop() if pool else self.P.dsem()
            st.callback(pool.append, ds)
        b = Buf(t, ds)
        b.r = dict(self.barrier)
        return b

    def ptile(self, st, name, shape, dt):
        t = st.enter_context(self.nc.psum_tensor(name, list(shape), dt))
        b = Buf(t)
        b.r = dict(self.barrier)
        return b

    def fence(self):
        self.barrier = self.P.barrier_tokens()

    def dma(self, eng, out_ap, in_ap, sem_buf, reads=(), writes=(), appends=()):
        return self.P.op(eng, lambda e: e.dma_start(out=out_ap, in_=in_ap), reads=reads, writes=writes,
                         appends=appends, dsem=sem_buf.ds)

    def build(self):
        nc = self.nc
        with ExitStack() as top:
            self.P = Prog(nc, top)
            self.barrier = {}
            self.free_ds = []
            self.free_ds_sw = []
            self.declare()
            self.consts(top)
            self.stage0(top)
            if self.upto >= "A":
                self.stageA(top)
            if self.upto >= "B":
                self.stageB_mla(top)
            if self.upto >= "B2":
                self.stageB_ret(top)
            if self.upto >= "C":
                self.stageC(top)
            fin = [self.out_buf, self.wo_b, self.wg_b, self.wd_b] + ([] if not self.debug else list(self.dbg_bufs))
            self.P.emit(fin)
        return nc

    def declare(self):
        d = self.din
        self.x = d("x", [S, D])
        self.cT = d("cT", [128, KC])
        self.pos = d("pos", [1, S], I32)
        self.ada_w = d("ada_w", [D, 6 * D])
        self.ada_b = d("ada_b", [1, 6 * D])
        self.norm1_w = d("norm1_w", [1, D])
        self.w_in = d("w_in", [D, 5120])
        self.w_kr = d("w_kr", [D, 256])
        self.ret_decay = d("ret_decay", [1, 16])
        self.gnw = d("gnw", [128, 8])
        self.gnb = d("gnb", [128, 8])
        self.qnw = d("qnw", [128, 4])
        self.kvnw = d("kvnw", [128, 4])
        self.w_uqn = d("w_uqn", [512, 1024])
        self.w_uqr = d("w_uqr", [512, 512])
        self.w_uk = d("w_uk", [512, 1024])
        self.w_uv = d("w_uv", [512, 1024])
        self.mow = d("mow", [128, 8])
        self.w_o = d("w_o", [D, D])
        self.norm2_w = d("norm2_w", [1, D])
        self.w_gate = d("w_gate", [D, DFF])
        self.w_up = d("w_up", [D, DFF])
        self.w_down = d("w_down", [DFF, D])
        self.fnw = d("fnw", [1, D])
        self.invf = d("invf", [128, 2])
        out_t = nc_out = self.nc.dram_tensor("out", [S, D], F32, kind="ExternalOutput")
        self.outs["out"] = out_t
        self.out = out_t.ap()
        self.out_buf = Buf(self.out)
        s = self.dscr
        self.tabs = [s("tab%d" % i, [128, S], F32) for i in range(4)]
        self.qr_s = [s("qr_s%d" % h, [128, S], BF16) for h in range(8)]
        self.kr_s = [s("kr_s%d" % h, [128, S], BF16) for h in range(8)]
        self.vr_s = s("vr_s", [8, 128, NT, 128], BF16)
        self.sg_s = [s("sg_s%d" % h, [128, S], BF16) for h in range(8)]
        self.qn_s = [s("qn_s%d" % h, [128, S], BF16) for h in range(8)]
        self.qp_s = [s("qp_s%d" % h, [128, S], BF16) for h in range(4)]
        self.kn_s = [s("kn_s%d" % h, [128, S], BF16) for h in range(8)]
        self.vm_s = s("vm_s", [8, 128, NT, 128], BF16)
        self.kro_s = [s("kro_s%d" % h, [128, S], BF16) for h in range(2)]
        self.mix_s = s("mix_s", [D, S], BF16)
        self.rstdm_s = s("rstdm_s", [128, S], F32)
        self.cb_s = [s("cb_s%d" % i, [128, D], F32) for i in range(7)]
        self.wo_b = Buf(self.nc.dram_tensor("wo_t", [4, 128, KC, 512], BF16, kind="Internal").ap())
        self.wg_b = Buf(self.nc.dram_tensor("wgu_t", [FC // 2, 128, KC, 512], BF16, kind="Internal").ap())
        self.wd_b = Buf(self.nc.dram_tensor("wd_t", [4, 4, 128, 11, 512], BF16, kind="Internal").ap())
        self.dbg_bufs = list(self.tabs) + self.qr_s + self.kr_s + [self.vr_s] + self.sg_s + self.qn_s + \
            self.qp_s + self.kn_s + [self.vm_s] + self.kro_s + [self.mix_s, self.rstdm_s] + self.cb_s

    def consts(self, st):
        P = self.P
        T = lambda name, shape, dt, dma=False: self.tile(st, name, shape, dt, dma)
        self.ident = T("ident", [128, 128], BF16)
        self.ones32 = T("ones32", [128, 128], F32)
        self.onesb = T("onesb", [128, 128], BF16)
        self.epsT = T("epsT", [128, 1], F32)
        self.halfpi = T("halfpi", [128, 1], F32)
        with ExitStack() as tmp:
            ioi = self.tile(tmp, "ioi", [128, 128], I32)
            iof = self.tile(tmp, "iof", [128, 128], F32)
            P.op("gpsimd", lambda e: e.iota(ioi.t[:], pattern=[[1, 128]], base=0, channel_multiplier=-1), writes=[ioi])
            P.op("vector", lambda e: e.tensor_copy(out=iof.t[:], in_=ioi.t[:]), reads=[ioi], writes=[iof])
            P.op("vector", lambda e: e.tensor_single_scalar(out=self.ident.t[:], in_=iof.t[:], scalar=0.0, op=ALU.is_equal),
                 reads=[iof], writes=[self.ident])
            P.op("vector", lambda e: e.memset(self.ones32.t[:], 1.0), writes=[self.ones32])
            P.op("vector", lambda e: e.memset(self.onesb.t[:], 1.0), writes=[self.onesb])
            P.op("vector", lambda e: e.memset(self.epsT.t[:], EPS), writes=[self.epsT])
            P.op("vector", lambda e: e.memset(self.halfpi.t[:], math.pi / 2), writes=[self.halfpi])
        self.alias_guard = [self.ident]

    def stage0(self, top):
        P = self.P
        nc = self.nc
        self.fence()
        with ExitStack() as st:
            T = lambda name, shape, dt, dma=False: self.tile(st, name, shape, dt, dma)
            cT = T("cTs", [128, KC], F32, dma=True)
            cact = T("cact", [128, KC], F32)
            crep = T("crep", [128, KC, 128], BF16)
            adab = T("adab", [128, 6 * D], F32, dma=True)
            n1w = T("n1w", [128, D], F32, dma=True)
            n2w = T("n2w", [128, D], F32, dma=True)
            fnw = T("fnwb", [128, D], F32, dma=True)
            mod = [T("mod%d" % i, [128, D], F32, dma=True) for i in range(6)]
            wr = Ring([T("adw%d" % i, [128, KC, 512], BF16, dma="sw") for i in range(2)])
            pm = Ring([self.ptile(st, "pm%d" % i, [128, 512], F32) for i in range(2)])
            self.dma("sync", cT.t[:], self.cT, cT, writes=[cT])
            self.dma("sync", adab.t[:], self.ada_b[0].partition_broadcast(128), adab, writes=[adab])
            self.dma("sync", n1w.t[:], self.norm1_w[0].partition_broadcast(128), n1w, writes=[n1w])
            self.dma("sync", n2w.t[:], self.norm2_w[0].partition_broadcast(128), n2w, writes=[n2w])
            self.dma("sync", fnw.t[:], self.fnw[0].partition_broadcast(128), fnw, writes=[fnw])
            P.op("scalar", lambda e: e.activation(out=cact.t[:], in_=cT.t[:], func=AF.Silu), reads=[cT], writes=[cact])
            for kc in range(KC):
                P.op("vector", lambda e, kc=kc: e.tensor_copy(out=crep.t[:, kc, :], in_=cact.t[:, kc:kc + 1].to_broadcast([128, 128])),
                     reads=[cact], appends=[crep])
            adw_v = self.ada_w.rearrange("(kc p) n -> p kc n", p=128)
            QW = S // 4
            posi = T("posi", [128, QW], I32, dma=True)
            posf = T("posf", [128, QW], F32)
            invf = T("invf_sb", [128, 2], F32, dma=True)
            ang = T("ang", [128, QW], F32)
            kf = T("kf", [128, QW], F32)
            ki = T("ki", [128, QW], I32)
            r = T("r", [128, QW], F32)
            msk = T("msk", [128, QW], F32)
            res = [T("res%d" % i, [128, QW], F32, dma=True) for i in range(2)]
            self.dma("sync", invf.t[:], self.invf, invf, writes=[invf])
            V = lambda fn, reads, writes: P.op("vector", fn, reads=reads, writes=writes)

            def tab_piece(j, qi):
                cs = slice(qi * QW, (qi + 1) * QW)
                self.dma("sync", posi.t[:], self.pos[0, cs].partition_broadcast(128), posi, writes=[posi])
                V(lambda e: e.tensor_copy(out=posf.t[:], in_=posi.t[:]), [posi], [posf])
                V(lambda e: e.tensor_scalar_mul(out=ang.t[:], in0=posf.t[:], scalar1=invf.t[:, j:j + 1]), [posf, invf], [ang])
                V(lambda e: e.tensor_scalar(out=kf.t[:], in0=ang.t[:], scalar1=1.0 / TWO_PI, scalar2=0.5, op0=ALU.mult, op1=ALU.add), [ang], [kf])
                V(lambda e: e.tensor_copy(out=ki.t[:], in_=kf.t[:]), [kf], [ki])
                V(lambda e: e.tensor_copy(out=kf.t[:], in_=ki.t[:]), [ki], [kf])
                V(lambda e: e.scalar_tensor_tensor(out=r.t[:], in0=kf.t[:], scalar=-CW1, in1=ang.t[:], op0=ALU.mult, op1=ALU.add), [kf, ang], [r])
                V(lambda e: e.scalar_tensor_tensor(out=r.t[:], in0=kf.t[:], scalar=-CW2, in1=r.t[:], op0=ALU.mult, op1=ALU.add), [kf, r], [r])
                V(lambda e: e.tensor_single_scalar(out=msk.t[:], in_=r.t[:], scalar=-math.pi, op=ALU.is_lt), [r], [msk])
                V(lambda e: e.scalar_tensor_tensor(out=r.t[:], in0=msk.t[:], scalar=TWO_PI, in1=r.t[:], op0=ALU.mult, op1=ALU.add), [msk, r], [r])
                V(lambda e: e.tensor_single_scalar(out=msk.t[:], in_=r.t[:], scalar=math.pi, op=ALU.is_gt), [r], [msk])
                V(lambda e: e.scalar_tensor_tensor(out=r.t[:], in0=msk.t[:], scalar=-TWO_PI, in1=r.t[:], op0=ALU.mult, op1=ALU.add), [msk, r], [r])
                V(lambda e: e.tensor_scalar(out=r.t[:], in0=r.t[:], scalar1=-PI_LO, scalar2=PI_LO, op0=ALU.max, op1=ALU.min), [r], [r])
                P.op("scalar", lambda e: e.activation(out=res[1].t[:], in_=r.t[:], func=AF.Sin), reads=[r], writes=[res[1]])
                self.dma("sync", self.tabs[2 * j + 1].t[:, cs], res[1].t[:], res[1], reads=[res[1]], appends=[self.tabs[2 * j + 1]])
                V(lambda e: e.scalar_tensor_tensor(out=msk.t[:], in0=r.t[:], scalar=-1.0, in1=r.t[:], op0=ALU.mult, op1=ALU.max), [r], [msk])
                P.op("scalar", lambda e: e.activation(out=res[0].t[:], in_=msk.t[:], func=AF.Sin, scale=-1.0, bias=self.halfpi.t[:]),
                     reads=[msk, self.halfpi], writes=[res[0]])
                self.dma("sync", self.tabs[2 * j].t[:, cs], res[0].t[:], res[0], reads=[res[0]], appends=[self.tabs[2 * j]])

            tab_jobs = [(j, qi) for j in range(2) for qi in range(4)]
            for g in range(24):
                def ada_g(g):
                    wb = wr.next()
                    self.dma("gpsimd", wb.t[:], adw_v[:, :, g * 512:(g + 1) * 512], wb, writes=[wb])
                    pb = pm.next()

                    def mm(e):
                        ins = None
                        for kc in range(KC):
                            ins = e.matmul(pb.t[:], lhsT=crep.t[:, kc, :], rhs=wb.t[:, kc, :], start=(kc == 0), stop=(kc == KC - 1))
                        return ins
                    P.op("tensor", mm, reads=[crep, wb], writes=[pb])
                    m = mod[g // 4]
                    c0 = (g % 4) * 512
                    P.op("vector", lambda e: e.tensor_tensor(out=m.t[:, c0:c0 + 512], in0=pb.t[:], in1=adab.t[:, g * 512:(g + 1) * 512], op=ALU.add),
                         reads=[pb, adab], appends=[m])
                ada_g(g)
                if g % 3 == 1 and tab_jobs:
                    tab_piece(*tab_jobs.pop(0))
            while tab_jobs:
                tab_piece(*tab_jobs.pop(0))
            shift_a, scale_a, gate_a, shift_f, scale_f, gate_f = mod
            P.op("vector", lambda e: e.scalar_tensor_tensor(out=scale_a.t[:], in0=scale_a.t[:], scalar=1.0, in1=n1w.t[:],
                                                            op0=ALU.add, op1=ALU.mult), reads=[scale_a, n1w], writes=[scale_a])
            P.op("vector", lambda e: e.scalar_tensor_tensor(out=scale_f.t[:], in0=scale_f.t[:], scalar=1.0, in1=n2w.t[:],
                                                            op0=ALU.add, op1=ALU.mult), reads=[scale_f, n2w], writes=[scale_f])
            for i, src in enumerate([gate_a, scale_f, shift_f, gate_f, fnw, scale_a, shift_a]):
                self.dma("sync", self.cb_s[i].t, src.t[:], src, reads=[src], writes=[self.cb_s[i]])

    def rope_fm(self, pb, cos_t, sin_t, scale, tmpA, tmpB, ob):
        P = self.P
        P.op("vector", lambda e: e.scalar_tensor_tensor(out=tmpA.t[:], in0=pb.t[:], scalar=scale, in1=cos_t.t[:], op0=ALU.mult, op1=ALU.mult),
             reads=[pb, cos_t], writes=[tmpA])
        P.op("vector", lambda e: e.scalar_tensor_tensor(out=tmpB.t[0:64, :], in0=pb.t[64:128, :], scalar=scale, in1=sin_t.t[64:128, :],
                                                        op0=ALU.mult, op1=ALU.mult), reads=[pb, sin_t], writes=[tmpB])
        P.op("vector", lambda e: e.scalar_tensor_tensor(out=tmpB.t[64:128, :], in0=pb.t[0:64, :], scalar=scale, in1=sin_t.t[0:64, :],
                                                        op0=ALU.mult, op1=ALU.mult), reads=[pb, sin_t], appends=[tmpB])
        P.op("gpsimd", lambda e: e.tensor_tensor(out=ob.t[0:64, :], in0=tmpA.t[0:64, :], in1=tmpB.t[0:64, :], op=ALU.subtract),
             reads=[tmpA, tmpB], writes=[ob])
        P.op("gpsimd", lambda e: e.tensor_tensor(out=ob.t[64:128, :], in0=tmpA.t[64:128, :], in1=tmpB.t[64:128, :], op=ALU.add),
             reads=[tmpA, tmpB], appends=[ob])

    def stageA(self, top):
        P = self.P
        self.fence()
        with ExitStack() as st:
            T = lambda name, shape, dt, dma=False: self.tile(st, name, shape, dt, dma)
            hT = st.enter_context(self.nc.sbuf_tensor("hT", [128, KC, S], BF16))
            hTb = [Buf(hT) for _ in range(NTT)]
            for b_ in hTb:
                b_.r = dict(self.barrier)
            with ExitStack() as s1:
                T1 = lambda name, shape, dt, dma=False: self.tile(s1, name, shape, dt, dma)
                self.Aabc = T1("Aabc", [128, D], F32, dma=True)
                self.Babc = T1("Babc", [128, D], F32, dma=True)
                self.dma("sync", self.Aabc.t[:], self.cb_s[5].t, self.Aabc, reads=[self.cb_s[5]], writes=[self.Aabc])
                self.dma("sync", self.Babc.t[:], self.cb_s[6].t, self.Babc, reads=[self.cb_s[6]], writes=[self.Babc])
                xr = Ring([T1("xt%d" % i, [128, D], F32, dma=True) for i in range(3)])
                junk = T1("junk", [128, D], BF16)
                msr = Ring([T1("ms%d" % i, [128, 1], F32) for i in range(3)])
                sdr = Ring([T1("sd%d" % i, [128, 1], F32) for i in range(3)])
                rsr = Ring([T1("rs%d" % i, [128, 1], F32) for i in range(3)])
                hbr = Ring([T1("hb%d" % i, [128, D], BF16) for i in range(2)])
                ptr = Ring([self.ptile(s1, "ptA%d" % i, [128, 1024], BF16) for i in range(4)])
                pending = []

                def flush():
                    for (eng, dst, src, pt, tb) in pending:
                        if eng == "scalar":
                            P.op(eng, lambda e, dst=dst, src=src: e.activation(out=dst, in_=src, func=AF.Copy), reads=[pt], appends=[tb])
                        else:
                            P.op(eng, lambda e, dst=dst, src=src: e.tensor_copy(out=dst, in_=src), reads=[pt], appends=[tb])
                    del pending[:]

                for i in range(NT):
                    def a1(i):
                        xt = xr.next(); ms = msr.next(); sd = sdr.next(); rs = rsr.next(); hb = hbr.next()
                        self.dma("sync", xt.t[:], self.x[i * 128:(i + 1) * 128, :], xt, writes=[xt])
                        P.op("scalar", lambda e: e.activation(out=junk.t[:], in_=xt.t[:], func=AF.Square, scale=1.0 / math.sqrt(D),
                                                              accum_out=ms.t[:]), reads=[xt], writes=[ms])
                        P.op("scalar", lambda e: e.activation(out=sd.t[:], in_=ms.t[:], func=AF.Sqrt, bias=self.epsT.t[:], scale=1.0),
                             reads=[ms, self.epsT], writes=[sd])
                        P.op("vector", lambda e: e.reciprocal(out=rs.t[:], in_=sd.t[:]), reads=[sd], writes=[rs])
                        P.op("vector", lambda e: e.scalar_tensor_tensor(out=xt.t[:], in0=xt.t[:], scalar=rs.t[:, 0:1], in1=self.Aabc.t[:],
                                                                        op0=ALU.mult, op1=ALU.mult), reads=[xt, rs, self.Aabc], writes=[xt])
                        H = D // 4
                        P.op("gpsimd", lambda e: e.tensor_tensor(out=hb.t[:, 0:H], in0=xt.t[:, 0:H], in1=self.Babc.t[:, 0:H], op=ALU.add),
                             reads=[xt, self.Babc], writes=[hb])
                        P.op("vector", lambda e: e.tensor_tensor(out=hb.t[:, H:D], in0=xt.t[:, H:D], in1=self.Babc.t[:, H:D], op=ALU.add),
                             reads=[xt, self.Babc], appends=[hb])
                        flush()
                        for half in range(2):
                            pt = ptr.next()

                            def tr(e, pt=pt, half=half):
                                ins = None
                                for j in range(8):
                                    kc = half * 8 + j
                                    ins = e.transpose(out=pt.t[:, j * 128:(j + 1) * 128], in_=hb.t[:, kc * 128:(kc + 1) * 128], identity=self.ident.t[:])
                                return ins
                            P.op("tensor", tr, reads=[hb, self.ident], writes=[pt])
                            eng = "scalar" if half == 0 else "vector"
                            dst = hT[:, half * 8:(half + 1) * 8, i * 128:(i + 1) * 128]
                            src = pt.t[:].rearrange("p (k t) -> p k t", k=8)
                            pending.append((eng, dst, src, pt, hTb[i // 4]))
                    a1(i)
                flush()
            self.fence()
            with ExitStack() as s2:
                T2 = lambda name, shape, dt, dma=False: self.tile(s2, name, shape, dt, dma)
                wr = Ring([T2("wA%d" % i, [128, KC, 256], BF16, dma="sw") for i in range(2)])
                cosr = Ring([T2("cosA%d" % i, [128, 512], F32, dma=True) for i in range(2)])
                sinr = Ring([T2("sinA%d" % i, [128, 512], F32, dma=True) for i in range(2)])
                tmpAr = Ring([T2("tmpA%d" % i, [128, 512], F32) for i in range(2)])
                tmpBr = Ring([T2("tmpB%d" % i, [128, 512], F32) for i in range(2)])
                obr = Ring([T2("obA%d" % i, [128, 512], BF16, dma=True) for i in range(4)])
                sq = T2("sqA", [128, 4, 512], BF16)
                lat32 = T2("lat32", [128, 4, 512], F32)
                sdA = T2("sdA", [128, 512], F32)
                rsA = T2("rsA", [128, 512], F32)
                latR = Ring([T2("lat%d" % i, [128, 4, 512], BF16) for i in range(2)])
                wup = T2("wup", [128, 4, 2048], BF16, dma="sw")
                nwq = T2("nwq", [128, 4], F32, dma=True)
                nwkv = T2("nwkv", [128, 4], F32, dma=True)
                psA = Ring([self.ptile(s2, "psA%d" % i, [128, 512], F32) for i in range(4)])
                pss = self.ptile(s2, "pssA", [128, 512], F32)
                pu = Ring([self.ptile(s2, "puA%d" % i, [128, 512], F32) for i in range(3)])
                self.dma("sync", nwq.t[:], self.qnw, nwq, writes=[nwq])
                self.dma("sync", nwkv.t[:], self.kvnw, nwkv, writes=[nwkv])
                win_v = self.w_in.rearrange("(kc p) n -> p kc n", p=128)
                wkr_v = self.w_kr.rearrange("(kc p) n -> p kc n", p=128)
                groups = []
                for i in range(4):
                    groups.append(("cq" if i < 2 else "ckv", i % 2, win_v[:, :, 4096 + i * 256: 4096 + (i + 1) * 256]))
                groups.append(("kro", 0, wkr_v))
                for i in range(4):
                    groups.append(("q", i, win_v[:, :, i * 256:(i + 1) * 256]))
                for i in range(4):
                    groups.append(("k", i, win_v[:, :, 1024 + i * 256: 1024 + (i + 1) * 256]))
                for i in range(4):
                    groups.append(("v", i, win_v[:, :, 2048 + i * 256: 2048 + (i + 1) * 256]))
                for i in range(4):
                    groups.append(("g", i, win_v[:, :, 3072 + i * 256: 3072 + (i + 1) * 256]))
                wbufs = {}

                def load_w(gi):
                    if gi < len(groups) and gi not in wbufs:
                        wb = wr.next()
                        self.dma("gpsimd", wb.t[:], groups[gi][2], wb, writes=[wb])
                        wbufs[gi] = wb

                def proj_fm(pb, wb, j, tt):
                    def mm(e):
                        ins = None
                        for kc in range(KC):
                            ins = e.matmul(pb.t[:], lhsT=wb.t[:, kc, j * 128:(j + 1) * 128], rhs=hT[:, kc, tt * 512:(tt + 1) * 512],
                                           start=(kc == 0), stop=(kc == KC - 1))
                        return ins
                    P.op("tensor", mm, reads=[wb, hTb[tt]], writes=[pb])

                def store(dst_buf, dst_ap, ob):
                    self.dma("sync", dst_ap, ob.t[:], ob, reads=[ob], appends=[dst_buf])

                def load_tab(ci, tt):
                    ct = cosr.next(); stt = sinr.next()
                    self.dma("sync", ct.t[:], self.tabs[ci].t[:, tt * 512:(tt + 1) * 512], ct, reads=[self.tabs[ci]], writes=[ct])
                    self.dma("sync", stt.t[:], self.tabs[ci + 1].t[:, tt * 512:(tt + 1) * 512], stt, reads=[self.tabs[ci + 1]], writes=[stt])
                    return ct, stt

                gi = 0
                while gi < len(groups):
                    kind, idx, _ = groups[gi]
                    load_w(gi)
                    load_w(gi + 1)
                    if kind in ("cq", "ckv"):
                        wb0, wb1 = wbufs[gi], wbufs[gi + 1]
                        if kind == "cq":
                            self.dma("gpsimd", wup.t[:, :, 0:1024], self.w_uqn.rearrange("(kc p) n -> p kc n", p=128), wup, writes=[wup])
                            self.dma("gpsimd", wup.t[:, :, 1024:1536], self.w_uqr.rearrange("(kc p) n -> p kc n", p=128), wup, appends=[wup])
                            nw = nwq
                        else:
                            self.dma("gpsimd", wup.t[:, :, 0:1024], self.w_uk.rearrange("(kc p) n -> p kc n", p=128), wup, writes=[wup])
                            self.dma("gpsimd", wup.t[:, :, 1024:2048], self.w_uv.rearrange("(kc p) n -> p kc n", p=128), wup, appends=[wup])
                            nw = nwkv
                        def norm_part(tt, kind=kind, nw=nw, wb0=wb0, wb1=wb1):
                            pbs = [psA.next() for _ in range(4)]
                            for fc in range(4):
                                proj_fm(pbs[fc], wb0 if fc < 2 else wb1, fc % 2, tt)
                            for fc in range(4):
                                def ev(fc):
                                    pb = pbs[fc]
                                    P.op("scalar", lambda e: e.activation(out=lat32.t[:, fc, :], in_=pb.t[:], func=AF.Copy),
                                         reads=[pb], writes=[lat32] if fc == 0 else (), appends=[lat32] if fc > 0 else ())
                                    P.op("scalar", lambda e: e.activation(out=sq.t[:, fc, :], in_=pb.t[:], func=AF.Square),
                                         reads=[pb], writes=[sq] if fc == 0 else (), appends=[sq] if fc > 0 else ())
                                ev(fc)
                            return pbs

                        def norm_finish(tt, nw=nw):
                            lt = latR.next()

                            def ones_mm(e):
                                ins = None
                                for fc in range(4):
                                    ins = e.matmul(pss.t[:], lhsT=self.onesb.t[:], rhs=sq.t[:, fc, :], start=(fc == 0), stop=(fc == 3))
                                return ins
                            P.op("tensor", ones_mm, reads=[sq, self.onesb], writes=[pss])
                            P.op("scalar", lambda e: e.activation(out=sdA.t[:], in_=pss.t[:], func=AF.Ln, bias=self.epsT.t[:], scale=1.0 / 512.0),
                                 reads=[pss, self.epsT], writes=[sdA])
                            P.op("scalar", lambda e: e.activation(out=rsA.t[:], in_=sdA.t[:], func=AF.Exp, scale=-0.5), reads=[sdA], writes=[rsA])
                            for fc in range(4):
                                def nm(fc):
                                    P.op("vector", lambda e: e.scalar_tensor_tensor(
                                        out=lt.t[:, fc, :], in0=lat32.t[:, fc, :], scalar=nw.t[:, fc:fc + 1], in1=rsA.t[:], op0=ALU.mult, op1=ALU.mult),
                                        reads=[lat32, nw, rsA], writes=[lt] if fc == 0 else (), appends=[lt] if fc > 0 else ())
                                nm(fc)
                            return lt

                        def up_part(tt, lt, tabs_, kind=kind):
                            def up_fm(pb, c0):
                                def mm(e):
                                    ins = None
                                    for kc in range(4):
                                        ins = e.matmul(pb.t[:], lhsT=wup.t[:, kc, c0:c0 + 128], rhs=lt.t[:, kc, :], start=(kc == 0), stop=(kc == 3))
                                    return ins
                                P.op("tensor", mm, reads=[wup, lt], writes=[pb])

                            def plain(h, dst_list):
                                pb = pu.next(); ob = obr.next()
                                up_fm(pb, h * 128)
                                P.op("scalar", lambda e: e.activation(out=ob.t[:], in_=pb.t[:], func=AF.Copy), reads=[pb], writes=[ob])
                                store(dst_list[h], dst_list[h].t[:, tt * 512:(tt + 1) * 512], ob)
                            if kind == "cq":
                                ct, stt = tabs_
                                for h in range(8):
                                    plain(h, self.qn_s)
                                for pr in range(4):
                                    pb = pu.next(); ob = obr.next()
                                    up_fm(pb, 1024 + pr * 128)
                                    self.rope_fm(pb, ct, stt, 1.0, tmpAr.next(), tmpBr.next(), ob)
                                    store(self.qp_s[pr], self.qp_s[pr].t[:, tt * 512:(tt + 1) * 512], ob)
                            else:
                                for h in range(8):
                                    plain(h, self.kn_s)
                                for sub in range(4):
                                    for half in range(2):
                                        def vpart(sub, half):
                                            pb = pu.next(); ob = obr.next()

                                            def mmv(e):
                                                ins = None
                                                for kc in range(4):
                                                    ins = e.matmul(pb.t[:], lhsT=lt.t[:, kc, sub * 128:(sub + 1) * 128],
                                                                   rhs=wup.t[:, kc, 1024 + half * 512: 1024 + (half + 1) * 512], start=(kc == 0), stop=(kc == 3))
                                                return ins
                                            P.op("tensor", mmv, reads=[wup, lt], writes=[pb])
                                            P.op("vector", lambda e: e.tensor_copy(out=ob.t[:], in_=pb.t[:]), reads=[pb], writes=[ob])
                                            c = tt * 4 + sub
                                            dst = self.vm_s.t[half * 4:(half + 1) * 4, :, c, :].rearrange("h p e -> p h e")
                                            self.dma("sync", dst, ob.t[:].rearrange("p (h e) -> p h e", h=4), ob, reads=[ob], appends=[self.vm_s])
                                        vpart(sub, half)

                        prev = None
                        for tt in range(NTT):
                            tabs_ = load_tab(2, tt) if kind == "cq" else None
                            norm_part(tt)
                            if prev is not None:
                                up_part(*prev)
                            lt = norm_finish(tt)
                            prev = (tt, lt, tabs_)
                        up_part(*prev)
                        gi += 2
                        continue
                    wb = wbufs[gi]
                    if kind in ("q", "k", "kro"):
                        tabi = 0 if kind != "kro" else 2
                        nxt_tab = load_tab(tabi, 0)
                        for tt in range(NTT):
                            ct, stt = nxt_tab
                            if tt + 1 < NTT:
                                nxt_tab = load_tab(tabi, tt + 1)
                            for j in range(2):
                                pb = psA.next(); ob = obr.next()
                                proj_fm(pb, wb, j, tt)
                                self.rope_fm(pb, ct, stt, (128 ** -0.5) if kind == "k" else 1.0, tmpAr.next(), tmpBr.next(), ob)
                                if kind == "kro":
                                    dstb = self.kro_s[j]
                                else:
                                    dstb = (self.qr_s if kind == "q" else self.kr_s)[idx * 2 + j]
                                store(dstb, dstb.t[:, tt * 512:(tt + 1) * 512], ob)
                    elif kind == "g":
                        for tt in range(NTT):
                            for j in range(2):
                                pb = psA.next(); ob = obr.next()
                                proj_fm(pb, wb, j, tt)
                                P.op("scalar", lambda e, pb=pb, ob=ob: e.activation(out=ob.t[:], in_=pb.t[:], func=AF.Silu), reads=[pb], writes=[ob])
                                dstb = self.sg_s[idx * 2 + j]
                                store(dstb, dstb.t[:, tt * 512:(tt + 1) * 512], ob)
                    elif kind == "v":
                        for i in range(NT):
                            pb = psA.next(); ob = obr.next()

                            def mmv(e, pb=pb, i=i, wb=wb):
                                ins = None
                                for kc in range(KC):
                                    ins = e.matmul(pb.t[:, 0:256], lhsT=hT[:, kc, i * 128:(i + 1) * 128], rhs=wb.t[:, kc, :], start=(kc == 0), stop=(kc == KC - 1))
                                return ins
                            P.op("tensor", mmv, reads=[wb, hTb[i // 4]], writes=[pb])
                            eng = "scalar" if i % 2 == 0 else "vector"
                            if eng == "scalar":
                                P.op(eng, lambda e, pb=pb, ob=ob: e.activation(out=ob.t[:, 0:256], in_=pb.t[:, 0:256], func=AF.Copy), reads=[pb], writes=[ob])
                            else:
                                P.op(eng, lambda e, pb=pb, ob=ob: e.tensor_copy(out=ob.t[:, 0:256], in_=pb.t[:, 0:256]), reads=[pb], writes=[ob])
                            dst = self.vr_s.t[idx * 2:(idx + 1) * 2, :, i, :].rearrange("h p e -> p h e")
                            self.dma("sync", dst, ob.t[:, 0:256].rearrange("p (h e) -> p h e", h=2), ob, reads=[ob], appends=[self.vr_s])
                    gi += 1

    def stageB_mla(self, top):
        P = self.P
        SC = 192 ** -0.5
        self.fence()
        with ExitStack() as st:
            T = lambda name, shape, dt, dma=False: self.tile(st, name, shape, dt, dma)
            Qn = Ring([T("Qn%d" % i, [128, S], BF16, dma=True) for i in range(2)])
            Qp = Ring([T("Qp%d" % i, [128, S], BF16, dma=True) for i in range(2)])
            Kn = Ring([T("Kn%d" % i, [128, S], BF16, dma=True) for i in range(2)])
            Vm = Ring([T("Vm%d" % i, [128, NT, 128], BF16, dma=True) for i in range(2)])
            Kr = [T("Kr%d" % i, [128, S], BF16, dma=True) for i in range(2)]
            ssq = T("ssqm", [128, S], F32, dma=True)
            ptr = Ring([T("ptm%d" % i, [128, 512], BF16) for i in range(6)])
            rec = T("recm", [128, 512], F32)
            p2R = Ring([T("p2m%d" % i, [128, 512], BF16) for i in range(4)])
            o32 = Ring([T("o32m%d" % i, [128, 512], F32) for i in range(2)])
            sqm = T("sqm", [128, 512], F32)
            obr = Ring([T("obm%d" % i, [128, 512], BF16, dma=True) for i in range(2)])
            psS = Ring([self.ptile(st, "psS%d" % i, [128, 512], F32) for i in range(4)])
            poR = Ring([self.ptile(st, "poM%d" % i, [128, 512], F32) for i in range(2)])
            pdR = Ring([self.ptile(st, "pdM%d" % i, [128, 512], F32) for i in range(2)])
            for i in range(2):
                self.dma("sync", Kr[i].t[:], self.kro_s[i].t, Kr[i], reads=[self.kro_s[i]], writes=[Kr[i]])
            stg = Ring([T("stgW%d" % i, [128, KC, 512], BF16, dma="sw") for i in range(2)])
            pieces = []
            sv = self.w_o.rearrange("(kc p) n -> p kc n", p=128)
            for cg in range(4):
                pieces.append((sv[:, :, cg * 512:(cg + 1) * 512], None, self.wo_b.t[cg], None, self.wo_b, KC))
            for (src, c_off) in ((self.w_gate, 0), (self.w_up, 256)):
                sv = src.rearrange("(kc p) n -> p kc n", p=128)
                for i in range(FC // 4):
                    dv = self.wg_b.t[2 * i:2 * i + 2, :, :, c_off:c_off + 256].rearrange("g p kc c -> p kc g c")
                    pieces.append((sv[:, :, i * 512:(i + 1) * 512], "split", dv, None, self.wg_b, KC))
            sv = self.w_down.rearrange("(fc p) n -> p fc n", p=128)
            for fq in range(4):
                for cg in range(4):
                    pieces.append((sv[:, fq * 11:(fq + 1) * 11, cg * 512:(cg + 1) * 512], None, self.wd_b.t[cg, fq], None, self.wd_b, 11))
            cast_state = {"i": 0, "pend": None}

            def cast_step():
                if cast_state["pend"] is not None:
                    (dst_ap, src_sb, sb_, dbuf) = cast_state["pend"]
                    self.dma("gpsimd", dst_ap, src_sb, sb_, reads=[sb_], appends=[dbuf])
                    cast_state["pend"] = None
                i = cast_state["i"]
                if i < len(pieces):
                    (sap, mode, dst_ap, _, dbuf, nk) = pieces[i]
                    sb_ = stg.next()
                    self.dma("gpsimd", sb_.t[:, 0:nk, :], sap, sb_, writes=[sb_])
                    src_sb = sb_.t[:, 0:nk, :]
                    if mode == "split":
                        src_sb = sb_.t[:, :, :].rearrange("p kc (g c) -> p kc g c", g=2)
                    cast_state["pend"] = (dst_ap, src_sb, sb_, dbuf)
                    cast_state["i"] = i + 1
            loaded = {}

            def load_head(h):
                if h >= 8 or h in loaded:
                    return
                qn = Qn.next(); qp = Qp.next(); kn = Kn.next(); vm = Vm.next()
                self.dma("sync", qn.t[:], self.qn_s[h].t, qn, reads=[self.qn_s[h]], writes=[qn])
                self.dma("sync", qp.t[:], self.qp_s[h // 2].t, qp, reads=[self.qp_s[h // 2]], writes=[qp])
                self.dma("sync", kn.t[:], self.kn_s[h].t, kn, reads=[self.kn_s[h]], writes=[kn])
                self.dma("sync", vm.t[:], self.vm_s.t[h], vm, reads=[self.vm_s], writes=[vm])
                loaded[h] = (qn, qp, kn, vm)

            tails = []
            load_head(0)
            for h in range(8):
                load_head(h + 1)
                qn, qp, kn, vm = loaded[h]
                kr = Kr[h % 2]
                for qt in range(NTT):
                  def do_qt(h, qt, qn, qp, kn, vm, kr):
                    po = poR.next(); pd = pdR.next()
                    qs = slice(qt * 512, (qt + 1) * 512)

                    def s_op(kt):
                        ps = psS.next()

                        def mm(e, ps=ps, kt=kt):
                            e.matmul(ps.t[:], lhsT=kn.t[:, kt * 128:(kt + 1) * 128], rhs=qn.t[:, qs], start=True, stop=False)
                            return e.matmul(ps.t[:], lhsT=kr.t[:, kt * 128:(kt + 1) * 128], rhs=qp.t[:, qs], start=False, stop=True)
                        P.op("tensor", mm, reads=[kn, qn, kr, qp], writes=[ps])
                        return ps
                    LA = 2
                    psq = [s_op(k_) for k_ in range(LA)]
                    prev_tail = tails.pop(0) if tails else None
                    cast_step()
                    pt_prev = None
                    p2_prev = None
                    for kt in range(NT):
                        ps = psq.pop(0)
                        if kt + LA < NT:
                            psq.append(s_op(kt + LA))
                        if prev_tail is not None and kt in prev_tail:
                            prev_tail[kt]()
                        pt = ptr.next()
                        P.op("scalar", lambda e, ps=ps, pt=pt: e.activation(out=pt.t[:], in_=ps.t[:], func=AF.Exp, scale=SC), reads=[ps], writes=[pt])

                        def pv(e, pt=pt, kt=kt, po=po):
                            return e.matmul(po.t[:], lhsT=vm.t[:, kt, :], rhs=pt.t[:], start=(kt == 0), stop=(kt == NT - 1))
                        if kt == 0:
                            P.op("tensor", pv, reads=[vm, pt], writes=[po])
                        else:
                            P.op("tensor", pv, reads=[vm, pt], appends=[po])
                        if kt % 2 == 0:
                            pt_prev = pt
                        else:
                            p2 = p2R.next()
                            P.op("vector", lambda e, p2=p2, a=pt_prev, b=pt: e.tensor_tensor(out=p2.t[:], in0=a.t[:], in1=b.t[:], op=ALU.add), reads=[pt_prev, pt], writes=[p2])

                            def dn(e, p2=p2, kt=kt, pd=pd):
                                return e.matmul(pd.t[:], lhsT=self.onesb.t[:], rhs=p2.t[:], start=(kt == 1), stop=(kt == NT - 1))
                            if kt == 1:
                                P.op("tensor", dn, reads=[p2, self.onesb], writes=[pd])
                            else:
                                P.op("tensor", dn, reads=[p2, self.onesb], appends=[pd])
                    o3 = o32.next(); ob = obr.next()

                    def t3r(c):
                        def f():
                            cc = slice(c * 128, (c + 1) * 128)
                            P.op("vector", lambda e: e.reciprocal(out=rec.t[:, cc], in_=pd.t[:, cc]), reads=[pd], writes=[rec] if c == 0 else [], appends=[] if c == 0 else [rec])
                        return f

                    def t3():
                        P.op("vector", lambda e: e.tensor_tensor(out=o3.t[:], in0=po.t[:], in1=rec.t[:], op=ALU.mult), reads=[po, rec], writes=[o3])
                        P.op("scalar", lambda e: e.activation(out=sqm.t[:], in_=o3.t[:], func=AF.Square), reads=[o3], writes=[sqm])
                        P.op("vector", lambda e: e.tensor_copy(out=ob.t[:], in_=o3.t[:]), reads=[o3], writes=[ob])
                        r0 = 1024 + h * 128
                        self.dma("sync", self.mix_s.t[r0:r0 + 128, qs], ob.t[:], ob, reads=[ob], appends=[self.mix_s])

                    def t45():
                        P.op("tensor", lambda e: e.matmul(pd.t[:], lhsT=self.ones32.t[:], rhs=sqm.t[:], start=True, stop=True), reads=[sqm, self.ones32], writes=[pd])
                        if h == 0:
                            P.op("vector", lambda e: e.tensor_copy(out=ssq.t[:, qs], in_=pd.t[:]), reads=[pd], appends=[ssq])
                        else:
                            P.op("vector", lambda e: e.tensor_tensor(out=ssq.t[:, qs], in0=ssq.t[:, qs], in1=pd.t[:], op=ALU.add), reads=[pd, ssq], appends=[ssq])
                    tails.append({2: t3r(0), 4: t3r(1), 6: t3r(2), 8: t3r(3), 10: t3, 16: t45})
                  do_qt(h, qt, qn, qp, kn, vm, kr)
            for tl in tails:
                for k_ in sorted(tl):
                    tl[k_]()
            while cast_state["pend"] is not None or cast_state["i"] < len(pieces):
                cast_step()
            P.op("scalar", lambda e: e.activation(out=ssq.t[:], in_=ssq.t[:], func=AF.Sqrt, bias=self.epsT.t[:], scale=1.0 / 1024.0),
                 reads=[ssq, self.epsT], writes=[ssq])
            P.op("vector", lambda e: e.reciprocal(out=ssq.t[:], in_=ssq.t[:]), reads=[ssq], writes=[ssq])
            self.dma("sync", self.rstdm_s.t, ssq.t[:], ssq, reads=[ssq], writes=[self.rstdm_s])

    def stageB_ret(self, top):
        P = self.P
        self.fence()
        with ExitStack() as st:
            T = lambda name, shape, dt, dma=False: self.tile(st, name, shape, dt, dma)
            V = lambda fn, reads, writes=(), appends=(): P.op("vector", fn, reads=reads, writes=writes, appends=appends)
            A = lambda fn, reads, writes=(), appends=(): P.op("scalar", fn, reads=reads, writes=writes, appends=appends)
            G = lambda fn, reads, writes=(), appends=(): P.op("gpsimd", fn, reads=reads, writes=writes, appends=appends)
            abc = T("abc", [128, 16], F32, dma=True)
            lg = T("lg", [128, 16], F32)
            gC = T("gC", [128, 16], F32)
            gnw = T("gnwS", [128, 8], F32, dma=True)
            gnb = T("gnbS", [128, 8], F32, dma=True)
            ii = T("iiR", [128, 128], I32)
            diff = T("diffR", [128, 128], F32)
            posp = T("pospR", [128, 128], F32)
            negp = T("negpR", [128, 128], F32)
            mF = T("mFR", [128, 128], F32)
            mB = T("mBR", [128, 128], F32)
            cp1 = T("cp1R", [128, 128], F32)
            r128 = T("r128R", [128, 128], F32)
            ci = T("ciR", [128, 2], I32)
            cf = T("cfR", [128, 2], F32)
            t1 = T("t1R", [128, 128], F32)
            t2 = T("t2R", [128, 128], F32)
            DT = T("DTR", [128, 8, 128], F32)
            xif = T("xifR", [128, 8, 128], F32)
            xib = T("xibR", [128, 8, 128], F32)
            zf = T("zfR", [128, 8], F32)
            zb = T("zbR", [128, 8], F32)
            self.dma("sync", abc.t[:], self.ret_decay[0].partition_broadcast(128), abc, writes=[abc])
            self.dma("sync", gnw.t[:], self.gnw, gnw, writes=[gnw])
            self.dma("sync", gnb.t[:], self.gnb, gnb, writes=[gnb])
            A(lambda e: e.activation(out=lg.t[:], in_=abc.t[:], func=AF.Exp), [abc], [lg])
            V(lambda e: e.tensor_scalar_mul(out=lg.t[:], in0=lg.t[:], scalar1=-1.0), [lg], [lg])
            A(lambda e: e.activation(out=gC.t[:], in_=lg.t[:], func=AF.Exp, scale=128.0), [lg], [gC])
            G(lambda e: e.iota(ii.t[:], pattern=[[1, 128]], base=0, channel_multiplier=-1), [], [ii])
            V(lambda e: e.tensor_copy(out=diff.t[:], in_=ii.t[:]), [ii], [diff])
            V(lambda e: e.tensor_scalar_max(out=posp.t[:], in0=diff.t[:], scalar1=0.0), [diff], [posp])
            V(lambda e: e.tensor_scalar(out=negp.t[:], in0=diff.t[:], scalar1=-1.0, scalar2=0.0, op0=ALU.mult, op1=ALU.max), [diff], [negp])
            V(lambda e: e.tensor_single_scalar(out=mF.t[:], in_=diff.t[:], scalar=0.0, op=ALU.is_ge), [diff], [mF])
            V(lambda e: e.tensor_single_scalar(out=mB.t[:], in_=diff.t[:], scalar=0.0, op=ALU.is_lt), [diff], [mB])
            G(lambda e: e.iota(ii.t[:], pattern=[[1, 128]], base=1, channel_multiplier=0), [], [ii])
            V(lambda e: e.tensor_copy(out=cp1.t[:], in_=ii.t[:]), [ii], [cp1])
            V(lambda e: e.tensor_scalar(out=r128.t[:], in0=cp1.t[:], scalar1=-1.0, scalar2=129.0, op0=ALU.mult, op1=ALU.add), [cp1], [r128])
            G(lambda e: e.iota(ci.t[:, 0:1], pattern=[[0, 1]], base=127, channel_multiplier=-1), [], [ci])
            G(lambda e: e.iota(ci.t[:, 1:2], pattern=[[0, 1]], base=0, channel_multiplier=1), [], [], [ci])
            V(lambda e: e.tensor_copy(out=cf.t[:], in_=ci.t[:]), [ci], [cf])
            for h in range(8):
                def tabs_h(h):
                    lf = lg.t[:, h:h + 1]
                    lb = lg.t[:, 8 + h:9 + h]
                    A(lambda e: e.activation(out=t1.t[:], in_=posp.t[:], func=AF.Exp, scale=lf), [posp, lg], [t1])
                    A(lambda e: e.activation(out=t2.t[:], in_=negp.t[:], func=AF.Exp, scale=lb), [negp, lg], [t2])
                    V(lambda e: e.tensor_tensor(out=t1.t[:], in0=t1.t[:], in1=mF.t[:], op=ALU.mult), [t1, mF], [t1])
                    V(lambda e: e.tensor_tensor(out=t2.t[:], in0=t2.t[:], in1=mB.t[:], op=ALU.mult), [t2, mB], [t2])
                    V(lambda e: e.tensor_tensor(out=DT.t[:, h, :], in0=t1.t[:], in1=t2.t[:], op=ALU.add), [t1, t2], [], [DT])
                    A(lambda e: e.activation(out=xif.t[:, h, :], in_=cp1.t[:], func=AF.Exp, scale=lf), [cp1, lg], [], [xif])
                    A(lambda e: e.activation(out=xib.t[:, h, :], in_=r128.t[:], func=AF.Exp, scale=lb), [r128, lg], [], [xib])
                    A(lambda e: e.activation(out=zf.t[:, h:h + 1], in_=cf.t[:, 0:1], func=AF.Exp, scale=lf), [cf, lg], [], [zf])
                    A(lambda e: e.activation(out=zb.t[:, h:h + 1], in_=cf.t[:, 1:2], func=AF.Exp, scale=lb), [cf, lg], [], [zb])
                tabs_h(h)
            qR = Ring([T("qR%d" % i, [128, S], BF16, dma=True) for i in range(2)])
            kR = Ring([T("kR%d" % i, [128, S], BF16, dma=True) for i in range(2)])
            sR = Ring([T("sR%d" % i, [128, S], BF16, dma=True) for i in range(2)])
            vR = Ring([T("vR%d" % i, [128, NT, 128], BF16, dma=True) for i in range(2)])
            Kzf = T("Kzf", [128, S], BF16)
            Kzb = T("Kzb", [128, S], BF16)
            Qxf = T("Qxf", [128, NT, 128], BF16)
            Qxb = T("Qxb", [128, NT, 128], BF16)
            SFb = T("SFb", [128, NT, 128], BF16)
            SBb = T("SBb", [128, NT, 128], BF16)
            S32 = [[T("S32_%d_%d" % (d_, i), [128, 128], F32) for i in range(2)] for d_ in range(2)]
            ATr = Ring([T("ATr%d" % i, [128, 4, 128], BF16) for i in range(2)])
            mhalf = T("mhalfR", [128, 512], F32)
            V(lambda e: e.memset(mhalf.t[:], -0.5), [], [mhalf])
            tmpR = Ring([[T("r6_%d_%d" % (k_, i), [128, 512], F32) for k_ in range(8)] for i in range(2)])
            obr = Ring([T("obR%d" % i, [128, 512], BF16, dma=True) for i in range(2)])
            banks = [self.ptile(st, "bkR%d" % i, [128, 512], F32) for i in range(8)]
            ptk = Ring(banks[0:2])
            pkv = Ring(banks[2:4])
            pscR = Ring(banks[4:6])
            pyR = Ring(banks[6:8])
            pstR = Ring([(banks[0], banks[1]), (banks[2], banks[3])])
            loaded = {}

            def load_head(h):
                if h >= 8 or h in loaded:
                    return
                q = qR.next(); k = kR.next(); sg = sR.next(); v = vR.next()
                self.dma("sync", q.t[:], self.qr_s[h].t, q, reads=[self.qr_s[h]], writes=[q])
                self.dma("sync", k.t[:], self.kr_s[h].t, k, reads=[self.kr_s[h]], writes=[k])
                self.dma("sync", sg.t[:], self.sg_s[h].t, sg, reads=[self.sg_s[h]], writes=[sg])
                self.dma("sync", v.t[:], self.vr_s.t[h], v, reads=[self.vr_s], writes=[v])
                loaded[h] = (q, k, sg, v)

            def do_head(h, q, k, sg, v):
                for g8 in range(4):
                    def r1(g8):
                        pt = ptk.next()

                        ptb = pt.t[:].bitcast(BF16)

                        def tr(e):
                            ins = None
                            for j in range(8):
                                n = g8 * 8 + j
                                ins = e.transpose(out=ptb[:, j * 128:(j + 1) * 128], in_=k.t[:, n * 128:(n + 1) * 128], identity=self.ident.t[:])
                            return ins
                        P.op("tensor", tr, reads=[k, self.ident], writes=[pt])
                        cs = slice(g8 * 1024, (g8 + 1) * 1024)
                        V(lambda e: e.tensor_scalar_mul(out=Kzf.t[:, cs], in0=ptb, scalar1=zf.t[:, h:h + 1]), [pt, zf], [Kzf] if g8 == 0 else [], [] if g8 == 0 else [Kzf])
                        A(lambda e: e.activation(out=Kzb.t[:, cs], in_=ptb, func=AF.Identity, scale=zb.t[:, h:h + 1]), [pt, zb, Kzf], [Kzb] if g8 == 0 else [], [] if g8 == 0 else [Kzb])
                    r1(g8)
                q3 = q.t[:].rearrange("p (n c) -> p n c", c=128)
                V(lambda e: e.tensor_tensor(out=Qxf.t[:], in0=q3, in1=xif.t[:, h:h + 1, :].to_broadcast([128, NT, 128]), op=ALU.mult), [q, xif], [Qxf])
                G(lambda e: e.tensor_tensor(out=Qxb.t[:], in0=q3, in1=xib.t[:, h:h + 1, :].to_broadcast([128, NT, 128]), op=ALU.mult), [q, xib], [Qxb])
                V(lambda e: e.memset(SFb.t[:, 0, :], 0.0), [], [SFb])
                V(lambda e: e.memset(SBb.t[:, NT - 1, :], 0.0), [], [SBb])
                V(lambda e: e.memset(S32[0][0].t[:], 0.0), [], [S32[0][0]])
                V(lambda e: e.memset(S32[1][0].t[:], 0.0), [], [S32[1][0]])
                cur = [0, 0]
                for gi in range(8):
                    def scan_g(gi):
                        pks = []
                        for direction in range(2):
                            g4 = gi if direction == 0 else 7 - gi
                            Kz = Kzf if direction == 0 else Kzb
                            pk = pkv.next()

                            def kvmm(e, g4=g4, pk=pk, Kz=Kz):
                                ins = None
                                for j in range(4):
                                    n = g4 * 4 + j
                                    ins = e.matmul(pk.t[:, j * 128:(j + 1) * 128], lhsT=Kz.t[:, n * 128:(n + 1) * 128], rhs=v.t[:, n, :], start=True, stop=True)
                                return ins
                            P.op("tensor", kvmm, reads=[Kz, v], writes=[pk])
                            pks.append((g4, pk))
                        for jj in range(4):
                            for direction in range(2):
                                g4, pk = pks[direction]
                                j = jj if direction == 0 else 3 - jj
                                n = g4 * 4 + j
                                nxt = n + 1 if direction == 0 else n - 1
                                if nxt < 0 or nxt >= NT:
                                    continue

                                def step(direction=direction, j=j, nxt=nxt, pk=pk):
                                    Sb = SFb if direction == 0 else SBb
                                    gcol = gC.t[:, h:h + 1] if direction == 0 else gC.t[:, 8 + h:9 + h]
                                    so = S32[direction][cur[direction]]; sn = S32[direction][1 - cur[direction]]
                                    V(lambda e: e.scalar_tensor_tensor(out=sn.t[:], in0=so.t[:], scalar=gcol, in1=pk.t[:, j * 128:(j + 1) * 128],
                                                                       op0=ALU.mult, op1=ALU.add), [so, gC, pk], [sn])
                                    G(lambda e: e.tensor_copy(out=Sb.t[:, nxt, :], in_=sn.t[:]), [sn], [], [Sb])
                                step()
                                cur[direction] = 1 - cur[direction]
                    scan_g(gi)
                ctx = {}

                def st1(g4):
                    c = {}
                    c["AT"] = ATr.next(); c["psc"] = pscR.next()
                    psc = c["psc"]; AT = c["AT"]

                    def scmm(e):
                        ins = None
                        for j in range(4):
                            n = g4 * 4 + j
                            ins = e.matmul(psc.t[:, j * 128:(j + 1) * 128], lhsT=k.t[:, n * 128:(n + 1) * 128], rhs=q.t[:, n * 128:(n + 1) * 128], start=True, stop=True)
                        return ins
                    P.op("tensor", scmm, reads=[k, q], writes=[psc])
                    V(lambda e: e.tensor_tensor(out=AT.t[:], in0=psc.t[:].rearrange("p (n c) -> p n c", c=128),
                                                in1=DT.t[:, h:h + 1, :].to_broadcast([128, 4, 128]), op=ALU.mult), [psc, DT], [AT])
                    ctx[g4] = c

                def st2(g4):
                    c = ctx[g4]
                    AT = c["AT"]
                    c["py"] = pyR.next(); c["tmp"] = tmpR.next()
                    py = c["py"]
                    y32, ysq = c["tmp"][0], c["tmp"][1]

                    def ymm(e):
                        ins = None
                        for j in range(4):
                            n = g4 * 4 + j
                            o = py.t[:, j * 128:(j + 1) * 128]
                            e.matmul(o, lhsT=v.t[:, n, :], rhs=AT.t[:, j, :], start=True, stop=False)
                            e.matmul(o, lhsT=SFb.t[:, n, :], rhs=Qxf.t[:, n, :], start=False, stop=False)
                            ins = e.matmul(o, lhsT=SBb.t[:, n, :], rhs=Qxb.t[:, n, :], start=False, stop=True)
                        return ins
                    P.op("tensor", ymm, reads=[v, AT, SFb, SBb, Qxf, Qxb], writes=[py])
                    A(lambda e: e.activation(out=y32.t[:], in_=py.t[:], func=AF.Copy), [py], [y32])
                    A(lambda e: e.activation(out=ysq.t[:], in_=py.t[:], func=AF.Square), [py], [ysq])

                def st3(g4):
                    c = ctx.pop(g4)
                    cs = slice(g4 * 512, (g4 + 1) * 512)
                    ob = obr.next(); pst = pstR.next()
                    y32, ysq, mn, msq, var, rsd, yc, yo = c["tmp"]
                    P.op("tensor", lambda e: e.matmul(pst[0].t[:], lhsT=self.ones32.t[:], rhs=y32.t[:], start=True, stop=True), reads=[y32, self.ones32], writes=[pst[0]])
                    P.op("tensor", lambda e: e.matmul(pst[1].t[:], lhsT=self.ones32.t[:], rhs=ysq.t[:], start=True, stop=True), reads=[ysq, self.ones32], writes=[pst[1]])
                    V(lambda e: e.tensor_scalar_mul(out=mn.t[:], in0=pst[0].t[:], scalar1=1.0 / 128.0), [pst[0]], [mn])
                    V(lambda e: e.tensor_tensor(out=msq.t[:], in0=mn.t[:], in1=mn.t[:], op=ALU.mult), [mn], [msq])
                    V(lambda e: e.scalar_tensor_tensor(out=var.t[:], in0=pst[1].t[:], scalar=1.0 / 128.0, in1=msq.t[:], op0=ALU.mult, op1=ALU.subtract), [pst[1], msq], [var])
                    A(lambda e: e.activation(out=var.t[:], in_=var.t[:], func=AF.Ln, bias=self.epsT.t[:], scale=1.0), [var, self.epsT], [var])
                    A(lambda e: e.activation(out=rsd.t[:], in_=var.t[:], func=AF.Exp, scale=-0.5), [var], [rsd])
                    G(lambda e: e.tensor_tensor(out=yc.t[:], in0=y32.t[:], in1=mn.t[:], op=ALU.subtract), [y32, mn], [yc])
                    V(lambda e: e.tensor_tensor(out=yc.t[:], in0=yc.t[:], in1=rsd.t[:], op=ALU.mult), [yc, rsd], [yc])
                    V(lambda e: e.tensor_scalar(out=yo.t[:], in0=yc.t[:], scalar1=gnw.t[:, h:h + 1], scalar2=gnb.t[:, h:h + 1], op0=ALU.mult, op1=ALU.add), [yc, gnw, gnb], [yo])
                    G(lambda e: e.tensor_tensor(out=ob.t[:], in0=yo.t[:], in1=sg.t[:, cs], op=ALU.mult), [yo, sg], [ob])
                    self.dma("sync", self.mix_s.t[h * 128:(h + 1) * 128, cs], ob.t[:], ob, reads=[ob], appends=[self.mix_s])

                for it in range(8 + 2):
                    if it < 8:
                        st1(it)
                    if 0 <= it - 1 < 8:
                        st2(it - 1)
                    if 0 <= it - 2 < 8:
                        st3(it - 2)

            load_head(0)
            for h in range(8):
                load_head(h + 1)
                do_head(h, *loaded[h])

    def stageC(self, top):
        P = self.P
        self.fence()
        with ExitStack() as st:
            T = lambda name, shape, dt, dma=False: self.tile(st, name, shape, dt, dma)
            V = lambda fn, reads, writes=(), appends=(): P.op("vector", fn, reads=reads, writes=writes, appends=appends)
            A = lambda fn, reads, writes=(), appends=(): P.op("scalar", fn, reads=reads, writes=writes, appends=appends)
            G = lambda fn, reads, writes=(), appends=(): P.op("gpsimd", fn, reads=reads, writes=writes, appends=appends)
            xt = T("xtC", [128, 4, D], F32, dma=True)
            xsub = [Buf(xt.t, xt.ds) for _ in range(4)]
            for b_ in xsub:
                b_.r = dict(self.barrier)
            osem = [T("osemC%d" % i, [128, 1], F32, dma=True) for i in range(4)]
            mix = T("mixC", [128, KC, 512], BF16, dma=True)
            rsm = T("rsmC", [128, 512], F32, dma=True)
            mow = T("mowC", [128, 8], F32, dma=True)
            h2T = T("h2T", [128, KC, 512], BF16)
            aT = T("aT", [128, FC, 512], BF16)
            cb = [T("cbC%d" % i, [128, D], F32, dma=True) for i in range(2)]
            gsl = Ring([T("gslC%d" % i, [128, 512], F32, dma=True) for i in range(2)])
            hbr = Ring([T("hbC%d" % i, [128, D], BF16) for i in range(2)])
            msr = Ring([T("msC%d" % i, [128, 1], F32) for i in range(2)])
            sdr = Ring([T("sdC%d" % i, [128, 1], F32) for i in range(2)])
            rsr = Ring([T("rsC%d" % i, [128, 1], F32) for i in range(2)])
            tmr = Ring([T("tmC%d" % i, [128, 512], F32) for i in range(2)])
            sgr = Ring([T("sgC%d" % i, [128, 512], F32) for i in range(2)])
            wr = Ring([T("wC%d" % i, [128, KC, 512], BF16, dma="sw") for i in range(2)])
            wdr = Ring([T("wdC%d" % i, [128, 11, 512], BF16, dma="sw") for i in range(2)])
            pgu = Ring([self.ptile(st, "pgu%d" % i, [128, 512], F32) for i in range(4)])
            pdn = [self.ptile(st, "pdn%d" % i, [128, 512], F32) for i in range(4)]
            self.dma("sync", mow.t[:], self.mow, mow, writes=[mow])
            self.dma("sync", cb[1].t[:], self.cb_s[2].t, cb[1], reads=[self.cb_s[2]], writes=[cb[1]])
            x_v = self.x.rearrange("(n p) d -> p n d", p=128)
            mix_v = self.mix_s.t.rearrange("(kc p) t -> p kc t", p=128)
            t32 = aT.t[:, 0:8, :].rearrange("p a b -> p (a b)").bitcast(F32)
            H = D // 2

            def rms(sub, junk_hb):
                ms = msr.next(); sd = sdr.next(); rs = rsr.next()
                A(lambda e: e.activation(out=junk_hb.t[:], in_=xt.t[:, sub, :], func=AF.Square, scale=1.0 / math.sqrt(D), accum_out=ms.t[:]),
                  [xsub[sub]], [junk_hb, ms])
                A(lambda e: e.activation(out=sd.t[:], in_=ms.t[:], func=AF.Sqrt, bias=self.epsT.t[:], scale=1.0), [ms, self.epsT], [sd])
                V(lambda e: e.reciprocal(out=rs.t[:], in_=sd.t[:]), [sd], [rs])
                return rs

            def load_mix(tt):
                ts = slice(tt * 512, (tt + 1) * 512)
                self.dma("sync", mix.t[:], mix_v[:, :, ts], mix, reads=[self.mix_s], writes=[mix])
                self.dma("sync", rsm.t[:], self.rstdm_s.t[:, ts], rsm, reads=[self.rstdm_s], writes=[rsm])

            def nrm_all():
                for j in range(8):
                    def nrm(j):
                        V(lambda e: e.scalar_tensor_tensor(out=mix.t[:, 8 + j, :], in0=mix.t[:, 8 + j, :], scalar=mow.t[:, j:j + 1], in1=rsm.t[:],
                                                           op0=ALU.mult, op1=ALU.mult), [mix, mow, rsm], [], [mix])
                    nrm(j)

            def do_tile(tt):
                if tt == 0:
                    load_mix(0)
                    nrm_all()
                for sub in range(4):
                    r0 = (tt * 4 + sub) * 128
                    self.dma("sync", xt.t[:, sub, :], self.x[r0:r0 + 128, :], osem[sub], writes=[xsub[sub]])
                self.dma("sync", cb[0].t[:], self.cb_s[1].t, cb[0], reads=[self.cb_s[1]], writes=[cb[0]])
                def load_ga(cg):
                    ga = gsl.next()
                    self.dma("sync", ga.t[:], self.cb_s[0].t[:, cg * 512:(cg + 1) * 512], ga, reads=[self.cb_s[0]], writes=[ga])
                    return ga
                ga_next = [load_ga(0)]
                for cg in range(4):
                    def wo_cg(cg):
                        wb = wr.next()
                        ga = ga_next[0]
                        if cg + 1 < 4:
                            ga_next[0] = load_ga(cg + 1)
                        cs = slice(cg * 512, (cg + 1) * 512)
                        self.dma("gpsimd", wb.t[:], self.wo_b.t[cg], wb, reads=[self.wo_b], writes=[wb])
                        for sub in range(4):
                            def wo_sub(sub):
                                pb = pdn[sub]

                                def mm(e):
                                    ins = None
                                    for kc in range(KC):
                                        ins = e.matmul(pb.t[:], lhsT=mix.t[:, kc, sub * 128:(sub + 1) * 128], rhs=wb.t[:, kc, :], start=(kc == 0), stop=(kc == KC - 1))
                                    return ins
                                P.op("tensor", mm, reads=[mix, wb], writes=[pb])
                                tm = tmr.next()
                                V(lambda e: e.tensor_tensor(out=tm.t[:], in0=pb.t[:], in1=ga.t[:], op=ALU.mult), [pb, ga], [tm])
                                V(lambda e: e.tensor_tensor(out=xt.t[:, sub, cs], in0=tm.t[:], in1=xt.t[:, sub, cs], op=ALU.add), [tm, xsub[sub]], [], [xsub[sub]])
                            wo_sub(sub)
                    wo_cg(cg)
                if tt + 1 < NTT:
                    load_mix(tt + 1)
                pending = []

                def flush():
                    for (eng, dst, src, pt, first) in pending:
                        if eng == "scalar":
                            A(lambda e, dst=dst, src=src: e.activation(out=dst, in_=src, func=AF.Copy), [pt], [h2T] if first else [], [] if first else [h2T])
                        else:
                            V(lambda e, dst=dst, src=src: e.tensor_copy(out=dst, in_=src), [pt], [], [h2T])
                    del pending[:]

                for sub in range(4):
                    def n2(sub):
                        hb = hbr.next()
                        rs = rms(sub, hb)
                        V(lambda e: e.scalar_tensor_tensor(out=t32, in0=xt.t[:, sub, :], scalar=rs.t[:, 0:1], in1=cb[0].t[:], op0=ALU.mult, op1=ALU.mult),
                          [xsub[sub], rs, cb[0]], [aT])
                        V(lambda e: e.tensor_tensor(out=hb.t[:], in0=t32, in1=cb[1].t[:], op=ALU.add), [aT, cb[1]], [hb])
                        flush()
                        for half in range(2):
                            def trh(half):
                                pt = pgu.next()
                                ptb = pt.t[:].bitcast(BF16)

                                def tr(e):
                                    ins = None
                                    for j in range(8):
                                        kc = half * 8 + j
                                        ins = e.transpose(out=ptb[:, j * 128:(j + 1) * 128], in_=hb.t[:, kc * 128:(kc + 1) * 128], identity=self.ident.t[:])
                                    return ins
                                P.op("tensor", tr, reads=[hb, self.ident], writes=[pt])
                                dst = h2T.t[:, half * 8:(half + 1) * 8, sub * 128:(sub + 1) * 128]
                                src = ptb.rearrange("p (k t) -> p k t", k=8)
                                pending.append(("scalar" if half == 0 else "vector", dst, src, pt, (sub == 0 and half == 0)))
                            trh(half)
                    n2(sub)
                flush()
                self.dma("sync", cb[0].t[:], self.cb_s[4].t, cb[0], reads=[self.cb_s[4]], writes=[cb[0]])
                for fg in range(FC // 2):
                    def gu(fg):
                        wb = wr.next()
                        self.dma("gpsimd", wb.t[:], self.wg_b.t[fg], wb, reads=[self.wg_b], writes=[wb])
                        if fg == 6 and tt + 1 < NTT:
                            nrm_all()
                        for j in range(2):
                            def gu_j(j):
                                fc = fg * 2 + j
                                pg = pgu.next(); pu = pgu.next()

                                def mm(e):
                                    ins = None
                                    for kc in range(KC):
                                        ins = e.matmul(pg.t[:], lhsT=wb.t[:, kc, j * 128:(j + 1) * 128], rhs=h2T.t[:, kc, :], start=(kc == 0), stop=(kc == KC - 1))
                                    return ins

                                def mm2(e):
                                    ins = None
                                    for kc in range(KC):
                                        ins = e.matmul(pu.t[:], lhsT=wb.t[:, kc, 256 + j * 128: 256 + (j + 1) * 128], rhs=h2T.t[:, kc, :], start=(kc == 0), stop=(kc == KC - 1))
                                    return ins
                                P.op("tensor", mm, reads=[wb, h2T], writes=[pg])
                                P.op("tensor", mm2, reads=[wb, h2T], writes=[pu])
                                sgt = sgr.next()
                                A(lambda e: e.activation(out=sgt.t[:], in_=pg.t[:], func=AF.Silu), [pg], [sgt])
                                first = (fc == 0)
                                V(lambda e: e.tensor_tensor(out=aT.t[:, fc, :], in0=sgt.t[:], in1=pu.t[:], op=ALU.mult), [sgt, pu], [aT] if first else [], [] if first else [aT])
                            gu_j(j)
                    gu(fg)
                for cg in range(4):
                    def dn_cg(cg):
                        cs = slice(cg * 512, (cg + 1) * 512)
                        gf = gsl.next()
                        self.dma("sync", gf.t[:], self.cb_s[3].t[:, cs], gf, reads=[self.cb_s[3]], writes=[gf])
                        for fq in range(4):
                            def dn_fq(fq):
                                wd = wdr.next()
                                self.dma("gpsimd", wd.t[:], self.wd_b.t[cg, fq], wd, reads=[self.wd_b], writes=[wd])
                                for sub in range(4):
                                    def dn_sub(sub):
                                        pb = pdn[sub]

                                        def mm(e):
                                            ins = None
                                            for j in range(11):
                                                fc = fq * 11 + j
                                                ins = e.matmul(pb.t[:], lhsT=aT.t[:, fc, sub * 128:(sub + 1) * 128], rhs=wd.t[:, j, :],
                                                               start=(fc == 0), stop=(fc == FC - 1))
                                            return ins
                                        if fq == 0:
                                            P.op("tensor", mm, reads=[aT, wd], writes=[pb])
                                        else:
                                            P.op("tensor", mm, reads=[aT, wd], appends=[pb])
                                    dn_sub(sub)
                            dn_fq(fq)
                        for sub in range(4):
                            def dn_ev(sub):
                                pb = pdn[sub]
                                tm = tmr.next()
                                V(lambda e: e.tensor_tensor(out=tm.t[:], in0=pb.t[:], in1=gf.t[:], op=ALU.mult), [pb, gf], [tm])
                                V(lambda e: e.tensor_tensor(out=xt.t[:, sub, cs], in0=tm.t[:], in1=xt.t[:, sub, cs], op=ALU.add), [tm, xsub[sub]], [], [xsub[sub]])
                            dn_ev(sub)
                    dn_cg(cg)
                for sub in range(4):
                    def fin(sub):
                        hb = hbr.next()
                        rs = rms(sub, hb)
                        V(lambda e: e.scalar_tensor_tensor(out=xt.t[:, sub, :], in0=xt.t[:, sub, :], scalar=rs.t[:, 0:1], in1=cb[0].t[:], op0=ALU.mult, op1=ALU.mult),
                          [xsub[sub], rs, cb[0]], [xsub[sub]])
                        r0 = (tt * 4 + sub) * 128
                        self.dma("sync", self.out[r0:r0 + 128, :], xt.t[:, sub, :], osem[sub], reads=[xsub[sub]], appends=[self.out_buf])
                    fin(sub)

            for tt in range(NTT):
                do_tile(tt)


def host_inputs(inp, b):
    f = lambda a: np.ascontiguousarray(a, dtype=np.float32)
    m = {}
    m["x"] = f(inp["x"][b])
    m["cT"] = f(inp["c"][b].reshape(KC, 128).T)
    m["pos"] = np.ascontiguousarray(inp["positions"][b].reshape(1, S).astype(np.int32))
    m["ada_w"] = f(inp["ada_w"][0])
    m["ada_b"] = f(inp["ada_b"][0].reshape(1, -1))
    m["norm1_w"] = f(inp["norm1_w"][0].reshape(1, -1))
    w_in = inp["w_in"][0]
    m["w_in"] = f(w_in[:, :5120])
    kr = w_in[:, 5120:5184]
    z = np.zeros((D, 32), np.float32)
    m["w_kr"] = f(np.concatenate([kr[:, 0:32], z, kr[:, 32:64], z, z, kr[:, 0:32], z, kr[:, 32:64]], axis=1))
    m["ret_decay"] = f(inp["ret_decay"][0].reshape(1, 16))
    m["gnw"] = f(inp["ret_gn_w"][0].reshape(8, 128).T)
    m["gnb"] = f(inp["ret_gn_b"][0].reshape(8, 128).T)
    m["qnw"] = f(inp["mla_q_norm_w"][0].reshape(4, 128).T)
    m["kvnw"] = f(inp["mla_kv_norm_w"][0].reshape(4, 128).T)
    w_uq = inp["w_uq"][0].reshape(512, 8, 192)
    m["w_uqn"] = f(w_uq[:, :, :128].reshape(512, 1024))
    rp = w_uq[:, :, 128:].reshape(512, 4, 2, 2, 32)
    m["w_uqr"] = f(rp.transpose(0, 1, 3, 2, 4).reshape(512, 512))
    w_ukv = inp["w_ukv"][0].reshape(512, 8, 256)
    m["w_uk"] = f(w_ukv[:, :, :128].reshape(512, 1024))
    m["w_uv"] = f(w_ukv[:, :, 128:].reshape(512, 1024))
    m["mow"] = f(inp["mla_out_w"][0].reshape(8, 128).T)
    m["w_o"] = f(inp["w_o"][0])
    m["norm2_w"] = f(inp["norm2_w"][0].reshape(1, -1))
    m["w_gate"] = f(inp["w_gate"][0])
    m["w_up"] = f(inp["w_up"][0])
    m["w_down"] = f(inp["w_down"][0])
    m["fnw"] = f(inp["final_norm_w"].reshape(1, -1))
    fr = (np.float32(10000.0) ** (-np.arange(0, 128, 2, dtype=np.float32) / np.float32(128))).astype(np.float32)
    fm = (np.float32(10000.0) ** (-np.arange(0, 64, 2, dtype=np.float32) / np.float32(64))).astype(np.float32)
    invf = np.zeros((128, 2), np.float32)
    invf[:, 0] = fr[np.arange(128) % 64]
    invf[:, 1] = fm[np.arange(128) % 32]
    m["invf"] = invf
    return m


_CACHE = {}


def kernel(**inputs):
    inp = {k: np.asarray(v) for k, v in inputs.items()}
    if "nc" not in _CACHE:
        _CACHE["nc"] = Builder(debug=False, upto="C").build()
    nc = _CACHE["nc"]
    shared = None
    in_maps = []
    for b in range(8):
        m = host_inputs(inp, b)
        if shared is None:
            shared = m
        else:
            for k in m:
                if k not in ("x", "cT", "pos"):
                    m[k] = shared[k]
        in_maps.append(m)
    res = run_bass_kernel_spmd(nc, in_maps, core_ids=list(range(8)))
    out = np.stack([np.asarray(res.results[b]["out"], dtype=np.float32) for b in range(8)], axis=0)
    return out
```

```python
import math
from contextlib import ExitStack

import numpy as np
import ml_dtypes

import concourse.bass as bass
import concourse.mybir as mybir
from concourse.bass_utils import run_bass_kernel_spmd

F32 = mybir.dt.float32
BF16 = mybir.dt.bfloat16
I32 = mybir.dt.int32
ALU = mybir.AluOpType
AF = mybir.ActivationFunctionType

S = 4096
D = 2048
NT = S // 128
NTT = S // 512
KC = D // 128
DFF = 5632
FC = DFF // 128
EPS = 1e-6
ENGS = ["sync", "gpsimd", "scalar", "vector", "tensor"]

TWO_PI = 2.0 * math.pi
CW1 = 6.28125
CW2 = TWO_PI - CW1
PI_LO = 3.14159


class DSem:
    __slots__ = ("h", "n")

    def __init__(self, h):
        self.h = h
        self.n = 0


class Buf:
    __slots__ = ("t", "w", "r", "pr", "ds")

    def __init__(self, t, ds=None):
        self.t = t
        self.w = {}
        self.r = {}
        self.pr = {}
        self.ds = ds


def _merge(dst, tok):
    s, v = tok
    k = id(s)
    if k not in dst or dst[k][1] < v:
        dst[k] = (s, v)


class Prog:
    def __init__(self, nc, stack):
        self.nc = nc
        self.stack = stack
        self.ops = {e: [] for e in ENGS}
        self.cnt = {e: 0 for e in ENGS}
        self.esem = {}
        for e in ["scalar", "vector", "tensor", "gpsimd"]:
            self.esem[e] = stack.enter_context(nc.semaphore("es_" + e))
        self.waited = {}
        self.nsem = 0
        self.nops = 0
        self.dsems = []

    def barrier_tokens(self):
        d = {}
        for e, s in self.esem.items():
            if self.cnt[e] > 0:
                d[id(s)] = (s, self.cnt[e])
        for ds in self.dsems:
            if ds.n > 0:
                d[id(ds.h)] = (ds.h, ds.n)
        return d

    def dsem(self):
        self.nsem += 1
        ds = DSem(self.stack.enter_context(self.nc.semaphore("ds%d" % self.nsem)))
        self.dsems.append(ds)
        return ds

    def op(self, eng, fn, reads=(), writes=(), appends=(), dsem=None):
        deps = {}
        for b in reads:
            for tok in b.w.values():
                _merge(deps, tok)
        for b in writes:
            if b.r:
                for tok in b.r.values():
                    _merge(deps, tok)
            else:
                for tok in b.w.values():
                    _merge(deps, tok)
        for b in appends:
            for tok in b.r.values():
                _merge(deps, tok)
            for tok in b.pr.values():
                _merge(deps, tok)
        waits = []
        for (s, v) in deps.values():
            key = (eng, id(s))
            if self.waited.get(key, 0) < v:
                self.waited[key] = v
                waits.append((s, v))
        if dsem is not None:
            dsem.n += 16
            tok = (dsem.h, dsem.n)
            inc = (dsem.h, 16)
        else:
            s = self.esem[eng]
            self.cnt[eng] += 1
            tok = (s, self.cnt[eng])
            inc = (s, 1)
        self.ops[eng].append((waits, fn, inc))
        self.nops += 1
        for b in reads:
            _merge(b.r, tok)
        for b in writes:
            b.w = {id(tok[0]): tok}
            b.pr = b.r
            b.r = {}
        for b in appends:
            _merge(b.w, tok)
        return tok

    def emit(self, final_bufs):
        nc = self.nc
        fin = {}
        for b in final_bufs:
            for tok in b.w.values():
                _merge(fin, tok)
            for tok in b.r.values():
                _merge(fin, tok)

        with nc.Block() as block:
            def mk(ename):
                def body(e):
                    for (waits, fn, inc) in self.ops[ename]:
                        for (s, v) in waits:
                            e.wait_ge(s, v)
                        ins = fn(e)
                        ins.then_inc(inc[0], inc[1])
                    if ename == "sync":
                        for (s, v) in fin.values():
                            e.wait_ge(s, v)
                return body
            block.sync(mk("sync"))
            block.gpsimd(mk("gpsimd"))
            block.scalar(mk("scalar"))
            block.vector(mk("vector"))
            block.tensor(mk("tensor"))


class Ring:
    def __init__(self, bufs):
        self.bufs = bufs
        self.i = 0

    def next(self):
        b = self.bufs[self.i % len(self.bufs)]
        self.i += 1
        return b


class Builder:
    def __init__(self, debug=False, upto="C"):
        self.debug = debug
        self.upto = upto
        self.nc = bass.Bass("TRN2", target_bir_lowering=False)
        self.inputs = {}
        self.outs = {}

    def din(self, name, shape, dt=F32):
        t = self.nc.dram_tensor(name, list(shape), dt, kind="ExternalInput")
        self.inputs[name] = t
        return t.ap()

    def dscr(self, name, shape, dt):
        kind = "ExternalOutput" if self.debug else "Internal"
        t = self.nc.dram_tensor(name, list(shape), dt, kind=kind)
        if self.debug:
            self.outs[name] = t
        return Buf(t.ap())

    def tile(self, st, name, shape, dt, dma=False):
        t = st.enter_context(self.nc.sbuf_tensor(name, list(shape), dt))
        ds = None
        if dma:
            pool = self.free_ds_sw if dma == "sw" else self.free_ds
            ds = pool.pop() if pool else self.P.dsem()
            st.callback(pool.append, ds)
        b = Buf(t, ds)
        b.r = dict(self.barrier)
        return b

    def ptile(self, st, name, shape, dt):
        t = st.enter_context(self.nc.psum_tensor(name, list(shape), dt))
        b = Buf(t)
        b.r = dict(self.barrier)
        return b

    def fence(self):
        self.barrier = self.P.barrier_tokens()

    def dma(self, eng, out_ap, in_ap, sem_buf, reads=(), writes=(), appends=()):
        return self.P.op(eng, lambda e: e.dma_start(out=out_ap, in_=in_ap), reads=reads, writes=writes,
                         appends=appends, dsem=sem_buf.ds)

    def build(self):
        nc = self.nc
        with ExitStack() as top:
            self.P = Prog(nc, top)
            self.barrier = {}
            self.free_ds = []
            self.free_ds_sw = []
            self.declare()
            self.consts(top)
            self.stage0(top)
            if self.upto >= "A":
                self.stageA(top)
            if self.upto >= "B":
                self.stageB_mla(top)
            if self.upto >= "B2":
                self.stageB_ret(top)
            if self.upto >= "C":
                self.stageC(top)
            fin = [self.out_buf, self.wo_b, self.wg_b, self.wd_b] + ([] if not self.debug else list(self.dbg_bufs))
            self.P.emit(fin)
        return nc

    def declare(self):
        d = self.din
        self.x = d("x", [S, D])
        self.cT = d("cT", [128, KC])
        self.pos = d("pos", [1, S], I32)
        self.ada_w = d("ada_w", [D, 6 * D])
        self.ada_b = d("ada_b", [1, 6 * D])
        self.norm1_w = d("norm1_w", [1, D])
        self.w_in = d("w_in", [D, 5120])
        self.w_kr = d("w_kr", [D, 256])
        self.ret_decay = d("ret_decay", [1, 16])
        self.gnw = d("gnw", [128, 8])
        self.gnb = d("gnb", [128, 8])
        self.qnw = d("qnw", [128, 4])
        self.kvnw = d("kvnw", [128, 4])
        self.w_uqn = d("w_uqn", [512, 1024])
        self.w_uqr = d("w_uqr", [512, 512])
        self.w_uk = d("w_uk", [512, 1024])
        self.w_uv = d("w_uv", [512, 1024])
        self.mow = d("mow", [128, 8])
        self.w_o = d("w_o", [D, D])
        self.norm2_w = d("norm2_w", [1, D])
        self.w_gate = d("w_gate", [D, DFF])
        self.w_up = d("w_up", [D, DFF])
        self.w_down = d("w_down", [DFF, D])
        self.fnw = d("fnw", [1, D])
        self.invf = d("invf", [128, 2])
        out_t = nc_out = self.nc.dram_tensor("out", [S, D], F32, kind="ExternalOutput")
        self.outs["out"] = out_t
        self.out = out_t.ap()
        self.out_buf = Buf(self.out)
        s = self.dscr
        self.tabs = [s("tab%d" % i, [128, S], F32) for i in range(4)]
        self.qr_s = [s("qr_s%d" % h, [128, S], BF16) for h in range(8)]
        self.kr_s = [s("kr_s%d" % h, [128, S], BF16) for h in range(8)]
        self.vr_s = s("vr_s", [8, 128, NT, 128], BF16)
        self.sg_s = [s("sg_s%d" % h, [128, S], BF16) for h in range(8)]
        self.qn_s = [s("qn_s%d" % h, [128, S], BF16) for h in range(8)]
        self.qp_s = [s("qp_s%d" % h, [128, S], BF16) for h in range(4)]
        self.kn_s = [s("kn_s%d" % h, [128, S], BF16) for h in range(8)]
        self.vm_s = s("vm_s", [8, 128, NT, 128], BF16)
        self.kro_s = [s("kro_s%d" % h, [128, S], BF16) for h in range(2)]
        self.mix_s = s("mix_s", [D, S], BF16)
        self.rstdm_s = s("rstdm_s", [128, S], F32)
        self.cb_s = [s("cb_s%d" % i, [128, D], F32) for i in range(7)]
        self.wo_b = Buf(self.nc.dram_tensor("wo_t", [4, 128, KC, 512], BF16, kind="Internal").ap())
        self.wg_b = Buf(self.nc.dram_tensor("wgu_t", [FC // 2, 128, KC, 512], BF16, kind="Internal").ap())
        self.wd_b = Buf(self.nc.dram_tensor("wd_t", [4, 4, 128, 11, 512], BF16, kind="Internal").ap())
        self.dbg_bufs = list(self.tabs) + self.qr_s + self.kr_s + [self.vr_s] + self.sg_s + self.qn_s + \
            self.qp_s + self.kn_s + [self.vm_s] + self.kro_s + [self.mix_s, self.rstdm_s] + self.cb_s

    def consts(self, st):
        P = self.P
        T = lambda name, shape, dt, dma=False: self.tile(st, name, shape, dt, dma)
        self.ident = T("ident", [128, 128], BF16)
        self.ones32 = T("ones32", [128, 128], F32)
        self.onesb = T("onesb", [128, 128], BF16)
        self.epsT = T("epsT", [128, 1], F32)
        self.halfpi = T("halfpi", [128, 1], F32)
        with ExitStack() as tmp:
            ioi = self.tile(tmp, "ioi", [128, 128], I32)
            iof = self.tile(tmp, "iof", [128, 128], F32)
            P.op("gpsimd", lambda e: e.iota(ioi.t[:], pattern=[[1, 128]], base=0, channel_multiplier=-1), writes=[ioi])
            P.op("vector", lambda e: e.tensor_copy(out=iof.t[:], in_=ioi.t[:]), reads=[ioi], writes=[iof])
            P.op("vector", lambda e: e.tensor_single_scalar(out=self.ident.t[:], in_=iof.t[:], scalar=0.0, op=ALU.is_equal),
                 reads=[iof], writes=[self.ident])
            P.op("vector", lambda e: e.memset(self.ones32.t[:], 1.0), writes=[self.ones32])
            P.op("vector", lambda e: e.memset(self.onesb.t[:], 1.0), writes=[self.onesb])
            P.op("vector", lambda e: e.memset(self.epsT.t[:], EPS), writes=[self.epsT])
            P.op("vector", lambda e: e.memset(self.halfpi.t[:], math.pi / 2), writes=[self.halfpi])
        self.alias_guard = [self.ident]

    def stage0(self, top):
        P = self.P
        nc = self.nc
        self.fence()
        with ExitStack() as st:
            T = lambda name, shape, dt, dma=False: self.tile(st, name, shape, dt, dma)
            cT = T("cTs", [128, KC], F32, dma=True)
            cact = T("cact", [128, KC], F32)
            crep = T("crep", [128, KC, 128], BF16)
            adab = T("adab", [128, 6 * D], F32, dma=True)
            n1w = T("n1w", [128, D], F32, dma=True)
            n2w = T("n2w", [128, D], F32, dma=True)
            fnw = T("fnwb", [128, D], F32, dma=True)
            mod = [T("mod%d" % i, [128, D], F32, dma=True) for i in range(6)]
            wr = Ring([T("adw%d" % i, [128, KC, 512], BF16, dma="sw") for i in range(2)])
            pm = Ring([self.ptile(st, "pm%d" % i, [128, 512], F32) for i in range(2)])
            self.dma("sync", cT.t[:], self.cT, cT, writes=[cT])
            self.dma("sync", adab.t[:], self.ada_b[0].partition_broadcast(128), adab, writes=[adab])
            self.dma("sync", n1w.t[:], self.norm1_w[0].partition_broadcast(128), n1w, writes=[n1w])
            self.dma("sync", n2w.t[:], self.norm2_w[0].partition_broadcast(128), n2w, writes=[n2w])
            self.dma("sync", fnw.t[:], self.fnw[0].partition_broadcast(128), fnw, writes=[fnw])
            P.op("scalar", lambda e: e.activation(out=cact.t[:], in_=cT.t[:], func=AF.Silu), reads=[cT], writes=[cact])
            for kc in range(KC):
                P.op("vector", lambda e, kc=kc: e.tensor_copy(out=crep.t[:, kc, :], in_=cact.t[:, kc:kc + 1].to_broadcast([128, 128])),
                     reads=[cact], appends=[crep])
            adw_v = self.ada_w.rearrange("(kc p) n -> p kc n", p=128)
            QW = S // 4
            posi = T("posi", [128, QW], I32, dma=True)
            posf = T("posf", [128, QW], F32)
            invf = T("invf_sb", [128, 2], F32, dma=True)
            ang = T("ang", [128, QW], F32)
            kf = T("kf", [128, QW], F32)
            ki = T("ki", [128, QW], I32)
            r = T("r", [128, QW], F32)
            msk = T("msk", [128, QW], F32)
            res = [T("res%d" % i, [128, QW], F32, dma=True) for i in range(2)]
            self.dma("sync", invf.t[:], self.invf, invf, writes=[invf])
            V = lambda fn, reads, writes: P.op("vector", fn, reads=reads, writes=writes)

            def tab_piece(j, qi):
                cs = slice(qi * QW, (qi + 1) * QW)
                self.dma("sync", posi.t[:], self.pos[0, cs].partition_broadcast(128), posi, writes=[posi])
                V(lambda e: e.tensor_copy(out=posf.t[:], in_=posi.t[:]), [posi], [posf])
                V(lambda e: e.tensor_scalar_mul(out=ang.t[:], in0=posf.t[:], scalar1=invf.t[:, j:j + 1]), [posf, invf], [ang])
                V(lambda e: e.tensor_scalar(out=kf.t[:], in0=ang.t[:], scalar1=1.0 / TWO_PI, scalar2=0.5, op0=ALU.mult, op1=ALU.add), [ang], [kf])
                V(lambda e: e.tensor_copy(out=ki.t[:], in_=kf.t[:]), [kf], [ki])
                V(lambda e: e.tensor_copy(out=kf.t[:], in_=ki.t[:]), [ki], [kf])
                V(lambda e: e.scalar_tensor_tensor(out=r.t[:], in0=kf.t[:], scalar=-CW1, in1=ang.t[:], op0=ALU.mult, op1=ALU.add), [kf, ang], [r])
                V(lambda e: e.scalar_tensor_tensor(out=r.t[:], in0=kf.t[:], scalar=-CW2, in1=r.t[:], op0=ALU.mult, op1=ALU.add), [kf, r], [r])
                V(lambda e: e.tensor_single_scalar(out=msk.t[:], in_=r.t[:], scalar=-math.pi, op=ALU.is_lt), [r], [msk])
                V(lambda e: e.scalar_tensor_tensor(out=r.t[:], in0=msk.t[:], scalar=TWO_PI, in1=r.t[:], op0=ALU.mult, op1=ALU.add), [msk, r], [r])
                V(lambda e: e.tensor_single_scalar(out=msk.t[:], in_=r.t[:], scalar=math.pi, op=ALU.is_gt), [r], [msk])
                V(lambda e: e.scalar_tensor_tensor(out=r.t[:], in0=msk.t[:], scalar=-TWO_PI, in1=r.t[:], op0=ALU.mult, op1=ALU.add), [msk, r], [r])
                V(lambda e: e.tensor_scalar(out=r.t[:], in0=r.t[:], scalar1=-PI_LO, scalar2=PI_LO, op0=ALU.max, op1=ALU.min), [r], [r])
                P.op("scalar", lambda e: e.activation(out=res[1].t[:], in_=r.t[:], func=AF.Sin), reads=[r], writes=[res[1]])
                self.dma("sync", self.tabs[2 * j + 1].t[:, cs], res[1].t[:], res[1], reads=[res[1]], appends=[self.tabs[2 * j + 1]])
                V(lambda e: e.scalar_tensor_tensor(out=msk.t[:], in0=r.t[:], scalar=-1.0, in1=r.t[:], op0=ALU.mult, op1=ALU.max), [r], [msk])
                P.op("scalar", lambda e: e.activation(out=res[0].t[:], in_=msk.t[:], func=AF.Sin, scale=-1.0, bias=self.halfpi.t[:]),
                     reads=[msk, self.halfpi], writes=[res[0]])
                self.dma("sync", self.tabs[2 * j].t[:, cs], res[0].t[:], res[0], reads=[res[0]], appends=[self.tabs[2 * j]])

            tab_jobs = [(j, qi) for j in range(2) for qi in range(4)]
            for g in range(24):
                def ada_g(g):
                    wb = wr.next()
                    self.dma("gpsimd", wb.t[:], adw_v[:, :, g * 512:(g + 1) * 512], wb, writes=[wb])
                    pb = pm.next()

                    def mm(e):
                        ins = None
                        for kc in range(KC):
                            ins = e.matmul(pb.t[:], lhsT=crep.t[:, kc, :], rhs=wb.t[:, kc, :], start=(kc == 0), stop=(kc == KC - 1))
                        return ins
                    P.op("tensor", mm, reads=[crep, wb], writes=[pb])
                    m = mod[g // 4]
                    c0 = (g % 4) * 512
                    P.op("vector", lambda e: e.tensor_tensor(out=m.t[:, c0:c0 + 512], in0=pb.t[:], in1=adab.t[:, g * 512:(g + 1) * 512], op=ALU.add),
                         reads=[pb, adab], appends=[m])
                ada_g(g)
                if g % 3 == 1 and tab_jobs:
                    tab_piece(*tab_jobs.pop(0))
            while tab_jobs:
                tab_piece(*tab_jobs.pop(0))
            shift_a, scale_a, gate_a, shift_f, scale_f, gate_f = mod
            P.op("vector", lambda e: e.scalar_tensor_tensor(out=scale_a.t[:], in0=scale_a.t[:], scalar=1.0, in1=n1w.t[:],
                                                            op0=ALU.add, op1=ALU.mult), reads=[scale_a, n1w], writes=[scale_a])
            P.op("vector", lambda e: e.scalar_tensor_tensor(out=scale_f.t[:], in0=scale_f.t[:], scalar=1.0, in1=n2w.t[:],
                                                            op0=ALU.add, op1=ALU.mult), reads=[scale_f, n2w], writes=[scale_f])
            for i, src in enumerate([gate_a, scale_f, shift_f, gate_f, fnw, scale_a, shift_a]):
                self.dma("sync", self.cb_s[i].t, src.t[:], src, reads=[src], writes=[self.cb_s[i]])

    def rope_fm(self, pb, cos_t, sin_t, scale, tmpA, tmpB, ob):
        P = self.P
        P.op("vector", lambda e: e.scalar_tensor_tensor(out=tmpA.t[:], in0=pb.t[:], scalar=scale, in1=cos_t.t[:], op0=ALU.mult, op1=ALU.mult),
             reads=[pb, cos_t], writes=[tmpA])
        P.op("vector", lambda e: e.scalar_tensor_tensor(out=tmpB.t[0:64, :], in0=pb.t[64:128, :], scalar=scale, in1=sin_t.t[64:128, :],
                                                        op0=ALU.mult, op1=ALU.mult), reads=[pb, sin_t], writes=[tmpB])
        P.op("vector", lambda e: e.scalar_tensor_tensor(out=tmpB.t[64:128, :], in0=pb.t[0:64, :], scalar=scale, in1=sin_t.t[0:64, :],
                                                        op0=ALU.mult, op1=ALU.mult), reads=[pb, sin_t], appends=[tmpB])
        P.op("gpsimd", lambda e: e.tensor_tensor(out=ob.t[0:64, :], in0=tmpA.t[0:64, :], in1=tmpB.t[0:64, :], op=ALU.subtract),
             reads=[tmpA, tmpB], writes=[ob])
        P.op("gpsimd", lambda e: e.tensor_tensor(out=ob.t[64:128, :], in0=tmpA.t[64:128, :], in1=tmpB.t[64:128, :], op=ALU.add),
             reads=[tmpA, tmpB], appends=[ob])

    def stageA(self, top):
        P = self.P
        self.fence()
        with ExitStack() as st:
            T = lambda name, shape, dt, dma=False: self.tile(st, name, shape, dt, dma)
            hT = st.enter_context(self.nc.sbuf_tensor("hT", [128, KC, S], BF16))
            hTb = [Buf(hT) for _ in range(NTT)]
            for b_ in hTb:
                b_.r = dict(self.barrier)
            with ExitStack() as s1:
                T1 = lambda name, shape, dt, dma=False: self.tile(s1, name, shape, dt, dma)
                self.Aabc = T1("Aabc", [128, D], F32, dma=True)
                self.Babc = T1("Babc", [128, D], F32, dma=True)
                self.dma("sync", self.Aabc.t[:], self.cb_s[5].t, self.Aabc, reads=[self.cb_s[5]], writes=[self.Aabc])
                self.dma("sync", self.Babc.t[:], self.cb_s[6].t, self.Babc, reads=[self.cb_s[6]], writes=[self.Babc])
                xr = Ring([T1("xt%d" % i, [128, D], F32, dma=True) for i in range(4)])
                junk = T1("junk", [128, D], BF16)
                msr = Ring([T1("ms%d" % i, [128, 1], F32) for i in range(4)])
                sdr = Ring([T1("sd%d" % i, [128, 1], F32) for i in range(4)])
                rsr = Ring([T1("rs%d" % i, [128, 1], F32) for i in range(4)])
                hbr = Ring([T1("hb%d" % i, [128, D], BF16) for i in range(3)])
                ptr = Ring([self.ptile(s1, "ptA%d" % i, [128, 1024], BF16) for i in range(4)])
                pending = []

                def flush():
                    for (eng, dst, src, pt, tb) in pending:
                        if eng == "scalar":
                            P.op(eng, lambda e, dst=dst, src=src: e.activation(out=dst, in_=src, func=AF.Copy), reads=[pt], appends=[tb])
                        else:
                            P.op(eng, lambda e, dst=dst, src=src: e.tensor_copy(out=dst, in_=src), reads=[pt], appends=[tb])
                    del pending[:]

                for i in range(NT):
                    def a1(i):
                        xt = xr.next(); ms = msr.next(); sd = sdr.next(); rs = rsr.next(); hb = hbr.next()
                        self.dma("sync", xt.t[:], self.x[i * 128:(i + 1) * 128, :], xt, writes=[xt])
                        P.op("scalar", lambda e: e.activation(out=junk.t[:], in_=xt.t[:], func=AF.Square, scale=1.0 / math.sqrt(D),
                                                              accum_out=ms.t[:]), reads=[xt], writes=[ms])
                        P.op("scalar", lambda e: e.activation(out=sd.t[:], in_=ms.t[:], func=AF.Sqrt, bias=self.epsT.t[:], scale=1.0),
                             reads=[ms, self.epsT], writes=[sd])
                        P.op("vector", lambda e: e.reciprocal(out=rs.t[:], in_=sd.t[:]), reads=[sd], writes=[rs])
                        P.op("vector", lambda e: e.scalar_tensor_tensor(out=xt.t[:], in0=xt.t[:], scalar=rs.t[:, 0:1], in1=self.Aabc.t[:],
                                                                        op0=ALU.mult, op1=ALU.mult), reads=[xt, rs, self.Aabc], writes=[xt])
                        H = D // 4
                        P.op("gpsimd", lambda e: e.tensor_tensor(out=hb.t[:, 0:H], in0=xt.t[:, 0:H], in1=self.Babc.t[:, 0:H], op=ALU.add),
                             reads=[xt, self.Babc], writes=[hb])
                        P.op("vector", lambda e: e.tensor_tensor(out=hb.t[:, H:D], in0=xt.t[:, H:D], in1=self.Babc.t[:, H:D], op=ALU.add),
                             reads=[xt, self.Babc], appends=[hb])
                        flush()
                        for half in range(2):
                            pt = ptr.next()

                            def tr(e, pt=pt, half=half):
                                ins = None
                                for j in range(8):
                                    kc = half * 8 + j
                                    ins = e.transpose(out=pt.t[:, j * 128:(j + 1) * 128], in_=hb.t[:, kc * 128:(kc + 1) * 128], identity=self.ident.t[:])
                                return ins
                            P.op("tensor", tr, reads=[hb, self.ident], writes=[pt])
                            eng = "scalar" if half == 0 else "vector"
                            dst = hT[:, half * 8:(half + 1) * 8, i * 128:(i + 1) * 128]
                            src = pt.t[:].rearrange("p (k t) -> p k t", k=8)
                            pending.append((eng, dst, src, pt, hTb[i // 4]))
                    a1(i)
                flush()
            self.fence()
            with ExitStack() as s2:
                T2 = lambda name, shape, dt, dma=False: self.tile(s2, name, shape, dt, dma)
                wr = Ring([T2("wA%d" % i, [128, KC, 256], BF16, dma="sw") for i in range(2)])
                cosr = Ring([T2("cosA%d" % i, [128, 512], F32, dma=True) for i in range(2)])
                sinr = Ring([T2("sinA%d" % i, [128, 512], F32, dma=True) for i in range(2)])
                tmpAr = Ring([T2("tmpA%d" % i, [128, 512], F32) for i in range(2)])
                tmpBr = Ring([T2("tmpB%d" % i, [128, 512], F32) for i in range(2)])
                obr = Ring([T2("obA%d" % i, [128, 512], BF16, dma=True) for i in range(4)])
                sq = T2("sqA", [128, 4, 512], BF16)
                lat32 = T2("lat32", [128, 4, 512], F32)
                sdA = T2("sdA", [128, 512], F32)
                rsA = T2("rsA", [128, 512], F32)
                latR = Ring([T2("lat%d" % i, [128, 4, 512], BF16) for i in range(2)])
                wup = T2("wup", [128, 4, 2048], BF16, dma="sw")
                nwq = T2("nwq", [128, 4], F32, dma=True)
                nwkv = T2("nwkv", [128, 4], F32, dma=True)
                psA = Ring([self.ptile(s2, "psA%d" % i, [128, 512], F32) for i in range(4)])
                pss = self.ptile(s2, "pssA", [128, 512], F32)
                pu = Ring([self.ptile(s2, "puA%d" % i, [128, 512], F32) for i in range(3)])
                self.dma("sync", nwq.t[:], self.qnw, nwq, writes=[nwq])
                self.dma("sync", nwkv.t[:], self.kvnw, nwkv, writes=[nwkv])
                win_v = self.w_in.rearrange("(kc p) n -> p kc n", p=128)
                wkr_v = self.w_kr.rearrange("(kc p) n -> p kc n", p=128)
                groups = []
                for i in range(4):
                    groups.append(("cq" if i < 2 else "ckv", i % 2, win_v[:, :, 4096 + i * 256: 4096 + (i + 1) * 256]))
                groups.append(("kro", 0, wkr_v))
                for i in range(4):
                    groups.append(("q", i, win_v[:, :, i * 256:(i + 1) * 256]))
                for i in range(4):
                    groups.append(("k", i, win_v[:, :, 1024 + i * 256: 1024 + (i + 1) * 256]))
                for i in range(4):
                    groups.append(("v", i, win_v[:, :, 2048 + i * 256: 2048 + (i + 1) * 256]))
                for i in range(4):
                    groups.append(("g", i, win_v[:, :, 3072 + i * 256: 3072 + (i + 1) * 256]))
                wbufs = {}

                def load_w(gi):
                    if gi < len(groups) and gi not in wbufs:
                        wb = wr.next()
                        self.dma("gpsimd", wb.t[:], groups[gi][2], wb, writes=[wb])
                        wbufs[gi] = wb

                def proj_fm(pb, wb, j, tt):
                    def mm(e):
                        ins = None
                        for kc in range(KC):
                            ins = e.matmul(pb.t[:], lhsT=wb.t[:, kc, j * 128:(j + 1) * 128], rhs=hT[:, kc, tt * 512:(tt + 1) * 512],
                                           start=(kc == 0), stop=(kc == KC - 1))
                        return ins
                    P.op("tensor", mm, reads=[wb, hTb[tt]], writes=[pb])

                def store(dst_buf, dst_ap, ob):
                    self.dma("sync", dst_ap, ob.t[:], ob, reads=[ob], appends=[dst_buf])

                def load_tab(ci, tt):
                    ct = cosr.next(); stt = sinr.next()
                    self.dma("sync", ct.t[:], self.tabs[ci].t[:, tt * 512:(tt + 1) * 512], ct, reads=[self.tabs[ci]], writes=[ct])
                    self.dma("sync", stt.t[:], self.tabs[ci + 1].t[:, tt * 512:(tt + 1) * 512], stt, reads=[self.tabs[ci + 1]], writes=[stt])
                    return ct, stt

                gi = 0
                while gi < len(groups):
                    kind, idx, _ = groups[gi]
                    load_w(gi)
                    load_w(gi + 1)
                    if kind in ("cq", "ckv"):
                        wb0, wb1 = wbufs[gi], wbufs[gi + 1]
                        if kind == "cq":
                            self.dma("gpsimd", wup.t[:, :, 0:1024], self.w_uqn.rearrange("(kc p) n -> p kc n", p=128), wup, writes=[wup])
                            self.dma("gpsimd", wup.t[:, :, 1024:1536], self.w_uqr.rearrange("(kc p) n -> p kc n", p=128), wup, appends=[wup])
                            nw = nwq
                        else:
                            self.dma("gpsimd", wup.t[:, :, 0:1024], self.w_uk.rearrange("(kc p) n -> p kc n", p=128), wup, writes=[wup])
                            self.dma("gpsimd", wup.t[:, :, 1024:2048], self.w_uv.rearrange("(kc p) n -> p kc n", p=128), wup, appends=[wup])
                            nw = nwkv
                        def norm_part(tt, kind=kind, nw=nw, wb0=wb0, wb1=wb1):
                            pbs = [psA.next() for _ in range(4)]
                            for fc in range(4):
                                proj_fm(pbs[fc], wb0 if fc < 2 else wb1, fc % 2, tt)
                            for fc in range(4):
                                def ev(fc):
                                    pb = pbs[fc]
                                    P.op("scalar", lambda e: e.activation(out=lat32.t[:, fc, :], in_=pb.t[:], func=AF.Copy),
                                         reads=[pb], writes=[lat32] if fc == 0 else (), appends=[lat32] if fc > 0 else ())
                                    P.op("scalar", lambda e: e.activation(out=sq.t[:, fc, :], in_=pb.t[:], func=AF.Square),
                                         reads=[pb], writes=[sq] if fc == 0 else (), appends=[sq] if fc > 0 else ())
                                ev(fc)
                            return pbs

                        def norm_finish(tt, nw=nw):
                            lt = latR.next()

                            def ones_mm(e):
                                ins = None
                                for fc in range(4):
                                    ins = e.matmul(pss.t[:], lhsT=self.onesb.t[:], rhs=sq.t[:, fc, :], start=(fc == 0), stop=(fc == 3))
                                return ins
                            P.op("tensor", ones_mm, reads=[sq, self.onesb], writes=[pss])
                            P.op("scalar", lambda e: e.activation(out=sdA.t[:], in_=pss.t[:], func=AF.Ln, bias=self.epsT.t[:], scale=1.0 / 512.0),
                                 reads=[pss, self.epsT], writes=[sdA])
                            P.op("scalar", lambda e: e.activation(out=rsA.t[:], in_=sdA.t[:], func=AF.Exp, scale=-0.5), reads=[sdA], writes=[rsA])
                            for fc in range(4):
                                def nm(fc):
                                    P.op("vector", lambda e: e.scalar_tensor_tensor(
                                        out=lt.t[:, fc, :], in0=lat32.t[:, fc, :], scalar=nw.t[:, fc:fc + 1], in1=rsA.t[:], op0=ALU.mult, op1=ALU.mult),
                                        reads=[lat32, nw, rsA], writes=[lt] if fc == 0 else (), appends=[lt] if fc > 0 else ())
                                nm(fc)
                            return lt

                        def up_part(tt, lt, tabs_, kind=kind):
                            def up_fm(pb, c0):
                                def mm(e):
                                    ins = None
                                    for kc in range(4):
                                        ins = e.matmul(pb.t[:], lhsT=wup.t[:, kc, c0:c0 + 128], rhs=lt.t[:, kc, :], start=(kc == 0), stop=(kc == 3))
                                    return ins
                                P.op("tensor", mm, reads=[wup, lt], writes=[pb])

                            def plain(h, dst_list):
                                pb = pu.next(); ob = obr.next()
                                up_fm(pb, h * 128)
                                P.op("scalar", lambda e: e.activation(out=ob.t[:], in_=pb.t[:], func=AF.Copy), reads=[pb], writes=[ob])
                                store(dst_list[h], dst_list[h].t[:, tt * 512:(tt + 1) * 512], ob)
                            if kind == "cq":
                                ct, stt = tabs_
                                for h in range(8):
                                    plain(h, self.qn_s)
                                for pr in range(4):
                                    pb = pu.next(); ob = obr.next()
                                    up_fm(pb, 1024 + pr * 128)
                                    self.rope_fm(pb, ct, stt, 1.0, tmpAr.next(), tmpBr.next(), ob)
                                    store(self.qp_s[pr], self.qp_s[pr].t[:, tt * 512:(tt + 1) * 512], ob)
                            else:
                                for h in range(8):
                                    plain(h, self.kn_s)
                                for sub in range(4):
                                    for half in range(2):
                                        def vpart(sub, half):
                                            pb = pu.next(); ob = obr.next()

                                            def mmv(e):
                                                ins = None
                                                for kc in range(4):
                                                    ins = e.matmul(pb.t[:], lhsT=lt.t[:, kc, sub * 128:(sub + 1) * 128],
                                                                   rhs=wup.t[:, kc, 1024 + half * 512: 1024 + (half + 1) * 512], start=(kc == 0), stop=(kc == 3))
                                                return ins
                                            P.op("tensor", mmv, reads=[wup, lt], writes=[pb])
                                            P.op("vector", lambda e: e.tensor_copy(out=ob.t[:], in_=pb.t[:]), reads=[pb], writes=[ob])
                                            c = tt * 4 + sub
                                            dst = self.vm_s.t[half * 4:(half + 1) * 4, :, c, :].rearrange("h p e -> p h e")
                                            self.dma("sync", dst, ob.t[:].rearrange("p (h e) -> p h e", h=4), ob, reads=[ob], appends=[self.vm_s])
                                        vpart(sub, half)

                        prev = None
                        for tt in range(NTT):
                            tabs_ = load_tab(2, tt) if kind == "cq" else None
                            norm_part(tt)
                            if prev is not None:
                                up_part(*prev)
                            lt = norm_finish(tt)
                            prev = (tt, lt, tabs_)
                        up_part(*prev)
                        gi += 2
                        continue
                    wb = wbufs[gi]
                    if kind in ("q", "k", "kro"):
                        tabi = 0 if kind != "kro" else 2
                        nxt_tab = load_tab(tabi, 0)
                        for tt in range(NTT):
                            ct, stt = nxt_tab
                            if tt + 1 < NTT:
                                nxt_tab = load_tab(tabi, tt + 1)
                            for j in range(2):
                                pb = psA.next(); ob = obr.next()
                                proj_fm(pb, wb, j, tt)
                                self.rope_fm(pb, ct, stt, (128 ** -0.5) if kind == "k" else 1.0, tmpAr.next(), tmpBr.next(), ob)
                                if kind == "kro":
                                    dstb = self.kro_s[j]
                                else:
                                    dstb = (self.qr_s if kind == "q" else self.kr_s)[idx * 2 + j]
                                store(dstb, dstb.t[:, tt * 512:(tt + 1) * 512], ob)
                    elif kind == "g":
                        for tt in range(NTT):
                            for j in range(2):
                                pb = psA.next(); ob = obr.next()
                                proj_fm(pb, wb, j, tt)
                                P.op("scalar", lambda e, pb=pb, ob=ob: e.activation(out=ob.t[:], in_=pb.t[:], func=AF.Silu), reads=[pb], writes=[ob])
                                dstb = self.sg_s[idx * 2 + j]
                                store(dstb, dstb.t[:, tt * 512:(tt + 1) * 512], ob)
                    elif kind == "v":
                        for i in range(NT):
                            pb = psA.next(); ob = obr.next()

                            def mmv(e, pb=pb, i=i, wb=wb):
                                ins = None
                                for kc in range(KC):
                                    ins = e.matmul(pb.t[:, 0:256], lhsT=hT[:, kc, i * 128:(i + 1) * 128], rhs=wb.t[:, kc, :], start=(kc == 0), stop=(kc == KC - 1))
                                return ins
                            P.op("tensor", mmv, reads=[wb, hTb[i // 4]], writes=[pb])
                            eng = "scalar" if i % 2 == 0 else "vector"
                            if eng == "scalar":
                                P.op(eng, lambda e, pb=pb, ob=ob: e.activation(out=ob.t[:, 0:256], in_=pb.t[:, 0:256], func=AF.Copy), reads=[pb], writes=[ob])
                            else:
                                P.op(eng, lambda e, pb=pb, ob=ob: e.tensor_copy(out=ob.t[:, 0:256], in_=pb.t[:, 0:256]), reads=[pb], writes=[ob])
                            dst = self.vr_s.t[idx * 2:(idx + 1) * 2, :, i, :].rearrange("h p e -> p h e")
                            self.dma("sync", dst, ob.t[:, 0:256].rearrange("p (h e) -> p h e", h=2), ob, reads=[ob], appends=[self.vr_s])
                    gi += 1

    def stageB_mla(self, top):
        P = self.P
        SC = 192 ** -0.5
        self.fence()
        with ExitStack() as st:
            T = lambda name, shape, dt, dma=False: self.tile(st, name, shape, dt, dma)
            Qn = Ring([T("Qn%d" % i, [128, S], BF16, dma=True) for i in range(2)])
            Qp = Ring([T("Qp%d" % i, [128, S], BF16, dma=True) for i in range(2)])
            Kn = Ring([T("Kn%d" % i, [128, S], BF16, dma=True) for i in range(2)])
            Vm = Ring([T("Vm%d" % i, [128, NT, 128], BF16, dma=True) for i in range(2)])
            Kr = [T("Kr%d" % i, [128, S], BF16, dma=True) for i in range(2)]
            ssq = T("ssqm", [128, S], F32, dma=True)
            ptr = Ring([T("ptm%d" % i, [128, 512], BF16) for i in range(6)])
            rec = T("recm", [128, 512], F32)
            p2R = Ring([T("p2m%d" % i, [128, 512], BF16) for i in range(4)])
            o32 = Ring([T("o32m%d" % i, [128, 512], F32) for i in range(2)])
            sqm = T("sqm", [128, 512], F32)
            obr = Ring([T("obm%d" % i, [128, 512], BF16, dma=True) for i in range(2)])
            psS = Ring([self.ptile(st, "psS%d" % i, [128, 512], F32) for i in range(4)])
            poR = Ring([self.ptile(st, "poM%d" % i, [128, 512], F32) for i in range(2)])
            pdR = Ring([self.ptile(st, "pdM%d" % i, [128, 512], F32) for i in range(2)])
            for i in range(2):
                self.dma("sync", Kr[i].t[:], self.kro_s[i].t, Kr[i], reads=[self.kro_s[i]], writes=[Kr[i]])
            stg = Ring([T("stgW%d" % i, [128, KC, 512], BF16, dma="sw") for i in range(2)])
            pieces = []
            sv = self.w_o.rearrange("(kc p) n -> p kc n", p=128)
            for cg in range(4):
                pieces.append((sv[:, :, cg * 512:(cg + 1) * 512], None, self.wo_b.t[cg], None, self.wo_b, KC))
            for (src, c_off) in ((self.w_gate, 0), (self.w_up, 256)):
                sv = src.rearrange("(kc p) n -> p kc n", p=128)
                for i in range(FC // 4):
                    dv = self.wg_b.t[2 * i:2 * i + 2, :, :, c_off:c_off + 256].rearrange("g p kc c -> p kc g c")
                    pieces.append((sv[:, :, i * 512:(i + 1) * 512], "split", dv, None, self.wg_b, KC))
            sv = self.w_down.rearrange("(fc p) n -> p fc n", p=128)
            for fq in range(4):
                for cg in range(4):
                    pieces.append((sv[:, fq * 11:(fq + 1) * 11, cg * 512:(cg + 1) * 512], None, self.wd_b.t[cg, fq], None, self.wd_b, 11))
            cast_state = {"i": 0, "pend": None}

            def cast_step():
                if cast_state["pend"] is not None:
                    (dst_ap, src_sb, sb_, dbuf) = cast_state["pend"]
                    self.dma("gpsimd", dst_ap, src_sb, sb_, reads=[sb_], appends=[dbuf])
                    cast_state["pend"] = None
                i = cast_state["i"]
                if i < len(pieces):
                    (sap, mode, dst_ap, _, dbuf, nk) = pieces[i]
                    sb_ = stg.next()
                    self.dma("gpsimd", sb_.t[:, 0:nk, :], sap, sb_, writes=[sb_])
                    src_sb = sb_.t[:, 0:nk, :]
                    if mode == "split":
                        src_sb = sb_.t[:, :, :].rearrange("p kc (g c) -> p kc g c", g=2)
                    cast_state["pend"] = (dst_ap, src_sb, sb_, dbuf)
                    cast_state["i"] = i + 1
            loaded = {}

            def load_head(h):
                if h >= 8 or h in loaded:
                    return
                qn = Qn.next(); qp = Qp.next(); kn = Kn.next(); vm = Vm.next()
                self.dma("sync", qn.t[:], self.qn_s[h].t, qn, reads=[self.qn_s[h]], writes=[qn])
                self.dma("sync", qp.t[:], self.qp_s[h // 2].t, qp, reads=[self.qp_s[h // 2]], writes=[qp])
                self.dma("sync", kn.t[:], self.kn_s[h].t, kn, reads=[self.kn_s[h]], writes=[kn])
                self.dma("sync", vm.t[:], self.vm_s.t[h], vm, reads=[self.vm_s], writes=[vm])
                loaded[h] = (qn, qp, kn, vm)

            tails = []
            load_head(0)
            for h in range(8):
                load_head(h + 1)
                qn, qp, kn, vm = loaded[h]
                kr = Kr[h % 2]
                for qt in range(NTT):
                  def do_qt(h, qt, qn, qp, kn, vm, kr):
                    po = poR.next(); pd = pdR.next()
                    qs = slice(qt * 512, (qt + 1) * 512)

                    def s_op(kt):
                        ps = psS.next()

                        def mm(e, ps=ps, kt=kt):
                            e.matmul(ps.t[:], lhsT=kn.t[:, kt * 128:(kt + 1) * 128], rhs=qn.t[:, qs], start=True, stop=False)
                            return e.matmul(ps.t[:], lhsT=kr.t[:, kt * 128:(kt + 1) * 128], rhs=qp.t[:, qs], start=False, stop=True)
                        P.op("tensor", mm, reads=[kn, qn, kr, qp], writes=[ps])
                        return ps
                    LA = 2
                    psq = [s_op(k_) for k_ in range(LA)]
                    prev_tail = tails.pop(0) if tails else None
                    cast_step()
                    pt_prev = None
                    p2_prev = None
                    for kt in range(NT):
                        ps = psq.pop(0)
                        if kt + LA < NT:
                            psq.append(s_op(kt + LA))
                        if prev_tail is not None and kt in prev_tail:
                            prev_tail[kt]()
                        pt = ptr.next()
                        P.op("scalar", lambda e, ps=ps, pt=pt: e.activation(out=pt.t[:], in_=ps.t[:], func=AF.Exp, scale=SC), reads=[ps], writes=[pt])

                        def pv(e, pt=pt, kt=kt, po=po):
                            return e.matmul(po.t[:], lhsT=vm.t[:, kt, :], rhs=pt.t[:], start=(kt == 0), stop=(kt == NT - 1))
                        if kt == 0:
                            P.op("tensor", pv, reads=[vm, pt], writes=[po])
                        else:
                            P.op("tensor", pv, reads=[vm, pt], appends=[po])
                        if kt % 2 == 0:
                            pt_prev = pt
                        else:
                            p2 = p2R.next()
                            P.op("vector", lambda e, p2=p2, a=pt_prev, b=pt: e.tensor_tensor(out=p2.t[:], in0=a.t[:], in1=b.t[:], op=ALU.add), reads=[pt_prev, pt], writes=[p2])

                            def dn(e, p2=p2, kt=kt, pd=pd):
                                return e.matmul(pd.t[:], lhsT=self.onesb.t[:], rhs=p2.t[:], start=(kt == 1), stop=(kt == NT - 1))
                            if kt == 1:
                                P.op("tensor", dn, reads=[p2, self.onesb], writes=[pd])
                            else:
                                P.op("tensor", dn, reads=[p2, self.onesb], appends=[pd])
                    o3 = o32.next(); ob = obr.next()

                    def t3r(c):
                        def f():
                            cc = slice(c * 128, (c + 1) * 128)
                            P.op("vector", lambda e: e.reciprocal(out=rec.t[:, cc], in_=pd.t[:, cc]), reads=[pd], writes=[rec] if c == 0 else [], appends=[] if c == 0 else [rec])
                        return f

                    def t3():
                        P.op("vector", lambda e: e.tensor_tensor(out=o3.t[:], in0=po.t[:], in1=rec.t[:], op=ALU.mult), reads=[po, rec], writes=[o3])
                        P.op("scalar", lambda e: e.activation(out=sqm.t[:], in_=o3.t[:], func=AF.Square), reads=[o3], writes=[sqm])
                        P.op("vector", lambda e: e.tensor_copy(out=ob.t[:], in_=o3.t[:]), reads=[o3], writes=[ob])
                        r0 = 1024 + h * 128
                        self.dma("sync", self.mix_s.t[r0:r0 + 128, qs], ob.t[:], ob, reads=[ob], appends=[self.mix_s])

                    def t45():
                        P.op("tensor", lambda e: e.matmul(pd.t[:], lhsT=self.ones32.t[:], rhs=sqm.t[:], start=True, stop=True), reads=[sqm, self.ones32], writes=[pd])
                        if h == 0:
                            P.op("vector", lambda e: e.tensor_copy(out=ssq.t[:, qs], in_=pd.t[:]), reads=[pd], appends=[ssq])
                        else:
                            P.op("vector", lambda e: e.tensor_tensor(out=ssq.t[:, qs], in0=ssq.t[:, qs], in1=pd.t[:], op=ALU.add), reads=[pd, ssq], appends=[ssq])
                    tails.append({2: t3r(0), 4: t3r(1), 6: t3r(2), 8: t3r(3), 10: t3, 16: t45})
                  do_qt(h, qt, qn, qp, kn, vm, kr)
            for tl in tails:
                for k_ in sorted(tl):
                    tl[k_]()
            while cast_state["pend"] is not None or cast_state["i"] < len(pieces):
                cast_step()
            P.op("scalar", lambda e: e.activation(out=ssq.t[:], in_=ssq.t[:], func=AF.Sqrt, bias=self.epsT.t[:], scale=1.0 / 1024.0),
                 reads=[ssq, self.epsT], writes=[ssq])
            P.op("vector", lambda e: e.reciprocal(out=ssq.t[:], in_=ssq.t[:]), reads=[ssq], writes=[ssq])
            self.dma("sync", self.rstdm_s.t, ssq.t[:], ssq, reads=[ssq], writes=[self.rstdm_s])

    def stageB_ret(self, top):
        P = self.P
        self.fence()
        with ExitStack() as st:
            T = lambda name, shape, dt, dma=False: self.tile(st, name, shape, dt, dma)
            V = lambda fn, reads, writes=(), appends=(): P.op("vector", fn, reads=reads, writes=writes, appends=appends)
            A = lambda fn, reads, writes=(), appends=(): P.op("scalar", fn, reads=reads, writes=writes, appends=appends)
            G = lambda fn, reads, writes=(), appends=(): P.op("gpsimd", fn, reads=reads, writes=writes, appends=appends)
            abc = T("abc", [128, 16], F32, dma=True)
            lg = T("lg", [128, 16], F32)
            gC = T("gC", [128, 16], F32)
            gnw = T("gnwS", [128, 8], F32, dma=True)
            gnb = T("gnbS", [128, 8], F32, dma=True)
            ii = T("iiR", [128, 128], I32)
            diff = T("diffR", [128, 128], F32)
            posp = T("pospR", [128, 128], F32)
            negp = T("negpR", [128, 128], F32)
            mF = T("mFR", [128, 128], F32)
            mB = T("mBR", [128, 128], F32)
            cp1 = T("cp1R", [128, 128], F32)
            r128 = T("r128R", [128, 128], F32)
            ci = T("ciR", [128, 2], I32)
            cf = T("cfR", [128, 2], F32)
            t1 = T("t1R", [128, 128], F32)
            t2 = T("t2R", [128, 128], F32)
            DT = T("DTR", [128, 8, 128], F32)
            xif = T("xifR", [128, 8, 128], F32)
            xib = T("xibR", [128, 8, 128], F32)
            zf = T("zfR", [128, 8], F32)
            zb = T("zbR", [128, 8], F32)
            self.dma("sync", abc.t[:], self.ret_decay[0].partition_broadcast(128), abc, writes=[abc])
            self.dma("sync", gnw.t[:], self.gnw, gnw, writes=[gnw])
            self.dma("sync", gnb.t[:], self.gnb, gnb, writes=[gnb])
            A(lambda e: e.activation(out=lg.t[:], in_=abc.t[:], func=AF.Exp), [abc], [lg])
            V(lambda e: e.tensor_scalar_mul(out=lg.t[:], in0=lg.t[:], scalar1=-1.0), [lg], [lg])
            A(lambda e: e.activation(out=gC.t[:], in_=lg.t[:], func=AF.Exp, scale=128.0), [lg], [gC])
            G(lambda e: e.iota(ii.t[:], pattern=[[1, 128]], base=0, channel_multiplier=-1), [], [ii])
            V(lambda e: e.tensor_copy(out=diff.t[:], in_=ii.t[:]), [ii], [diff])
            V(lambda e: e.tensor_scalar_max(out=posp.t[:], in0=diff.t[:], scalar1=0.0), [diff], [posp])
            V(lambda e: e.tensor_scalar(out=negp.t[:], in0=diff.t[:], scalar1=-1.0, scalar2=0.0, op0=ALU.mult, op1=ALU.max), [diff], [negp])
            V(lambda e: e.tensor_single_scalar(out=mF.t[:], in_=diff.t[:], scalar=0.0, op=ALU.is_ge), [diff], [mF])
            V(lambda e: e.tensor_single_scalar(out=mB.t[:], in_=diff.t[:], scalar=0.0, op=ALU.is_lt), [diff], [mB])
            G(lambda e: e.iota(ii.t[:], pattern=[[1, 128]], base=1, channel_multiplier=0), [], [ii])
            V(lambda e: e.tensor_copy(out=cp1.t[:], in_=ii.t[:]), [ii], [cp1])
            V(lambda e: e.tensor_scalar(out=r128.t[:], in0=cp1.t[:], scalar1=-1.0, scalar2=129.0, op0=ALU.mult, op1=ALU.add), [cp1], [r128])
            G(lambda e: e.iota(ci.t[:, 0:1], pattern=[[0, 1]], base=127, channel_multiplier=-1), [], [ci])
            G(lambda e: e.iota(ci.t[:, 1:2], pattern=[[0, 1]], base=0, channel_multiplier=1), [], [], [ci])
            V(lambda e: e.tensor_copy(out=cf.t[:], in_=ci.t[:]), [ci], [cf])
            for h in range(8):
                def tabs_h(h):
                    lf = lg.t[:, h:h + 1]
                    lb = lg.t[:, 8 + h:9 + h]
                    A(lambda e: e.activation(out=t1.t[:], in_=posp.t[:], func=AF.Exp, scale=lf), [posp, lg], [t1])
                    A(lambda e: e.activation(out=t2.t[:], in_=negp.t[:], func=AF.Exp, scale=lb), [negp, lg], [t2])
                    V(lambda e: e.tensor_tensor(out=t1.t[:], in0=t1.t[:], in1=mF.t[:], op=ALU.mult), [t1, mF], [t1])
                    V(lambda e: e.tensor_tensor(out=t2.t[:], in0=t2.t[:], in1=mB.t[:], op=ALU.mult), [t2, mB], [t2])
                    V(lambda e: e.tensor_tensor(out=DT.t[:, h, :], in0=t1.t[:], in1=t2.t[:], op=ALU.add), [t1, t2], [], [DT])
                    A(lambda e: e.activation(out=xif.t[:, h, :], in_=cp1.t[:], func=AF.Exp, scale=lf), [cp1, lg], [], [xif])
                    A(lambda e: e.activation(out=xib.t[:, h, :], in_=r128.t[:], func=AF.Exp, scale=lb), [r128, lg], [], [xib])
                    A(lambda e: e.activation(out=zf.t[:, h:h + 1], in_=cf.t[:, 0:1], func=AF.Exp, scale=lf), [cf, lg], [], [zf])
                    A(lambda e: e.activation(out=zb.t[:, h:h + 1], in_=cf.t[:, 1:2], func=AF.Exp, scale=lb), [cf, lg], [], [zb])
                tabs_h(h)
            qR = Ring([T("qR%d" % i, [128, S], BF16, dma=True) for i in range(2)])
            kR = Ring([T("kR%d" % i, [128, S], BF16, dma=True) for i in range(2)])
            sR = Ring([T("sR%d" % i, [128, S], BF16, dma=True) for i in range(2)])
            vR = Ring([T("vR%d" % i, [128, NT, 128], BF16, dma=True) for i in range(2)])
            Kzf = T("Kzf", [128, S], BF16)
            Kzb = T("Kzb", [128, S], BF16)
            Qxf = T("Qxf", [128, NT, 128], BF16)
            Qxb = T("Qxb", [128, NT, 128], BF16)
            SFb = T("SFb", [128, NT, 128], BF16)
            SBb = T("SBb", [128, NT, 128], BF16)
            S32 = [[T("S32_%d_%d" % (d_, i), [128, 128], F32) for i in range(2)] for d_ in range(2)]
            ATr = Ring([T("ATr%d" % i, [128, 4, 128], BF16) for i in range(2)])
            mhalf = T("mhalfR", [128, 512], F32)
            V(lambda e: e.memset(mhalf.t[:], -0.5), [], [mhalf])
            tmpR = Ring([[T("r6_%d_%d" % (k_, i), [128, 512], F32) for k_ in range(8)] for i in range(2)])
            obr = Ring([T("obR%d" % i, [128, 512], BF16, dma=True) for i in range(2)])
            banks = [self.ptile(st, "bkR%d" % i, [128, 512], F32) for i in range(8)]
            ptk = Ring(banks[0:2])
            pkv = Ring(banks[2:4])
            pscR = Ring(banks[4:6])
            pyR = Ring(banks[6:8])
            pstR = Ring([(banks[0], banks[1]), (banks[2], banks[3])])
            loaded = {}

            def load_head(h):
                if h >= 8 or h in loaded:
                    return
                q = qR.next(); k = kR.next(); sg = sR.next(); v = vR.next()
                self.dma("sync", q.t[:], self.qr_s[h].t, q, reads=[self.qr_s[h]], writes=[q])
                self.dma("sync", k.t[:], self.kr_s[h].t, k, reads=[self.kr_s[h]], writes=[k])
                self.dma("sync", sg.t[:], self.sg_s[h].t, sg, reads=[self.sg_s[h]], writes=[sg])
                self.dma("sync", v.t[:], self.vr_s.t[h], v, reads=[self.vr_s], writes=[v])
                loaded[h] = (q, k, sg, v)

            def do_head(h, q, k, sg, v):
                for g8 in range(4):
                    def r1(g8):
                        pt = ptk.next()

                        ptb = pt.t[:].bitcast(BF16)

                        def tr(e):
                            ins = None
                            for j in range(8):
                                n = g8 * 8 + j
                                ins = e.transpose(out=ptb[:, j * 128:(j + 1) * 128], in_=k.t[:, n * 128:(n + 1) * 128], identity=self.ident.t[:])
                            return ins
                        P.op("tensor", tr, reads=[k, self.ident], writes=[pt])
                        cs = slice(g8 * 1024, (g8 + 1) * 1024)
                        V(lambda e: e.tensor_scalar_mul(out=Kzf.t[:, cs], in0=ptb, scalar1=zf.t[:, h:h + 1]), [pt, zf], [Kzf] if g8 == 0 else [], [] if g8 == 0 else [Kzf])
                        A(lambda e: e.activation(out=Kzb.t[:, cs], in_=ptb, func=AF.Identity, scale=zb.t[:, h:h + 1]), [pt, zb, Kzf], [Kzb] if g8 == 0 else [], [] if g8 == 0 else [Kzb])
                    r1(g8)
                q3 = q.t[:].rearrange("p (n c) -> p n c", c=128)
                V(lambda e: e.tensor_tensor(out=Qxf.t[:], in0=q3, in1=xif.t[:, h:h + 1, :].to_broadcast([128, NT, 128]), op=ALU.mult), [q, xif], [Qxf])
                G(lambda e: e.tensor_tensor(out=Qxb.t[:], in0=q3, in1=xib.t[:, h:h + 1, :].to_broadcast([128, NT, 128]), op=ALU.mult), [q, xib], [Qxb])
                V(lambda e: e.memset(SFb.t[:, 0, :], 0.0), [], [SFb])
                V(lambda e: e.memset(SBb.t[:, NT - 1, :], 0.0), [], [SBb])
                V(lambda e: e.memset(S32[0][0].t[:], 0.0), [], [S32[0][0]])
                V(lambda e: e.memset(S32[1][0].t[:], 0.0), [], [S32[1][0]])
                cur = [0, 0]
                for gi in range(8):
                    def scan_g(gi):
                        pks = []
                        for direction in range(2):
                            g4 = gi if direction == 0 else 7 - gi
                            Kz = Kzf if direction == 0 else Kzb
                            pk = pkv.next()

                            def kvmm(e, g4=g4, pk=pk, Kz=Kz):
                                ins = None
                                for j in range(4):
                                    n = g4 * 4 + j
                                    ins = e.matmul(pk.t[:, j * 128:(j + 1) * 128], lhsT=Kz.t[:, n * 128:(n + 1) * 128], rhs=v.t[:, n, :], start=True, stop=True)
                                return ins
                            P.op("tensor", kvmm, reads=[Kz, v], writes=[pk])
                            pks.append((g4, pk))
                        for jj in range(4):
                            for direction in range(2):
                                g4, pk = pks[direction]
                                j = jj if direction == 0 else 3 - jj
                                n = g4 * 4 + j
                                nxt = n + 1 if direction == 0 else n - 1
                                if nxt < 0 or nxt >= NT:
                                    continue

                                def step(direction=direction, j=j, nxt=nxt, pk=pk):
                                    Sb = SFb if direction == 0 else SBb
                                    gcol = gC.t[:, h:h + 1] if direction == 0 else gC.t[:, 8 + h:9 + h]
                                    so = S32[direction][cur[direction]]; sn = S32[direction][1 - cur[direction]]
                                    V(lambda e: e.scalar_tensor_tensor(out=sn.t[:], in0=so.t[:], scalar=gcol, in1=pk.t[:, j * 128:(j + 1) * 128],
                                                                       op0=ALU.mult, op1=ALU.add), [so, gC, pk], [sn])
                                    G(lambda e: e.tensor_copy(out=Sb.t[:, nxt, :], in_=sn.t[:]), [sn], [], [Sb])
                                step()
                                cur[direction] = 1 - cur[direction]
                    scan_g(gi)
                ctx = {}

                def st1(g4):
                    c = {}
                    c["AT"] = ATr.next(); c["psc"] = pscR.next()
                    psc = c["psc"]; AT = c["AT"]

                    def scmm(e):
                        ins = None
                        for j in range(4):
                            n = g4 * 4 + j
                            ins = e.matmul(psc.t[:, j * 128:(j + 1) * 128], lhsT=k.t[:, n * 128:(n + 1) * 128], rhs=q.t[:, n * 128:(n + 1) * 128], start=True, stop=True)
                        return ins
                    P.op("tensor", scmm, reads=[k, q], writes=[psc])
                    V(lambda e: e.tensor_tensor(out=AT.t[:], in0=psc.t[:].rearrange("p (n c) -> p n c", c=128),
                                                in1=DT.t[:, h:h + 1, :].to_broadcast([128, 4, 128]), op=ALU.mult), [psc, DT], [AT])
                    ctx[g4] = c

                def st2(g4):
                    c = ctx[g4]
                    AT = c["AT"]
                    c["py"] = pyR.next(); c["tmp"] = tmpR.next()
                    py = c["py"]
                    y32, ysq = c["tmp"][0], c["tmp"][1]

                    def ymm(e):
                        ins = None
                        for j in range(4):
                            n = g4 * 4 + j
                            o = py.t[:, j * 128:(j + 1) * 128]
                            e.matmul(o, lhsT=v.t[:, n, :], rhs=AT.t[:, j, :], start=True, stop=False)
                            e.matmul(o, lhsT=SFb.t[:, n, :], rhs=Qxf.t[:, n, :], start=False, stop=False)
                            ins = e.matmul(o, lhsT=SBb.t[:, n, :], rhs=Qxb.t[:, n, :], start=False, stop=True)
                        return ins
                    P.op("tensor", ymm, reads=[v, AT, SFb, SBb, Qxf, Qxb], writes=[py])
                    A(lambda e: e.activation(out=y32.t[:], in_=py.t[:], func=AF.Copy), [py], [y32])
                    A(lambda e: e.activation(out=ysq.t[:], in_=py.t[:], func=AF.Square), [py], [ysq])

                def st3(g4):
                    c = ctx.pop(g4)
                    cs = slice(g4 * 512, (g4 + 1) * 512)
                    ob = obr.next(); pst = pstR.next()
                    y32, ysq, mn, msq, var, rsd, yc, yo = c["tmp"]
                    P.op("tensor", lambda e: e.matmul(pst[0].t[:], lhsT=self.ones32.t[:], rhs=y32.t[:], start=True, stop=True), reads=[y32, self.ones32], writes=[pst[0]])
                    P.op("tensor", lambda e: e.matmul(pst[1].t[:], lhsT=self.ones32.t[:], rhs=ysq.t[:], start=True, stop=True), reads=[ysq, self.ones32], writes=[pst[1]])
                    V(lambda e: e.tensor_scalar_mul(out=mn.t[:], in0=pst[0].t[:], scalar1=1.0 / 128.0), [pst[0]], [mn])
                    V(lambda e: e.tensor_tensor(out=msq.t[:], in0=mn.t[:], in1=mn.t[:], op=ALU.mult), [mn], [msq])
                    V(lambda e: e.scalar_tensor_tensor(out=var.t[:], in0=pst[1].t[:], scalar=1.0 / 128.0, in1=msq.t[:], op0=ALU.mult, op1=ALU.subtract), [pst[1], msq], [var])
                    A(lambda e: e.activation(out=var.t[:], in_=var.t[:], func=AF.Ln, bias=self.epsT.t[:], scale=1.0), [var, self.epsT], [var])
                    A(lambda e: e.activation(out=rsd.t[:], in_=var.t[:], func=AF.Exp, scale=-0.5), [var], [rsd])
                    G(lambda e: e.tensor_tensor(out=yc.t[:], in0=y32.t[:], in1=mn.t[:], op=ALU.subtract), [y32, mn], [yc])
                    V(lambda e: e.tensor_tensor(out=yc.t[:], in0=yc.t[:], in1=rsd.t[:], op=ALU.mult), [yc, rsd], [yc])
                    V(lambda e: e.tensor_scalar(out=yo.t[:], in0=yc.t[:], scalar1=gnw.t[:, h:h + 1], scalar2=gnb.t[:, h:h + 1], op0=ALU.mult, op1=ALU.add), [yc, gnw, gnb], [yo])
                    G(lambda e: e.tensor_tensor(out=ob.t[:], in0=yo.t[:], in1=sg.t[:, cs], op=ALU.mult), [yo, sg], [ob])
                    self.dma("sync", self.mix_s.t[h * 128:(h + 1) * 128, cs], ob.t[:], ob, reads=[ob], appends=[self.mix_s])

                for it in range(8 + 2):
                    if it < 8:
                        st1(it)
                    if 0 <= it - 1 < 8:
                        st2(it - 1)
                    if 0 <= it - 2 < 8:
                        st3(it - 2)

            load_head(0)
            for h in range(8):
                load_head(h + 1)
                do_head(h, *loaded[h])

    def stageC(self, top):
        P = self.P
        self.fence()
        with ExitStack() as st:
            T = lambda name, shape, dt, dma=False: self.tile(st, name, shape, dt, dma)
            V = lambda fn, reads, writes=(), appends=(): P.op("vector", fn, reads=reads, writes=writes, appends=appends)
            A = lambda fn, reads, writes=(), appends=(): P.op("scalar", fn, reads=reads, writes=writes, appends=appends)
            G = lambda fn, reads, writes=(), appends=(): P.op("gpsimd", fn, reads=reads, writes=writes, appends=appends)
            xt = T("xtC", [128, 4, D], F32, dma=True)
            xsub = [Buf(xt.t, xt.ds) for _ in range(4)]
            for b_ in xsub:
                b_.r = dict(self.barrier)
            osem = [T("osemC%d" % i, [128, 1], F32, dma=True) for i in range(4)]
            mix = T("mixC", [128, KC, 512], BF16, dma=True)
            rsm = T("rsmC", [128, 512], F32, dma=True)
            mow = T("mowC", [128, 8], F32, dma=True)
            h2T = T("h2T", [128, KC, 512], BF16)
            aT = T("aT", [128, FC, 512], BF16)
            cb = [T("cbC%d" % i, [128, D], F32, dma=True) for i in range(2)]
            gsl = Ring([T("gslC%d" % i, [128, 512], F32, dma=True) for i in range(2)])
            hbr = Ring([T("hbC%d" % i, [128, D], BF16) for i in range(2)])
            msr = Ring([T("msC%d" % i, [128, 1], F32) for i in range(2)])
            sdr = Ring([T("sdC%d" % i, [128, 1], F32) for i in range(2)])
            rsr = Ring([T("rsC%d" % i, [128, 1], F32) for i in range(2)])
            tmr = Ring([T("tmC%d" % i, [128, 512], F32) for i in range(2)])
            sgr = Ring([T("sgC%d" % i, [128, 512], F32) for i in range(2)])
            wr = Ring([T("wC%d" % i, [128, KC, 512], BF16, dma="sw") for i in range(2)])
            wdr = Ring([T("wdC%d" % i, [128, 11, 512], BF16, dma="sw") for i in range(2)])
            pgu = Ring([self.ptile(st, "pgu%d" % i, [128, 512], F32) for i in range(4)])
            pdn = [self.ptile(st, "pdn%d" % i, [128, 512], F32) for i in range(4)]
            self.dma("sync", mow.t[:], self.mow, mow, writes=[mow])
            self.dma("sync", cb[1].t[:], self.cb_s[2].t, cb[1], reads=[self.cb_s[2]], writes=[cb[1]])
            x_v = self.x.rearrange("(n p) d -> p n d", p=128)
            mix_v = self.mix_s.t.rearrange("(kc p) t -> p kc t", p=128)
            t32 = aT.t[:, 0:8, :].rearrange("p a b -> p (a b)").bitcast(F32)
            H = D // 2

            def rms(sub, junk_hb):
                ms = msr.next(); sd = sdr.next(); rs = rsr.next()
                A(lambda e: e.activation(out=junk_hb.t[:], in_=xt.t[:, sub, :], func=AF.Square, scale=1.0 / math.sqrt(D), accum_out=ms.t[:]),
                  [xsub[sub]], [junk_hb, ms])
                A(lambda e: e.activation(out=sd.t[:], in_=ms.t[:], func=AF.Sqrt, bias=self.epsT.t[:], scale=1.0), [ms, self.epsT], [sd])
                V(lambda e: e.reciprocal(out=rs.t[:], in_=sd.t[:]), [sd], [rs])
                return rs

            def load_mix(tt):
                ts = slice(tt * 512, (tt + 1) * 512)
                self.dma("sync", mix.t[:], mix_v[:, :, ts], mix, reads=[self.mix_s], writes=[mix])
                self.dma("sync", rsm.t[:], self.rstdm_s.t[:, ts], rsm, reads=[self.rstdm_s], writes=[rsm])

            def nrm_all():
                for j in range(8):
                    def nrm(j):
                        V(lambda e: e.scalar_tensor_tensor(out=mix.t[:, 8 + j, :], in0=mix.t[:, 8 + j, :], scalar=mow.t[:, j:j + 1], in1=rsm.t[:],
                                                           op0=ALU.mult, op1=ALU.mult), [mix, mow, rsm], [], [mix])
                    nrm(j)

            def do_tile(tt):
                if tt == 0:
                    load_mix(0)
                    nrm_all()
                for sub in range(4):
                    r0 = (tt * 4 + sub) * 128
                    self.dma("sync", xt.t[:, sub, :], self.x[r0:r0 + 128, :], osem[sub], writes=[xsub[sub]])
                self.dma("sync", cb[0].t[:], self.cb_s[1].t, cb[0], reads=[self.cb_s[1]], writes=[cb[0]])
                def load_ga(cg):
                    ga = gsl.next()
                    self.dma("sync", ga.t[:], self.cb_s[0].t[:, cg * 512:(cg + 1) * 512], ga, reads=[self.cb_s[0]], writes=[ga])
                    return ga
                ga_next = [load_ga(0)]
                for cg in range(4):
                    def wo_cg(cg):
                        wb = wr.next()
                        ga = ga_next[0]
                        if cg + 1 < 4:
                            ga_next[0] = load_ga(cg + 1)
                        cs = slice(cg * 512, (cg + 1) * 512)
                        self.dma("gpsimd", wb.t[:], self.wo_b.t[cg], wb, reads=[self.wo_b], writes=[wb])
                        for sub in range(4):
                            def wo_sub(sub):
                                pb = pdn[sub]

                                def mm(e):
                                    ins = None
                                    for kc in range(KC):
                                        ins = e.matmul(pb.t[:], lhsT=mix.t[:, kc, sub * 128:(sub + 1) * 128], rhs=wb.t[:, kc, :], start=(kc == 0), stop=(kc == KC - 1))
                                    return ins
                                P.op("tensor", mm, reads=[mix, wb], writes=[pb])
                                tm = tmr.next()
                                V(lambda e: e.tensor_tensor(out=tm.t[:], in0=pb.t[:], in1=ga.t[:], op=ALU.mult), [pb, ga], [tm])
                                V(lambda e: e.tensor_tensor(out=xt.t[:, sub, cs], in0=tm.t[:], in1=xt.t[:, sub, cs], op=ALU.add), [tm, xsub[sub]], [], [xsub[sub]])
                            wo_sub(sub)
                    wo_cg(cg)
                if tt + 1 < NTT:
                    load_mix(tt + 1)
                pending = []

                def flush():
                    for (eng, dst, src, pt, first) in pending:
                        if eng == "scalar":
                            A(lambda e, dst=dst, src=src: e.activation(out=dst, in_=src, func=AF.Copy), [pt], [h2T] if first else [], [] if first else [h2T])
                        else:
                            V(lambda e, dst=dst, src=src: e.tensor_copy(out=dst, in_=src), [pt], [], [h2T])
                    del pending[:]

                for sub in range(4):
                    def n2(sub):
                        hb = hbr.next()
                        rs = rms(sub, hb)
                        V(lambda e: e.scalar_tensor_tensor(out=t32, in0=xt.t[:, sub, :], scalar=rs.t[:, 0:1], in1=cb[0].t[:], op0=ALU.mult, op1=ALU.mult),
                          [xsub[sub], rs, cb[0]], [aT])
                        V(lambda e: e.tensor_tensor(out=hb.t[:], in0=t32, in1=cb[1].t[:], op=ALU.add), [aT, cb[1]], [hb])
                        flush()
                        for half in range(2):
                            def trh(half):
                                pt = pgu.next()
                                ptb = pt.t[:].bitcast(BF16)

                                def tr(e):
                                    ins = None
                                    for j in range(8):
                                        kc = half * 8 + j
                                        ins = e.transpose(out=ptb[:, j * 128:(j + 1) * 128], in_=hb.t[:, kc * 128:(kc + 1) * 128], identity=self.ident.t[:])
                                    return ins
                                P.op("tensor", tr, reads=[hb, self.ident], writes=[pt])
                                dst = h2T.t[:, half * 8:(half + 1) * 8, sub * 128:(sub + 1) * 128]
                                src = ptb.rearrange("p (k t) -> p k t", k=8)
                                pending.append(("scalar" if half == 0 else "vector", dst, src, pt, (sub == 0 and half == 0)))
                            trh(half)
                    n2(sub)
                flush()
                self.dma("sync", cb[0].t[:], self.cb_s[4].t, cb[0], reads=[self.cb_s[4]], writes=[cb[0]])
                for fg in range(FC // 2):
                    def gu(fg):
                        wb = wr.next()
                        self.dma("gpsimd", wb.t[:], self.wg_b.t[fg], wb, reads=[self.wg_b], writes=[wb])
                        if fg == 6 and tt + 1 < NTT:
                            nrm_all()
                        for j in range(2):
                            def gu_j(j):
                                fc = fg * 2 + j
                                pg = pgu.next(); pu = pgu.next()

                                def mm(e):
                                    ins = None
                                    for kc in range(KC):
                                        ins = e.matmul(pg.t[:], lhsT=wb.t[:, kc, j * 128:(j + 1) * 128], rhs=h2T.t[:, kc, :], start=(kc == 0), stop=(kc == KC - 1))
                                    return ins

                                def mm2(e):
                                    ins = None
                                    for kc in range(KC):
                                        ins = e.matmul(pu.t[:], lhsT=wb.t[:, kc, 256 + j * 128: 256 + (j + 1) * 128], rhs=h2T.t[:, kc, :], start=(kc == 0), stop=(kc == KC - 1))
                                    return ins
                                P.op("tensor", mm, reads=[wb, h2T], writes=[pg])
                                P.op("tensor", mm2, reads=[wb, h2T], writes=[pu])
                                sgt = sgr.next()
                                A(lambda e: e.activation(out=sgt.t[:], in_=pg.t[:], func=AF.Silu), [pg], [sgt])
                                first = (fc == 0)
                                V(lambda e: e.tensor_tensor(out=aT.t[:, fc, :], in0=sgt.t[:], in1=pu.t[:], op=ALU.mult), [sgt, pu], [aT] if first else [], [] if first else [aT])
                            gu_j(j)
                    gu(fg)
                for cg in range(4):
                    def dn_cg(cg):
                        cs = slice(cg * 512, (cg + 1) * 512)
                        gf = gsl.next()
                        self.dma("sync", gf.t[:], self.cb_s[3].t[:, cs], gf, reads=[self.cb_s[3]], writes=[gf])
                        for fq in range(4):
                            def dn_fq(fq):
                                wd = wdr.next()
                                self.dma("gpsimd", wd.t[:], self.wd_b.t[cg, fq], wd, reads=[self.wd_b], writes=[wd])
                                for sub in range(4):
                                    def dn_sub(sub):
                                        pb = pdn[sub]

                                        def mm(e):
                                            ins = None
                                            for j in range(11):
                                                fc = fq * 11 + j
                                                ins = e.matmul(pb.t[:], lhsT=aT.t[:, fc, sub * 128:(sub + 1) * 128], rhs=wd.t[:, j, :],
                                                               start=(fc == 0), stop=(fc == FC - 1))
                                            return ins
                                        if fq == 0:
                                            P.op("tensor", mm, reads=[aT, wd], writes=[pb])
                                        else:
                                            P.op("tensor", mm, reads=[aT, wd], appends=[pb])
                                    dn_sub(sub)
                            dn_fq(fq)
                        for sub in range(4):
                            def dn_ev(sub):
                                pb = pdn[sub]
                                tm = tmr.next()
                                V(lambda e: e.tensor_tensor(out=tm.t[:], in0=pb.t[:], in1=gf.t[:], op=ALU.mult), [pb, gf], [tm])
                                V(lambda e: e.tensor_tensor(out=xt.t[:, sub, cs], in0=tm.t[:], in1=xt.t[:, sub, cs], op=ALU.add), [tm, xsub[sub]], [], [xsub[sub]])
                            dn_ev(sub)
                    dn_cg(cg)
                for sub in range(4):
                    def fin(sub):
                        hb = hbr.next()
                        rs = rms(sub, hb)
                        V(lambda e: e.scalar_tensor_tensor(out=xt.t[:, sub, :], in0=xt.t[:, sub, :], scalar=rs.t[:, 0:1], in1=cb[0].t[:], op0=ALU.mult, op1=ALU.mult),
                          [xsub[sub], rs, cb[0]], [xsub[sub]])
                        r0 = (tt * 4 + sub) * 128
                        self.dma("sync", self.out[r0:r0 + 128, :], xt.t[:, sub, :], osem[sub], reads=[xsub[sub]], appends=[self.out_buf])
                    fin(sub)

            for tt in range(NTT):
                do_tile(tt)


def host_inputs(inp, b):
    f = lambda a: np.ascontiguousarray(a, dtype=np.float32)
    m = {}
    m["x"] = f(inp["x"][b])
    m["cT"] = f(inp["c"][b].reshape(KC, 128).T)
    m["pos"] = np.ascontiguousarray(inp["positions"][b].reshape(1, S).astype(np.int32))
    m["ada_w"] = f(inp["ada_w"][0])
    m["ada_b"] = f(inp["ada_b"][0].reshape(1, -1))
    m["norm1_w"] = f(inp["norm1_w"][0].reshape(1, -1))
    w_in = inp["w_in"][0]
    m["w_in"] = f(w_in[:, :5120])
    kr = w_in[:, 5120:5184]
    z = np.zeros((D, 32), np.float32)
    m["w_kr"] = f(np.concatenate([kr[:, 0:32], z, kr[:, 32:64], z, z, kr[:, 0:32], z, kr[:, 32:64]], axis=1))
    m["ret_decay"] = f(inp["ret_decay"][0].reshape(1, 16))
    m["gnw"] = f(inp["ret_gn_w"][0].reshape(8, 128).T)
    m["gnb"] = f(inp["ret_gn_b"][0].reshape(8, 128).T)
    m["qnw"] = f(inp["mla_q_norm_w"][0].reshape(4, 128).T)
    m["kvnw"] = f(inp["mla_kv_norm_w"][0].reshape(4, 128).T)
    w_uq = inp["w_uq"][0].reshape(512, 8, 192)
    m["w_uqn"] = f(w_uq[:, :, :128].reshape(512, 1024))
    rp = w_uq[:, :, 128:].reshape(512, 4, 2, 2, 32)
    m["w_uqr"] = f(rp.transpose(0, 1, 3, 2, 4).reshape(512, 512))
    w_ukv = inp["w_ukv"][0].reshape(512, 8, 256)
    m["w_uk"] = f(w_ukv[:, :, :128].reshape(512, 1024))
    m["w_uv"] = f(w_ukv[:, :, 128:].reshape(512, 1024))
    m["mow"] = f(inp["mla_out_w"][0].reshape(8, 128).T)
    m["w_o"] = f(inp["w_o"][0])
    m["norm2_w"] = f(inp["norm2_w"][0].reshape(1, -1))
    m["w_gate"] = f(inp["w_gate"][0])
    m["w_up"] = f(inp["w_up"][0])
    m["w_down"] = f(inp["w_down"][0])
    m["fnw"] = f(inp["final_norm_w"].reshape(1, -1))
    fr = (np.float32(10000.0) ** (-np.arange(0, 128, 2, dtype=np.float32) / np.float32(128))).astype(np.float32)
    fm = (np.float32(10000.0) ** (-np.arange(0, 64, 2, dtype=np.float32) / np.float32(64))).astype(np.float32)
    invf = np.zeros((128, 2), np.float32)
    invf[:, 0] = fr[np.arange(128) % 64]
    invf[:, 1] = fm[np.arange(128) % 32]
    m["invf"] = invf
    return m


_CACHE = {}


def kernel(**inputs):
    inp = {k: np.asarray(v) for k, v in inputs.items()}
    if "nc" not in _CACHE:
        _CACHE["nc"] = Builder(debug=False, upto="C").build()
    nc = _CACHE["nc"]
    shared = None
    in_maps = []
    for b in range(8):
        m = host_inputs(inp, b)
        if shared is None:
            shared = m
        else:
            for k in m:
                if k not in ("x", "cT", "pos"):
                    m[k] = shared[k]
        in_maps.append(m)
    res = run_bass_kernel_spmd(nc, in_maps, core_ids=list(range(8)))
    out = np.stack([np.asarray(res.results[b]["out"], dtype=np.float32) for b in range(8)], axis=0)
    return out
```
